# Optimizing a Trainium2 kernel written in Bass

```python
import math
import jax, jax.numpy as jnp
from jax import lax
import numpy as np

D_MODEL = 1024
BATCH = 4
SEQ = 8192
DEPTH = 4

N_MIXERS = 4
RMS_EPS = 1e-6
LN_EPS = 1e-5

SSD_EXPAND = 2
SSD_D_INNER = SSD_EXPAND * D_MODEL
SSD_HEAD_DIM = 64
SSD_N_HEADS = SSD_D_INNER // SSD_HEAD_DIM
SSD_N_GROUPS = 8
SSD_HEADS_PER_GROUP = SSD_N_HEADS // SSD_N_GROUPS
SSD_D_STATE = 128
SSD_CONV = 4
SSD_CHUNK = 128
SSD_BC_DIM = SSD_N_GROUPS * SSD_D_STATE
SSD_CONV_DIM = SSD_D_INNER + 2 * SSD_BC_DIM
SSD_IN_DIM = SSD_D_INNER + SSD_CONV_DIM + SSD_N_HEADS

CONF_WIDTH = D_MODEL
CONF_KERNEL = 31

LRU_WIDTH = 1280
LRU_BLOCK = 256
LRU_N_BLOCKS = LRU_WIDTH // LRU_BLOCK
LRU_CONV = 4
LRU_C = 8.0

SGU_CHUNK = 128
SGU_FFN = 4 * D_MODEL
SGU_HALF = SGU_FFN // 2
SGU_GROUPS = 8
SGU_GROUP_DIM = SGU_HALF // SGU_GROUPS

FFN_HIDDEN = 2816
FFN_CONV = 3

kernel_name = 'hybrid_interleaved_ssd_conformer_rglru_gmlp'


def rms_norm(x, g):
    xf = x.astype(jnp.float32)
    y = xf * lax.rsqrt(jnp.mean(xf * xf, axis=-1, keepdims=True) + RMS_EPS)
    return (y * g.astype(jnp.float32)).astype(x.dtype)


def layer_norm(x, g, b):
    xf = x.astype(jnp.float32)
    mu = jnp.mean(xf, axis=-1, keepdims=True)
    xc = xf - mu
    y = xc * lax.rsqrt(jnp.mean(xc * xc, axis=-1, keepdims=True) + LN_EPS)
    return (y * g.astype(jnp.float32) + b.astype(jnp.float32)).astype(x.dtype)


def causal_dwconv(x, w, b):
    k, c = w.shape
    y = lax.conv_general_dilated(
        x, w[:, None, :].astype(x.dtype), window_strides=(1,), padding=[(k - 1, 0)],
        dimension_numbers=('NWC', 'WIO', 'NWC'), feature_group_count=c)
    return y + b.astype(x.dtype)


def ssd_mixer(x, in_proj, conv_w, conv_b, dt_bias, a_log, d_skip, norm_g, out_proj):
    bsz, seqlen, _ = x.shape
    f32 = jnp.float32
    nc = seqlen // SSD_CHUNK
    g, j, p, n, q = SSD_N_GROUPS, SSD_HEADS_PER_GROUP, SSD_HEAD_DIM, SSD_D_STATE, SSD_CHUNK
    zxbcdt = x @ in_proj
    z, xbc, dt = jnp.split(zxbcdt, [SSD_D_INNER, SSD_D_INNER + SSD_CONV_DIM], axis=-1)
    xbc = jax.nn.silu(causal_dwconv(xbc, conv_w, conv_b))
    xs, bm, cm = jnp.split(xbc, [SSD_D_INNER, SSD_D_INNER + SSD_BC_DIM], axis=-1)
    xs = jnp.moveaxis(xs.astype(f32).reshape(bsz, nc, q, g, j, p), 1, 0)
    bm = jnp.moveaxis(bm.astype(f32).reshape(bsz, nc, q, g, n), 1, 0)
    cm = jnp.moveaxis(cm.astype(f32).reshape(bsz, nc, q, g, n), 1, 0)
    dt = jax.nn.softplus(dt.astype(f32) + dt_bias.astype(f32))
    dt = jnp.moveaxis(dt.reshape(bsz, nc, q, g, j), 1, 0)
    a = -jnp.exp(a_log.astype(f32)).reshape(g, j)
    dsk = d_skip.astype(f32).reshape(g, j)
    causal = jnp.tril(jnp.ones((q, q), dtype=bool))[None, :, :, None, None]

    def chunk_step(state, inp):
        xc, bc, cc, dtc = inp
        acs = jnp.cumsum(dtc * a, axis=1)
        seg = acs[:, :, None] - acs[:, None, :]
        decay = jnp.exp(jnp.where(causal, seg, -jnp.inf))
        cb = jnp.einsum('btgn,bsgn->btsg', cc, bc)
        scores = cb[..., None] * decay * dtc[:, None]
        y_diag = jnp.einsum('btsgj,bsgjp->btgjp', scores, xc)
        y_off = jnp.einsum('btgn,bgjpn->btgjp', cc, state) * jnp.exp(acs)[..., None]
        decay_end = jnp.exp(acs[:, -1:] - acs) * dtc
        new_state = (state * jnp.exp(acs[:, -1])[..., None, None]
                     + jnp.einsum('bsgn,bsgj,bsgjp->bgjpn', bc, decay_end, xc))
        return new_state, y_diag + y_off + xc * dsk[..., None]

    state0 = jnp.zeros((bsz, g, j, p, n), f32)
    _, ys = lax.scan(chunk_step, state0, (xs, bm, cm, dt))
    y = jnp.moveaxis(ys, 0, 1).reshape(bsz, seqlen, SSD_D_INNER)
    y = rms_norm(y * jax.nn.silu(z.astype(f32)), norm_g)
    return y.astype(x.dtype) @ out_proj


def conformer_conv(x, pw1_w, pw1_b, dw_w, dw_b, ln_g, ln_b, pw2_w, pw2_b):
    h = jax.nn.glu(x @ pw1_w + pw1_b, axis=-1)
    h = causal_dwconv(h, dw_w, dw_b)
    h = jax.nn.silu(layer_norm(h, ln_g, ln_b))
    return h @ pw2_w + pw2_b


def rglru_block(x, in_w, in_b, conv_w, conv_b, ga_w, ga_b, gx_w, gx_b, lam, out_w, out_b):
    bsz, seqlen, _ = x.shape
    f32 = jnp.float32
    gate, xr = jnp.split(x @ in_w + in_b, 2, axis=-1)
    xr = causal_dwconv(xr, conv_w, conv_b)
    xb = xr.reshape(bsz, seqlen, LRU_N_BLOCKS, LRU_BLOCK)
    r = jax.nn.sigmoid(jnp.einsum('blhi,hij->blhj', xb, ga_w) + ga_b).reshape(bsz, seqlen, LRU_WIDTH)
    i = jax.nn.sigmoid(jnp.einsum('blhi,hij->blhj', xb, gx_w) + gx_b).reshape(bsz, seqlen, LRU_WIDTH)
    log_a = -LRU_C * r.astype(f32) * jax.nn.softplus(-lam.astype(f32))
    a = jnp.exp(log_a)
    bterm = jnp.sqrt(-jnp.expm1(2.0 * log_a)) * (i.astype(f32) * xr.astype(f32))

    def combine(lhs, rhs):
        a1, b1 = lhs
        a2, b2 = rhs
        return a1 * a2, a2 * b1 + b2

    _, h = lax.associative_scan(combine, (a, bterm), axis=1)
    y = jax.nn.gelu(gate) * h.astype(x.dtype)
    return y @ out_w + out_b


def chunked_sgu(x, in_w, in_b, ln_g, ln_b, sp_w, sp_b, out_w, out_b):
    bsz, seqlen, _ = x.shape
    nc = seqlen // SGU_CHUNK
    z = jax.nn.gelu(x @ in_w + in_b)
    u, v = jnp.split(z, 2, axis=-1)
    v = layer_norm(v, ln_g, ln_b).reshape(bsz, nc, SGU_CHUNK, SGU_GROUPS, SGU_GROUP_DIM)
    w = sp_w * jnp.tril(jnp.ones((SGU_CHUNK, SGU_CHUNK), sp_w.dtype))
    mixed = jnp.einsum('gts,bcsgk->bctgk', w, v) + jnp.swapaxes(sp_b, 0, 1)[None, None, :, :, None]
    return (u * mixed.reshape(bsz, seqlen, SGU_HALF)) @ out_w + out_b


def conv_ffn(x, up_w, conv_w, conv_b, down_w):
    h = causal_dwconv(x @ up_w, conv_w, conv_b)
    gt, val = jnp.split(h, 2, axis=-1)
    return (jax.nn.silu(gt) * val) @ down_w


def setup_inputs(seed: int = 0) -> dict:
    key = jax.random.key(seed)
    keys = iter(jax.random.split(key, 64))
    f32 = jnp.float32

    def nrm(shape, scale):
        return jax.random.normal(next(keys), shape, f32) * scale

    def gain(shape):
        return 1.0 + nrm(shape, 0.02)

    def unif(shape, lo, hi):
        return jax.random.uniform(next(keys), shape, f32, minval=lo, maxval=hi)

    n_a = (DEPTH + 3) // N_MIXERS
    n_b = (DEPTH + 2) // N_MIXERS
    n_c = (DEPTH + 1) // N_MIXERS
    n_d = DEPTH // N_MIXERS
    D = D_MODEL

    x = jax.random.normal(next(keys), (BATCH, SEQ, D), f32)
    norm_mix = gain((DEPTH, D))
    norm_ffn = gain((DEPTH, D))
    norm_final = gain((D,))

    a_in_proj = nrm((n_a, D, SSD_IN_DIM), D ** -0.5)
    a_conv_w = nrm((n_a, SSD_CONV, SSD_CONV_DIM), SSD_CONV ** -0.5)
    a_conv_b = nrm((n_a, SSD_CONV_DIM), 0.02)
    dt0 = jnp.exp(unif((n_a, SSD_N_HEADS), math.log(1e-3), math.log(1e-1)))
    a_dt_bias = dt0 + jnp.log(-jnp.expm1(-dt0))
    a_log = jnp.log(unif((n_a, SSD_N_HEADS), 1.0, 16.0))
    a_d_skip = gain((n_a, SSD_N_HEADS))
    a_norm = gain((n_a, SSD_D_INNER))
    a_out_proj = nrm((n_a, SSD_D_INNER, D), SSD_D_INNER ** -0.5)

    b_pw1_w = nrm((n_b, D, 2 * CONF_WIDTH), D ** -0.5)
    b_pw1_b = nrm((n_b, 2 * CONF_WIDTH), 0.02)
    b_dw_w = nrm((n_b, CONF_KERNEL, CONF_WIDTH), CONF_KERNEL ** -0.5)
    b_dw_b = nrm((n_b, CONF_WIDTH), 0.02)
    b_ln_g = gain((n_b, CONF_WIDTH))
    b_ln_b = nrm((n_b, CONF_WIDTH), 0.02)
    b_pw2_w = nrm((n_b, CONF_WIDTH, D), CONF_WIDTH ** -0.5)
    b_pw2_b = nrm((n_b, D), 0.02)

    c_in_w = nrm((n_c, D, 2 * LRU_WIDTH), D ** -0.5)
    c_in_b = nrm((n_c, 2 * LRU_WIDTH), 0.02)
    c_conv_w = nrm((n_c, LRU_CONV, LRU_WIDTH), LRU_CONV ** -0.5)
    c_conv_b = nrm((n_c, LRU_WIDTH), 0.02)
    c_ga_w = nrm((n_c, LRU_N_BLOCKS, LRU_BLOCK, LRU_BLOCK), LRU_BLOCK ** -0.5)
    c_ga_b = nrm((n_c, LRU_N_BLOCKS, LRU_BLOCK), 0.02)
    c_gx_w = nrm((n_c, LRU_N_BLOCKS, LRU_BLOCK, LRU_BLOCK), LRU_BLOCK ** -0.5)
    c_gx_b = nrm((n_c, LRU_N_BLOCKS, LRU_BLOCK), 0.02)
    s = unif((n_c, LRU_WIDTH), 0.9, 0.999) ** (1.0 / LRU_C)
    c_lambda = jnp.log(s) - jnp.log1p(-s)
    c_out_w = nrm((n_c, LRU_WIDTH, D), LRU_WIDTH ** -0.5)
    c_out_b = nrm((n_c, D), 0.02)

    d_in_w = nrm((n_d, D, SGU_FFN), D ** -0.5)
    d_in_b = nrm((n_d, SGU_FFN), 0.02)
    d_ln_g = gain((n_d, SGU_HALF))
    d_ln_b = nrm((n_d, SGU_HALF), 0.02)
    d_sp_w = nrm((n_d, SGU_GROUPS, SGU_CHUNK, SGU_CHUNK), SGU_CHUNK ** -0.5)
    d_sp_b = gain((n_d, SGU_GROUPS, SGU_CHUNK))
    d_out_w = nrm((n_d, SGU_HALF, D), SGU_HALF ** -0.5)
    d_out_b = nrm((n_d, D), 0.02)

    f_up_w = nrm((DEPTH, D, 2 * FFN_HIDDEN), D ** -0.5)
    f_conv_w = nrm((DEPTH, FFN_CONV, 2 * FFN_HIDDEN), FFN_CONV ** -0.5)
    f_conv_b = nrm((DEPTH, 2 * FFN_HIDDEN), 0.02)
    f_down_w = nrm((DEPTH, FFN_HIDDEN, D), FFN_HIDDEN ** -0.5)

    return {'x': x, 'norm_mix': norm_mix, 'norm_ffn': norm_ffn, 'norm_final': norm_final,
            'a_in_proj': a_in_proj, 'a_conv_w': a_conv_w, 'a_conv_b': a_conv_b, 'a_dt_bias': a_dt_bias,
            'a_log': a_log, 'a_d_skip': a_d_skip, 'a_norm': a_norm, 'a_out_proj': a_out_proj,
            'b_pw1_w': b_pw1_w, 'b_pw1_b': b_pw1_b, 'b_dw_w': b_dw_w, 'b_dw_b': b_dw_b,
            'b_ln_g': b_ln_g, 'b_ln_b': b_ln_b, 'b_pw2_w': b_pw2_w, 'b_pw2_b': b_pw2_b,
            'c_in_w': c_in_w, 'c_in_b': c_in_b, 'c_conv_w': c_conv_w, 'c_conv_b': c_conv_b,
            'c_ga_w': c_ga_w, 'c_ga_b': c_ga_b, 'c_gx_w': c_gx_w, 'c_gx_b': c_gx_b,
            'c_lambda': c_lambda, 'c_out_w': c_out_w, 'c_out_b': c_out_b,
            'd_in_w': d_in_w, 'd_in_b': d_in_b, 'd_ln_g': d_ln_g, 'd_ln_b': d_ln_b,
            'd_sp_w': d_sp_w, 'd_sp_b': d_sp_b, 'd_out_w': d_out_w, 'd_out_b': d_out_b,
            'f_up_w': f_up_w, 'f_conv_w': f_conv_w, 'f_conv_b': f_conv_b, 'f_down_w': f_down_w}


def reference(x, norm_mix, norm_ffn, norm_final,
              a_in_proj, a_conv_w, a_conv_b, a_dt_bias, a_log, a_d_skip, a_norm, a_out_proj,
              b_pw1_w, b_pw1_b, b_dw_w, b_dw_b, b_ln_g, b_ln_b, b_pw2_w, b_pw2_b,
              c_in_w, c_in_b, c_conv_w, c_conv_b, c_ga_w, c_ga_b, c_gx_w, c_gx_b, c_lambda, c_out_w, c_out_b,
              d_in_w, d_in_b, d_ln_g, d_ln_b, d_sp_w, d_sp_b, d_out_w, d_out_b,
              f_up_w, f_conv_w, f_conv_b, f_down_w):
    h = x
    for i in range(DEPTH):
        kind, j = i % N_MIXERS, i // N_MIXERS
        u = rms_norm(h, norm_mix[i])
        if kind == 0:
            m = ssd_mixer(u, a_in_proj[j], a_conv_w[j], a_conv_b[j], a_dt_bias[j], a_log[j],
                          a_d_skip[j], a_norm[j], a_out_proj[j])
        elif kind == 1:
            m = conformer_conv(u, b_pw1_w[j], b_pw1_b[j], b_dw_w[j], b_dw_b[j], b_ln_g[j], b_ln_b[j],
                               b_pw2_w[j], b_pw2_b[j])
        elif kind == 2:
            m = rglru_block(u, c_in_w[j], c_in_b[j], c_conv_w[j], c_conv_b[j], c_ga_w[j], c_ga_b[j],
                            c_gx_w[j], c_gx_b[j], c_lambda[j], c_out_w[j], c_out_b[j])
        else:
            m = chunked_sgu(u, d_in_w[j], d_in_b[j], d_ln_g[j], d_ln_b[j], d_sp_w[j], d_sp_b[j],
                            d_out_w[j], d_out_b[j])
        h = h + m
        h = h + conv_ffn(rms_norm(h, norm_ffn[i]), f_up_w[i], f_conv_w[i], f_conv_b[i], f_down_w[i])
    return rms_norm(h, norm_final)
```

```python
import os
import numpy as np
from contextlib import ExitStack
import concourse.bass as bass
import concourse.mybir as mybir
from concourse.bass_utils import run_bass_kernel_spmd

F32 = mybir.dt.float32
BF16 = mybir.dt.bfloat16
AF = mybir.ActivationFunctionType
ALU = mybir.AluOpType

D = 1024
T = 512
NQ = T // 128
ENGS = ("pe", "act", "dve", "pool", "sp")
SLOT_EL = 4096
NSLOT = 4
ARENA_CAP = 108 * 1024


class Buf:
    __slots__ = ("name", "t", "lw", "rd", "excl")

    def __init__(self, name, t, excl=False):
        self.name = name
        self.t = t
        self.lw = None
        self.rd = {}
        self.excl = excl

    def __getitem__(self, idx):
        return self.t[idx]


class Sched:
    SEM_ROLL = 30000

    def __init__(self, nc, stack):
        self.nc = nc
        self.stack = stack
        self.ops = {e: [] for e in ENGS}
        self.sems = {}
        self.cur = {}
        self.gen = {e: 0 for e in ENGS}
        self.waited = {e: {} for e in ENGS}
        self.nins = 0
        for e in ENGS:
            self._new_sem(e)

    def _sem(self, key):
        if key not in self.sems:
            self.sems[key] = self.stack.enter_context(self.nc.semaphore("s_" + "_".join(str(k) for k in key)))
        return self.sems[key]

    def _new_sem(self, e):
        key = (e, self.gen[e])
        self.gen[e] += 1
        self._sem(key)
        self.cur[e] = [key, 0]

    def sbuf(self, name, shape, dtype):
        t = self.stack.enter_context(self.nc.sbuf_tensor(name, list(shape), dtype))
        return Buf(name, t)

    def psum(self, name, shape, dtype=F32):
        t = self.stack.enter_context(self.nc.psum_tensor(name, list(shape), dtype))
        return Buf(name, t, excl=True)

    def _wait(self, e, key, val):
        if self.waited[e].get(key, 0) >= val:
            return
        self.waited[e][key] = val
        sem = self._sem(key)
        self.ops[e].append(lambda eng, sem=sem, val=val: eng.wait_ge(sem, val))
        self.nins += 1

    def _deps(self, e, reads, writes):
        for b in reads:
            if b.lw is not None:
                k, v, we = b.lw
                if not (we == e and e == "pe"):
                    self._wait(e, k, v)
            if b.excl:
                for k, (v, re_) in b.rd.items():
                    if re_ != e:
                        self._wait(e, k, v)
        for b in writes:
            if b.lw is not None:
                k, v, we = b.lw
                if not (we == e and e == "pe"):
                    self._wait(e, k, v)
            for k, (v, re_) in b.rd.items():
                if re_ == e and k[0] == e:
                    continue
                self._wait(e, k, v)

    def op(self, e, fn, reads=(), writes=(), inc=True):
        self._deps(e, reads, writes)
        key, cnt = self.cur[e]
        nxt = cnt + 1
        for b in reads:
            b.rd[key] = (nxt, e)
        for b in writes:
            b.lw = (key, nxt, e)
            b.rd = {}
        self.nins += 1
        if inc:
            sem = self._sem(key)
            self.ops[e].append(lambda eng, fn=fn, sem=sem: fn(eng).then_inc(sem, 1))
            self.cur[e][1] = nxt
            if nxt >= self.SEM_ROLL:
                self._new_sem(e)
        else:
            self.ops[e].append(lambda eng, fn=fn: fn(eng))

    def dma(self, e, fn, reads=(), writes=(), sem_key=None):
        self._deps(e, reads, writes)
        if sem_key is None:
            b0 = (list(writes) + list(reads))[0]
            sem_key = ("dma", b0.name)
        sem = self._sem(sem_key)
        cnt = self.cur.setdefault(sem_key, [sem_key, 0])
        cnt[1] += 16
        val = cnt[1]
        for b in reads:
            b.rd[sem_key] = (val, "dma")
        for b in writes:
            b.lw = (sem_key, val, "dma")
            b.rd = {}
        self.ops[e].append(lambda eng, fn=fn, sem=sem: fn(eng).then_inc(sem, 16))
        self.nins += 1

    def wait_buf(self, e, b):
        if b.lw is not None:
            self._wait(e, b.lw[0], b.lw[1])
        for k, (v, _) in b.rd.items():
            self._wait(e, k, v)

    def barrier(self):
        engs = ("pe", "act", "dve", "pool")
        snap = {e: tuple(self.cur[e]) for e in engs}
        for e in engs:
            for e2 in engs:
                if e2 != e and snap[e2][1] > 0:
                    self._wait(e, snap[e2][0], snap[e2][1])

    def emit(self):
        ops = self.ops
        with self.nc.Block() as block:
            @block.tensor
            def _(eng):
                for f in ops["pe"]:
                    f(eng)

            @block.scalar
            def _(eng):
                for f in ops["act"]:
                    f(eng)

            @block.vector
            def _(eng):
                for f in ops["dve"]:
                    f(eng)

            @block.gpsimd
            def _(eng):
                for f in ops["pool"]:
                    f(eng)

            @block.sync
            def _(eng):
                for f in ops["sp"]:
                    f(eng)


class _alias(Buf):
    __slots__ = ("base",)

    def __init__(self, base, t):
        object.__setattr__(self, "base", base)
        self.name = base.name
        self.t = t
        self.excl = base.excl

    lw = property(lambda self: self.base.lw, lambda self, v: setattr(self.base, "lw", v))
    rd = property(lambda self: self.base.rd, lambda self, v: setattr(self.base, "rd", v))


class Ring:
    def __init__(self, bufs):
        self.bufs = bufs
        self.i = 0

    def next(self):
        b = self.bufs[self.i % len(self.bufs)]
        self.i += 1
        return b


def wset_table():
    t = {}
    for i in range(4):
        t[f"f_up{i}"] = (8, 256, 22, 768)
        t[f"f_down{i}"] = (22, 128, 8)
    t["a_in_zx"] = (8, 256, 24, 1024)
    t["a_in_dt"] = (8, 32, 1)
    t["a_out"] = (16, 128, 8)
    t["b_pw1"] = (8, 256, 8)
    t["b_conv"] = (31, 128, 8)
    t["b_pw2"] = (8, 256, 4)
    t["c_in"] = (8, 256, 10)
    t["c_ga"] = (2, 128, 10)
    t["c_gx"] = (2, 128, 10)
    t["c_out"] = (10, 128, 8)
    t["d_in_u"] = (8, 256, 8)
    t["d_in_v"] = (8, 512, 4)
    t["d_out"] = (16, 128, 8)
    return {k: (v if len(v) == 4 else v + (0,)) for k, v in t.items()}


LAYER_WSETS = {
    0: ["a_in_zx", "a_in_dt", "a_out"],
    1: ["b_pw1", "b_conv", "b_pw2"],
    2: ["c_in", "c_ga", "c_gx", "c_out"],
    3: ["d_in_u", "d_in_v", "d_out"],
}

PCOL_SPEC = [("norm_mix", 32), ("norm_ffn", 32), ("norm_final", 8),
             ("a_conv_w", 128), ("a_conv_b", 32), ("a_dsk", 16), ("a_norm", 16),
             ("b_pw1_b", 16), ("b_dw_b", 8), ("b_ln_g", 8), ("b_ln_b", 8), ("b_pw2_b", 8),
             ("c_in_b", 20), ("c_conv_w", 40), ("c_conv_b", 10), ("c_ga_b", 10), ("c_gx_b", 10),
             ("c_lambda", 10), ("c_out_b", 8),
             ("d_in_b_u", 16), ("d_out_b", 8),
             ("f_conv_w", 4 * 132), ("f_conv_b", 4 * 44)]


def pcol_offsets():
    off, o = {}, 0
    for n, c in PCOL_SPEC:
        off[n] = o
        o += c
    return off, o


class Builder:
    def __init__(self, Lseq, layers, do_ffn=True, do_mix=True):
        assert Lseq % T == 0
        self.do_mix = do_mix
        self.mixl = tuple(layers) if do_mix else ()
        self.Lseq = Lseq
        self.layers = tuple(layers)
        self.do_ffn = do_ffn
        self.ntile = Lseq // T
        self.wtab = wset_table()
        self.poff, self.ncol = pcol_offsets()

    def used_wsets(self):
        names = []
        for l in self.layers:
            if self.do_mix:
                names += LAYER_WSETS[l]
            if self.do_ffn:
                names += [f"f_up{l}", f"f_down{l}"]
        return names

    def build(self):
        nc = bass.Bass("TRN2", target_bir_lowering=False)
        self.nc = nc
        Lseq = self.Lseq
        self.x_d = nc.dram_tensor("x", [Lseq, D], F32, kind="ExternalInput").ap()
        self.y_d = nc.dram_tensor("y", [Lseq, D], F32, kind="ExternalOutput").ap()
        self.pcols_d = nc.dram_tensor("pcols", [128, self.ncol], F32, kind="ExternalInput").ap()
        self.prows_d = nc.dram_tensor("prows", [128, 64], F32, kind="ExternalInput").ap()
        self.cmats_d = nc.dram_tensor("cmats", [128, 512], F32, kind="ExternalInput").ap()
        self.w_d, self.s_d = {}, {}
        for n in self.used_wsets():
            KC, ncb, nblk, extra = self.wtab[n]
            self.w_d[n] = nc.dram_tensor("w_" + n, [nblk, 128, KC * ncb + extra], F32, kind="ExternalInput").ap()
            self.s_d[n] = nc.dram_tensor("s_" + n, [nblk, 128, KC * ncb + extra], BF16, kind="Internal").ap()
        if 3 in self.mixl:
            self.wtsp_d = nc.dram_tensor("wt_sp", [128, 1024], F32, kind="ExternalInput").ap()
            self.dbc_d = nc.dram_tensor("d_bc", [4, 128, 2048], F32, kind="ExternalInput").ap()
        with ExitStack() as st:
            self.S = S = Sched(nc, st)
            self.alloc()
            self.prologue()
            for it in range(self.ntile):
                self.tile(it)
            for b in self.io:
                S.wait_buf("act", b)
            for key, cnt in list(S.cur.items()):
                if key[0] == "dma":
                    S._wait("act", key, cnt[1])
            S.emit()
        return nc

    def alloc(self):
        S = self.S
        self.h_t = S.sbuf("h", [128, 8, T], F32)
        self.h = [Buf(f"h{c}", self.h_t[:, c, :]) for c in range(8)]
        self.u_t = S.sbuf("u", [128, 8, T], BF16)
        self.u = [Buf(f"u{c}", self.u_t[:, c, :]) for c in range(8)]
        self.io = [S.sbuf(f"io{i}", [128, D], F32) for i in range(2)]
        self.slots = Ring([S.sbuf(f"slot{i}", [128, SLOT_EL], BF16) for i in range(NSLOT)])
        self.pcols = S.sbuf("pcols_sb", [128, self.ncol], F32)
        self.prows = S.sbuf("prows_sb", [128, 64], F32)
        self.cm = S.sbuf("cmats_sb", [128, 512], F32)
        self.cmb = S.sbuf("cmats_bf", [128, 512], BF16)
        self.wsbuf = {n: Buf("scr_" + n, None) for n in self.used_wsets()}
        self.dbcbuf = Buf("dbc", None)
        if 0 in self.mixl:
            self.a_tail = S.sbuf("a_tail", [128, 32, 3], BF16)
            self.st_t = S.sbuf("ssd_st", [128, 8, 256], F32)
            self.st = [Buf(f"st{g}", self.st_t[:, g, :]) for g in range(8)]
            self.stb_t = S.sbuf("ssd_stb", [128, 8, 256], BF16)
            self.stb = [Buf(f"stb{g}", self.stb_t[:, g, :]) for g in range(8)]
            self.a_neg = S.sbuf("a_neg", [128, 32], F32)
        if 1 in self.mixl:
            self.hcx_t = S.sbuf("hcx", [128, 8, 32 + T], BF16)
            self.hcx = [Buf(f"hcx{c}", self.hcx_t[:, c, :]) for c in range(8)]
        if 2 in self.mixl:
            self.c_tail = S.sbuf("c_tail", [128, 10, 3], F32)
            self.c_hst = S.sbuf("c_hst", [128, 10], F32)
            self.c_nsp = S.sbuf("c_nsp", [128, 10], F32)
            self.c_nsp2 = S.sbuf("c_nsp2", [128, 10], F32)
        if 3 in self.mixl:
            self.wtsp = S.sbuf("wtsp", [128, 1024], BF16)
        if self.do_ffn:
            self.f_tail = {l: S.sbuf(f"f_tail{l}", [128, 44, 2], BF16) for l in self.layers}
        self.ARENA = min(ARENA_CAP, (self.nc.sbuf_bytes_remaining - 4096) // 64 * 64)
        print('ARENA bytes', self.ARENA)
        self.arena = S.sbuf("arena", [128, self.ARENA // 4], F32)
        self.aoff = 0
        self.live = []
        ps2 = [S.psum(f"ps{i}", [128, 512], F32) for i in range(2)]
        pa2 = [S.psum(f"pa{i}", [128, 512], F32) for i in range(2)]
        py2 = [S.psum(f"py{i}", [128, 512], F32) for i in range(2)]
        self.pring6 = Ring(ps2 + pa2 + py2)
        self.pring2 = Ring(ps2)
        self.pring = self.pring6
        self.pa = Ring(pa2)
        self.py = Ring(py2)
        self.pT = S.psum("pT", [128, 1024], BF16)
        self.pstat = Ring([S.psum("pstat", [128, 512], F32)])

    def phase(self):
        self.aoff = 0

    def av(self, name, nel, dtype):
        nbytes = nel * (4 if dtype == F32 else 2)
        nbytes = (nbytes + 63) // 64 * 64
        assert self.aoff + nbytes <= self.ARENA, (name, self.aoff, nbytes)
        a = self.arena[:, self.aoff // 4:(self.aoff + nbytes) // 4]
        self.last_range = (self.aoff, self.aoff + nbytes)
        self.aoff += nbytes
        if dtype != F32:
            a = a.bitcast(dtype)
        return a[:, 0:nel]

    def abuf(self, name, nel, dtype):
        b = Buf(name, self.av(name, nel, dtype))
        s0, e0 = self.last_range
        keep = []
        for (s1, e1, ob) in self.live:
            if s1 < e0 and s0 < e1:
                ents = list(ob.rd.items())
                if ob.lw is not None:
                    ents.append((ob.lw[0], (ob.lw[1], ob.lw[2])))
                for k, (v, eng) in ents:
                    if k not in b.rd or b.rd[k][0] < v:
                        b.rd[k] = (v, eng)
                if s0 <= s1 and e1 <= e0:
                    continue
            keep.append((s1, e1, ob))
        keep.append((s0, e0, b))
        self.live = keep
        return b

    def abufs(self, name, n, nel, dtype):
        return [self.abuf(f"{name}{i}", nel, dtype) for i in range(n)]

    def pc(self, name, c, n=1):
        o = self.poff[name] + c
        return self.pcols[:, o:o + n]

    def prologue(self):
        S = self.S
        S.dma("act", lambda e: e.dma_start(out=self.pcols[:], in_=self.pcols_d), writes=[self.pcols])
        S.dma("act", lambda e: e.dma_start(out=self.prows[:], in_=self.prows_d), writes=[self.prows])
        S.dma("act", lambda e: e.dma_start(out=self.cm[:], in_=self.cmats_d), writes=[self.cm])
        S.op("dve", lambda e: e.tensor_copy(self.cmb[:], self.cm[:]), reads=[self.cm], writes=[self.cmb])
        self.conv_done = set()
        self.phase_sets = []
        for l in self.layers:
            if self.do_mix:
                self.phase_sets.append(list(LAYER_WSETS[l]))
            if self.do_ffn:
                self.phase_sets.append([f"f_up{l}", f"f_down{l}"])
        self.phase_i = 0
        self.convert_ahead(0)
        self.convert_ahead(1)
        if 0 in self.mixl:
            S.op("pool", lambda e: e.memset(self.a_tail[:], 0.0), writes=[self.a_tail])
            S.op("pool", lambda e: e.memset(self.st_t[:], 0.0), writes=self.st)
            S.op("pool", lambda e: e.memset(self.stb_t[:], 0.0), writes=self.stb)
            S.op("act", lambda e: e.activation(out=self.a_neg[:], in_=self.prows[:, 32:64], func=AF.Exp),
                 reads=[self.prows], writes=[self.a_neg])
            S.op("dve", lambda e: e.tensor_scalar(self.a_neg[:], self.a_neg[:], -1.0, None, ALU.mult),
                 reads=[self.a_neg], writes=[self.a_neg])
        if 1 in self.mixl:
            S.op("pool", lambda e: e.memset(self.hcx_t[:], 0.0), writes=self.hcx)
        if 2 in self.mixl:
            S.op("pool", lambda e: e.memset(self.c_tail[:], 0.0), writes=[self.c_tail])
            S.op("pool", lambda e: e.memset(self.c_hst[:], 0.0), writes=[self.c_hst])
            lam = self.pc("c_lambda", 0, 10)
            S.op("act", lambda e: e.activation(out=self.c_nsp[:], in_=lam, func=AF.Exp, scale=-1.0),
                 reads=[self.pcols], writes=[self.c_nsp])
            S.op("act", lambda e: e.activation(out=self.c_nsp[:], in_=self.c_nsp[:], func=AF.Ln, bias=1.0),
                 reads=[self.c_nsp], writes=[self.c_nsp])
            S.op("dve", lambda e: e.tensor_scalar(self.c_nsp2[:], self.c_nsp[:], -16.0, None, ALU.mult),
                 reads=[self.c_nsp], writes=[self.c_nsp2])
            S.op("dve", lambda e: e.tensor_scalar(self.c_nsp[:], self.c_nsp[:], -8.0, None, ALU.mult),
                 reads=[self.c_nsp, self.c_nsp2], writes=[self.c_nsp])
        if 3 in self.mixl:
            tmp = self.abuf("wtsp_f", 1024, F32)
            S.dma("act", lambda e: e.dma_start(out=tmp[:], in_=self.wtsp_d), writes=[tmp])
            tri = self.cm[:, 128:256]
            S.op("dve", lambda e: e.tensor_tensor(
                self.wtsp[:].rearrange("p (g t) -> p g t", g=8), tmp[:].rearrange("p (g t) -> p g t", g=8),
                tri.unsqueeze(1).to_broadcast([128, 8, 128]), ALU.mult), reads=[tmp, self.cm], writes=[self.wtsp])
        if self.do_ffn:
            for l in self.layers:
                S.op("pool", lambda e, l=l: e.memset(self.f_tail[l][:], 0.0), writes=[self.f_tail[l]])

    def convert_ahead(self, i):
        if i >= len(self.phase_sets):
            return
        S = self.S
        for n in self.phase_sets[i]:
            if n in self.conv_done:
                continue
            self.conv_done.add(n)
            KC, ncb, nblk, extra = self.wtab[n]
            for b in range(nblk):
                S.dma("pool", lambda e, n=n, b=b: e.dma_start(out=self.s_d[n][b], in_=self.w_d[n][b]),
                      writes=[self.wsbuf[n]])

    def wload(self, name, b, f32src=None):
        S = self.S
        slot = self.slots.next()
        if f32src is not None:
            n = f32src.shape[-1]
            dst = slot[:, 0:2 * n].bitcast(F32)
            S.dma("sp", lambda e: e.dma_start(out=dst, in_=f32src), reads=[self.dbcbuf], writes=[slot])
            return slot
        KC, ncb, nblk, extra = self.wtab[name]
        n = KC * ncb + extra
        S.dma("sp", lambda e: e.dma_start(out=slot[:, 0:n], in_=self.s_d[name][b]),
              reads=[self.wsbuf[name]], writes=[slot])
        return slot

    def dense_fm(self, name, rhs_fn, evac, ncols=T, blocks=None):
        S = self.S
        KC, ncb, nblk, extra = self.wtab[name]
        for b in (blocks if blocks is not None else range(nblk)):
            slot = self.wload(name, b)
            for j in range(ncb // 128):
                ps = self.pring.next()
                for kc in range(KC):
                    rap, rbuf = rhs_fn(kc)
                    lo = kc * ncb + j * 128
                    S.op("pe", lambda e, ps=ps, slot=slot, lo=lo, rap=rap, kc=kc: e.matmul(
                        ps[:, 0:ncols], slot[:, lo:lo + 128], rap, start=(kc == 0), stop=(kc == KC - 1)),
                        reads=[slot, rbuf], writes=[ps], inc=(kc == KC - 1))
                evac(b * (ncb // 128) + j, ps)

    def rmsnorm(self, src, gname, goff, dst, dim, tmp=None):
        S = self.S
        n = len(src)
        ones = self.cmb[:, 384:512]
        if tmp is None:
            sq = Ring(self.abufs("nsq", 2, T, BF16))
            rstd = self.abuf("nrstd", T, F32)
        else:
            b0, b1 = tmp[1].bufs
            sq = Ring([Buf(b0.name, b0[:, 0:T // 2].bitcast(BF16)), Buf(b1.name, b1[:, 0:T // 2].bitcast(BF16))])
            sq = Ring([_alias(b0, b0[:, 0:T // 2].bitcast(BF16)), _alias(b1, b1[:, 0:T // 2].bitcast(BF16))])
            rstd = tmp[0].bufs[0]
        pst = self.pstat.next()
        for c in range(n):
            q = sq.next()
            S.op("act", lambda e, q=q, c=c: e.activation(out=q[:], in_=src[c][:], func=AF.Square),
                 reads=[src[c]], writes=[q])
            S.op("pe", lambda e, q=q, c=c: e.matmul(pst[:], ones, q[:], start=(c == 0), stop=(c == n - 1)),
                 reads=[q, self.cmb], writes=[pst])
        S.op("act", lambda e: e.activation(out=rstd[:], in_=pst[:], func=AF.Ln, scale=1.0 / dim, bias=1e-6),
             reads=[pst], writes=[rstd])
        S.op("act", lambda e: e.activation(out=rstd[:], in_=rstd[:], func=AF.Exp, scale=-0.5),
             reads=[rstd], writes=[rstd])
        for c in range(n):
            g = self.pc(gname, goff + c)
            S.op("dve", lambda e, c=c, g=g: e.scalar_tensor_tensor(
                dst[c][:], src[c][:], g, rstd[:], ALU.mult, ALU.mult),
                reads=[src[c], rstd, self.pcols], writes=[dst[c]])

    def conv_taps(self, ps, raw, acc, tail_buf, tail_ap, wcols, bcol, K, in_bias=None):
        S = self.S
        H = K - 1
        S.op("pool", lambda e: e.tensor_copy(raw[:, 0:H], tail_ap), reads=[tail_buf], writes=[raw])
        if in_bias is None:
            S.op("act", lambda e: e.activation(out=raw[:, H:H + T], in_=ps[:], func=AF.Identity),
                 reads=[ps, raw], writes=[raw])
            S.op("act", lambda e: e.activation(out=acc[:], in_=ps[:], func=AF.Identity, scale=wcols[K - 1],
                                               bias=bcol), reads=[ps, self.pcols], writes=[acc])
        else:
            S.op("act", lambda e: e.activation(out=raw[:, H:H + T], in_=ps[:], func=AF.Identity, bias=in_bias),
                 reads=[ps, raw, self.pcols], writes=[raw])
            S.op("act", lambda e: e.activation(out=acc[:], in_=raw[:, H:H + T], func=AF.Identity,
                                               scale=wcols[K - 1], bias=bcol), reads=[raw, self.pcols], writes=[acc])
        for k in range(K - 1):
            S.op("dve", lambda e, k=k: e.scalar_tensor_tensor(
                acc[:], raw[:, k:k + T], wcols[k], acc[:], ALU.mult, ALU.add),
                reads=[raw, acc, self.pcols], writes=[acc])
        S.op("pool", lambda e: e.tensor_copy(tail_ap, raw[:, T:T + H]), reads=[raw], writes=[tail_buf])

    def tile(self, it):
        S = self.S
        ident = self.cm[:, 0:128]
        self.phase()
        for q in range(NQ):
            r0 = it * T + q * 128
            io = self.io[q % 2]
            S.dma("act", lambda e, io=io, r0=r0: e.dma_start(out=io[:], in_=self.x_d[r0:r0 + 128, :]),
                  writes=[io])
            for half in range(2):
                ps = self.pring.next()
                for j in range(4):
                    c = half * 4 + j
                    S.op("pe", lambda e, ps=ps, j=j, c=c, io=io: e.transpose(
                        ps[:, j * 128:(j + 1) * 128], io[:, c * 128:(c + 1) * 128], ident),
                        reads=[io, self.cm], writes=[ps], inc=(j == 3))
                for j in range(4):
                    c = half * 4 + j
                    S.op("dve", lambda e, ps=ps, j=j, c=c, q=q:
                         e.tensor_copy(self.h[c][:, q * 128:(q + 1) * 128], ps[:, j * 128:(j + 1) * 128]),
                         reads=[ps], writes=[self.h[c]])
        for l in self.layers:
            if self.do_mix:
                if it == 0:
                    self.phase_i += 1
                    self.convert_ahead(self.phase_i + 1)
                getattr(self, f"mixer{l}")(it)
            if self.do_ffn:
                if it == 0:
                    self.phase_i += 1
                    self.convert_ahead(self.phase_i + 1)
                self.ffn(l)
        self.phase()
        o = self.abufs("fin", 8, T, F32)
        self.rmsnorm_f32(self.h, "norm_final", 0, o)
        for q in range(NQ):
            for half in range(2):
                ps = self.pring.next()
                for j in range(4):
                    c = half * 4 + j
                    S.op("pe", lambda e, ps=ps, j=j, c=c, q=q: e.transpose(
                        ps[:, j * 128:(j + 1) * 128], o[c][:, q * 128:(q + 1) * 128], ident),
                        reads=[o[c], self.cm], writes=[ps], inc=(j == 3))
                S.op("dve", lambda e, ps=ps, q=q, half=half: e.tensor_copy(
                    self.io[q % 2][:, half * 512:(half + 1) * 512], ps[:]), reads=[ps], writes=[self.io[q % 2]])
            r0 = it * T + q * 128
            S.dma("act", lambda e, q=q, r0=r0: e.dma_start(out=self.y_d[r0:r0 + 128, :], in_=self.io[q % 2][:]),
                  reads=[self.io[q % 2]])

    def rmsnorm_f32(self, src, gname, goff, dst):
        self.rmsnorm(src, gname, goff, dst, D)

    def ffn(self, l):
        S = self.S
        self.phase()
        self.rmsnorm(self.h, "norm_ffn", l * 8, self.u, D)
        hid = self.abufs("hid", 22, T, BF16)
        raws = Ring(self.abufs("fraw", 6, T + 2, BF16))
        sgs = Ring(self.abufs("fsg", 3, T, F32))
        tail = self.f_tail[l]
        bo = self.poff["f_conv_b"] + l * 44
        name = f"f_up{l}"
        pend = None

        def conv_stage(b, slot, rg, rv):
            psg, psv = self.pring.next(), self.pring.next()
            for half, raw, ps2 in ((0, rg, psg), (1, rv, psv)):
                for k in range(3):
                    lo = 2048 + half * 384 + k * 128
                    S.op("pe", lambda e, ps2=ps2, slot=slot, lo=lo, raw=raw, k=k: e.matmul(
                        ps2[:, 0:T], slot[:, lo:lo + 128], raw[:, k:k + T], start=(k == 0), stop=(k == 2)),
                        reads=[slot, raw], writes=[ps2], inc=(k == 2))
            sg = sgs.next()
            bg = self.pcols[:, bo + b:bo + b + 1]
            bv = self.pcols[:, bo + 22 + b:bo + 22 + b + 1]
            S.op("act", lambda e: e.activation(out=sg[:], in_=psg[:, 0:T], func=AF.Silu, bias=bg),
                 reads=[psg, self.pcols], writes=[sg])
            S.op("dve", lambda e: e.scalar_tensor_tensor(hid[b][:], psv[:, 0:T], bv, sg[:], ALU.add, ALU.mult),
                 reads=[psv, sg, self.pcols], writes=[hid[b]])

        for b in range(22):
            slot = self.wload(name, b)
            rr = []
            for half in range(2):
                ps = self.pring.next()
                for kc in range(8):
                    lo = kc * 256 + half * 128
                    S.op("pe", lambda e, ps=ps, slot=slot, lo=lo, kc=kc: e.matmul(
                        ps[:, 0:T], slot[:, lo:lo + 128], self.u[kc][:], start=(kc == 0), stop=(kc == 7)),
                        reads=[slot, self.u[kc]], writes=[ps], inc=(kc == 7))
                idx = half * 22 + b
                raw = raws.next()
                S.op("pool", lambda e, raw=raw, idx=idx: e.tensor_copy(raw[:, 0:2], tail[:, idx, :]),
                     reads=[tail], writes=[raw])
                S.op("act", lambda e, raw=raw, ps=ps: e.activation(out=raw[:, 2:2 + T], in_=ps[:, 0:T],
                                                                   func=AF.Identity),
                     reads=[ps, raw], writes=[raw])
                S.op("pool", lambda e, raw=raw, idx=idx: e.tensor_copy(tail[:, idx, :], raw[:, T:T + 2]),
                     reads=[raw], writes=[tail])
                rr.append(raw)
            if pend is not None:
                conv_stage(*pend)
            pend = (b, slot, rr[0], rr[1])
        conv_stage(*pend)

        def evac2(oc, ps):
            S.op("dve", lambda e: e.tensor_tensor(self.h[oc][:], self.h[oc][:], ps[:], ALU.add),
                 reads=[self.h[oc], ps], writes=[self.h[oc]])

        self.dense_fm(f"f_down{l}", lambda kc: (hid[kc][:], hid[kc]), evac2)

    def resid_evac(self, bname):
        S = self.S

        def evac(oc, ps):
            b = self.pc(bname, oc)
            S.op("dve", lambda e: e.scalar_tensor_tensor(self.h[oc][:], ps[:], b, self.h[oc][:], ALU.add, ALU.add),
                 reads=[ps, self.h[oc], self.pcols], writes=[self.h[oc]])
        return evac

    def rstd_from_var(self, var, eps):
        S = self.S
        S.op("act", lambda e: e.activation(out=var[:], in_=var[:], func=AF.Ln, bias=eps), reads=[var], writes=[var])
        S.op("act", lambda e: e.activation(out=var[:], in_=var[:], func=AF.Exp, scale=-0.5), reads=[var], writes=[var])

    def mixer1(self, it):
        S = self.S
        self.phase()
        self.rmsnorm(self.h, "norm_mix", 8, self.u, D)
        ta = Ring(self.abufs("m1a", 2, T, F32))
        tb = Ring(self.abufs("m1b", 2, T, F32))
        cv = self.abufs("cv", 8, T, F32)
        sl = self.abufs("sl", 8, T, BF16)
        hcx = self.hcx
        pend = {}

        def evac(oc, ps):
            c, half = oc // 2, oc % 2
            bcol = self.pc("b_pw1_b", oc)
            if half == 0:
                a = ta.next()
                S.op("act", lambda e: e.activation(out=a[:], in_=ps[:], func=AF.Identity, bias=bcol),
                     reads=[ps, self.pcols], writes=[a])
                pend[c] = a
            else:
                a = pend.pop(c)
                sb = tb.next()
                S.op("act", lambda e: e.activation(out=sb[:], in_=ps[:], func=AF.Sigmoid, bias=bcol),
                     reads=[ps, self.pcols], writes=[sb])
                S.op("dve", lambda e: e.tensor_tensor(hcx[c][:, 30:30 + T], a[:], sb[:], ALU.mult),
                     reads=[a, sb], writes=[hcx[c]])

        self.dense_fm("b_pw1", lambda kc: (self.u[kc][:], self.u[kc]), evac)
        for c in range(8):
            slot = self.wload("b_conv", c)
            ps = self.pring.next()
            for k in range(31):
                S.op("pe", lambda e, ps=ps, slot=slot, k=k, c=c: e.matmul(
                    ps[:, 0:T], slot[:, k * 128:(k + 1) * 128], hcx[c][:, k:k + T], start=(k == 0), stop=(k == 30)),
                    reads=[slot, hcx[c]], writes=[ps], inc=(k == 30))
            bcol = self.pc("b_dw_b", c)
            S.op("act", lambda e, ps=ps, c=c, bcol=bcol: e.activation(out=cv[c][:], in_=ps[:], func=AF.Identity,
                                                                     bias=bcol),
                 reads=[ps, self.pcols], writes=[cv[c]])
            S.op("pool", lambda e, c=c: e.tensor_copy(hcx[c][:, 0:30], hcx[c][:, T:T + 30]),
                 reads=[hcx[c]], writes=[hcx[c]])
        onesb = self.cmb[:, 384:512]
        cvb = Ring(self.abufs("cvb", 2, T, BF16))
        sq = Ring(self.abufs("csq", 2, T, BF16))
        pA = self.pstat.next()
        pB = self.pring.next()
        for c in range(8):
            b1 = cvb.next()
            S.op("pool", lambda e, b1=b1, c=c: e.tensor_copy(b1[:], cv[c][:]), reads=[cv[c]], writes=[b1])
            S.op("pe", lambda e, b1=b1, c=c: e.matmul(pA[:], onesb, b1[:], start=(c == 0), stop=(c == 7)),
                 reads=[b1, self.cmb], writes=[pA])
            b2 = sq.next()
            S.op("act", lambda e, b2=b2, c=c: e.activation(out=b2[:], in_=cv[c][:], func=AF.Square),
                 reads=[cv[c]], writes=[b2])
            S.op("pe", lambda e, b2=b2, c=c: e.matmul(pB[:], onesb, b2[:], start=(c == 0), stop=(c == 7)),
                 reads=[b2, self.cmb], writes=[pB])
        mean = self.abuf("lnmean", T, F32)
        var = self.abuf("lnvar", T, F32)
        S.op("act", lambda e: e.activation(out=mean[:], in_=pA[:], func=AF.Identity, scale=1.0 / D),
             reads=[pA], writes=[mean])
        S.op("dve", lambda e: e.tensor_tensor(var[:], mean[:], mean[:], ALU.mult), reads=[mean], writes=[var])
        S.op("dve", lambda e: e.scalar_tensor_tensor(var[:], pB[:], 1.0 / D, var[:], ALU.mult, ALU.subtract),
             reads=[pB, var], writes=[var])
        self.rstd_from_var(var, 1e-5)
        for c in range(8):
            S.op("dve", lambda e, c=c: e.tensor_tensor(cv[c][:], cv[c][:], mean[:], ALU.subtract),
                 reads=[cv[c], mean], writes=[cv[c]])
            S.op("pool", lambda e, c=c: e.tensor_tensor(cv[c][:], cv[c][:], var[:], ALU.mult),
                 reads=[cv[c], var], writes=[cv[c]])
            g, b = self.pc("b_ln_g", c), self.pc("b_ln_b", c)
            S.op("act", lambda e, c=c, g=g, b=b: e.activation(out=sl[c][:], in_=cv[c][:], func=AF.Silu,
                                                             scale=g, bias=b),
                 reads=[cv[c], self.pcols], writes=[sl[c]])
        self.dense_fm("b_pw2", lambda kc: (sl[kc][:], sl[kc]), self.resid_evac("b_pw2_b"))

    def mixer2(self, it):
        S = self.S
        self.phase()
        self.rmsnorm(self.h, "norm_mix", 16, self.u, D)
        gg = self.abufs("gg", 10, T, BF16)
        xc = self.abufs("xc", 10, T, F32)
        xcb = self.abufs("xcb", 10, T, BF16)
        rr = self.abufs("rr", 10, T, F32)
        ii = self.abufs("ii", 10, T, F32)
        tmp = Ring(self.abufs("ltmp", 5, T, F32))
        raws = Ring(self.abufs("craw", 2, T + 3, F32))
        wo = self.poff["c_conv_w"]

        def evac(oc, ps):
            bcol = self.pc("c_in_b", oc)
            if oc < 10:
                S.op("act", lambda e: e.activation(out=gg[oc][:], in_=ps[:], func=AF.Gelu_apprx_tanh, bias=bcol),
                     reads=[ps, self.pcols], writes=[gg[oc]])
            else:
                j = oc - 10
                wc = [self.pcols[:, wo + k * 10 + j: wo + k * 10 + j + 1] for k in range(4)]
                self.conv_taps(ps, raws.next(), xc[j], self.c_tail, self.c_tail[:, j, :], wc,
                               self.pc("c_conv_b", j), 4, in_bias=bcol)
                S.op("pool", lambda e: e.tensor_copy(xcb[j][:], xc[j][:]), reads=[xc[j]], writes=[xcb[j]])

        self.dense_fm("c_in", lambda kc: (self.u[kc][:], self.u[kc]), evac)
        for nm, bn, dst in (("c_ga", "c_ga_b", rr), ("c_gx", "c_gx_b", ii)):
            for oc in range(10):
                slot = self.wload(nm, oc)
                ps = self.pring.next()
                for kc in range(2):
                    src = xcb[(oc // 2) * 2 + kc]
                    S.op("pe", lambda e, ps=ps, slot=slot, kc=kc, src=src: e.matmul(
                        ps[:, 0:T], slot[:, kc * 128:(kc + 1) * 128], src[:], start=(kc == 0), stop=(kc == 1)),
                        reads=[slot, src], writes=[ps], inc=(kc == 1))
                bcol = self.pc(bn, oc)
                S.op("act", lambda e, ps=ps, oc=oc, bcol=bcol, dst=dst: e.activation(
                    out=dst[oc][:], in_=ps[:], func=AF.Sigmoid, bias=bcol),
                    reads=[ps, self.pcols], writes=[dst[oc]])
        for oc0 in (0, 5):
          t2s = {}
          for oc in range(oc0, oc0 + 5):
              t1 = tmp.next()
              S.op("act", lambda e, oc=oc, t1=t1: e.activation(out=t1[:], in_=rr[oc][:], func=AF.Exp,
                                                               scale=self.c_nsp2[:, oc:oc + 1]),
                   reads=[rr[oc], self.c_nsp2], writes=[t1])
              S.op("act", lambda e, oc=oc: e.activation(out=rr[oc][:], in_=rr[oc][:], func=AF.Exp,
                                                        scale=self.c_nsp[:, oc:oc + 1]),
                   reads=[rr[oc], self.c_nsp], writes=[rr[oc]])
              t2s[oc] = t1
          for oc in range(oc0, oc0 + 5):
              t1 = t2s[oc]
              S.op("act", lambda e, t1=t1: e.activation(out=t1[:], in_=t1[:], func=AF.Sqrt, scale=-1.0, bias=1.0),
                   reads=[t1], writes=[t1])
              S.op("dve", lambda e, oc=oc: e.tensor_tensor(ii[oc][:], ii[oc][:], xc[oc][:], ALU.mult),
                   reads=[ii[oc], xc[oc]], writes=[ii[oc]])
              S.op("dve", lambda e, oc=oc, t1=t1: e.tensor_tensor(ii[oc][:], ii[oc][:], t1[:], ALU.mult),
                   reads=[ii[oc], t1], writes=[ii[oc]])
              S.op("dve", lambda e, oc=oc: e.tensor_tensor_scan(
                  xc[oc][:], rr[oc][:], ii[oc][:], self.c_hst[:, oc:oc + 1], ALU.mult, ALU.add),
                  reads=[rr[oc], ii[oc], self.c_hst], writes=[xc[oc]])
              S.op("pool", lambda e, oc=oc: e.tensor_copy(self.c_hst[:, oc:oc + 1], xc[oc][:, T - 1:T]),
                   reads=[xc[oc]], writes=[self.c_hst])
              S.op("pool", lambda e, oc=oc: e.tensor_tensor(xcb[oc][:], gg[oc][:], xc[oc][:], ALU.mult),
                   reads=[gg[oc], xc[oc]], writes=[xcb[oc]])
        self.dense_fm("c_out", lambda kc: (xcb[kc][:], xcb[kc]), self.resid_evac("c_out_b"))

    def mixer3(self, it):
        S = self.S
        self.phase()
        self.rmsnorm(self.h, "norm_mix", 24, self.u, D)
        ug = self.abufs("ug", 16, T, BF16)
        vt = self.abufs("vt", NQ, 2048, F32)
        vtb = self.abufs("vtb", NQ, 2048, BF16)
        gated = self.abufs("gated", 16, T, BF16)

        def evac(oc, ps):
            bcol = self.pc("d_in_b_u", oc)
            S.op("act", lambda e: e.activation(out=ug[oc][:], in_=ps[:], func=AF.Gelu_apprx_tanh, bias=bcol),
                 reads=[ps, self.pcols], writes=[ug[oc]])

        self.dense_fm("d_in_u", lambda kc: (self.u[kc][:], self.u[kc]), evac)
        for blk in range(4):
            bsl = self.wload(None, 0, f32src=self.dbc_d[0][:, blk * 512:(blk + 1) * 512])
            bf = bsl[:, 0:1024].bitcast(F32)
            slot = self.wload("d_in_v", blk)
            for q in range(NQ):
                ps = self.pring.next()
                for kc in range(8):
                    S.op("pe", lambda e, ps=ps, slot=slot, kc=kc, q=q: e.matmul(
                        ps[:, 0:512], self.u[kc][:, q * 128:(q + 1) * 128], slot[:, kc * 512:(kc + 1) * 512],
                        start=(kc == 0), stop=(kc == 7)), reads=[slot, self.u[kc]], writes=[ps], inc=(kc == 7))
                S.op("dve", lambda e, ps=ps, q=q, blk=blk, bf=bf: e.tensor_tensor(
                    vt[q][:, blk * 512:(blk + 1) * 512], ps[:, 0:512], bf, ALU.add),
                    reads=[ps, bsl], writes=[vt[q]])
        gsl = self.wload(None, 0, f32src=self.dbc_d[1])
        gf = gsl[:, 0:4096].bitcast(F32)
        b2sl = self.wload(None, 0, f32src=self.dbc_d[2])
        b2f = b2sl[:, 0:4096].bitcast(F32)
        st = self.abuf("bnst", NQ * 24, F32)
        mv = self.abuf("bnmv", NQ * 2, F32)
        for q in range(NQ):
            S.op("act", lambda e, q=q: e.activation(out=vt[q][:], in_=vt[q][:], func=AF.Gelu_apprx_tanh),
                 reads=[vt[q]], writes=[vt[q]])
            for j in range(4):
                S.op("dve", lambda e, q=q, j=j: e.bn_stats(st[:, q * 24 + j * 6: q * 24 + (j + 1) * 6],
                                                           vt[q][:, j * 512:(j + 1) * 512]),
                     reads=[vt[q]], writes=[st])
            S.op("dve", lambda e, q=q: e.bn_aggr(mv[:, q * 2:q * 2 + 2], st[:, q * 24:(q + 1) * 24]),
                 reads=[st], writes=[mv])
        S.op("act", lambda e: e.activation(out=st[:, 0:NQ], in_=mv[:].rearrange("p (q two) -> p q two", two=2)[:, :, 1],
                                           func=AF.Ln, bias=1e-5), reads=[mv, st], writes=[st])
        S.op("act", lambda e: e.activation(out=st[:, 0:NQ], in_=st[:, 0:NQ], func=AF.Exp, scale=-0.5),
             reads=[st], writes=[st])
        for q in range(NQ):
            S.op("dve", lambda e, q=q: e.tensor_scalar(vt[q][:], vt[q][:], mv[:, 2 * q:2 * q + 1], st[:, q:q + 1],
                                                       ALU.subtract, ALU.mult),
                 reads=[vt[q], mv, st], writes=[vt[q]])
            S.op("pool", lambda e, q=q: e.tensor_tensor(vt[q][:], vt[q][:], gf, ALU.mult),
                 reads=[vt[q], gsl], writes=[vt[q]])
            S.op("dve", lambda e, q=q: e.tensor_tensor(vtb[q][:], vt[q][:], b2f, ALU.add),
                 reads=[vt[q], b2sl], writes=[vtb[q]])
        spsl = self.wload(None, 0, f32src=self.dbc_d[3][:, 0:1024])
        spf = spsl[:, 0:2048].bitcast(F32)
        ones1 = self.cm[0:1, 384:512]
        for j in range(16):
            g = j // 2
            ps = self.pring.next()
            for q in range(NQ):
                S.op("pe", lambda e, ps=ps, q=q, j=j, g=g: e.matmul(
                    ps[:, q * 128:(q + 1) * 128], vtb[q][:, j * 128:(j + 1) * 128],
                    self.wtsp[:, g * 128:(g + 1) * 128], start=True, stop=False),
                    reads=[vtb[q], self.wtsp], writes=[ps], inc=False)
                S.op("pe", lambda e, ps=ps, q=q, g=g: e.matmul(
                    ps[:, q * 128:(q + 1) * 128], ones1, spf[0:1, g * 128:(g + 1) * 128], start=False, stop=True),
                    reads=[spsl, self.cm], writes=[ps], inc=(q == NQ - 1))
            S.op("dve", lambda e, ps=ps, j=j: e.tensor_tensor(gated[j][:], ps[:], ug[j][:], ALU.mult),
                 reads=[ps, ug[j]], writes=[gated[j]])
        self.dense_fm("d_out", lambda kc: (gated[kc][:], gated[kc]), self.resid_evac("d_out_b"))

    def mixer0(self, it):
        S = self.S
        self.phase()
        y = self.abufs("ssd_y", 16, T, F32)
        self.rmsnorm(self.h, "norm_mix", 0, self.u, D)
        xbc = self.abufs("xbc", 32, T, BF16)
        raws = Ring(self.abufs("araw32", 2, T + 3, F32))
        accs = Ring(self.abufs("aacc", 2, T, F32))
        identb = self.cmb[:, 0:128]
        mask01 = self.cmb[:, 128:256]
        tri = self.cm[:, 128:256]
        Umat = self.cm[:, 256:384]
        onesf = self.cm[:, 384:512]

        araws = Ring(self.abufs("araw", 6, T + 4, BF16))
        pend = None

        def conv_stage(slot, items):
            for (idx, j, raw) in items:
                ps2 = self.pring.next()
                for k in range(4):
                    lo = 2048 + j * 512 + k * 128
                    S.op("pe", lambda e, ps2=ps2, slot=slot, lo=lo, raw=raw, k=k: e.matmul(
                        ps2[:, 0:T], slot[:, lo:lo + 128], raw[:, k:k + T], start=(k == 0), stop=(k == 3)),
                        reads=[slot, raw], writes=[ps2], inc=(k == 3))
                bcol = self.pc("a_conv_b", idx)
                S.op("act", lambda e, ps2=ps2, idx=idx, bcol=bcol: e.activation(
                    out=xbc[idx][:], in_=ps2[:, 0:T], func=AF.Silu, bias=bcol),
                    reads=[ps2, self.pcols], writes=[xbc[idx]])

        for b in range(8, 24):
            slot = self.wload("a_in_zx", b)
            items = []
            for j in range(2):
                ps = self.pring.next()
                for kc in range(8):
                    lo = kc * 256 + j * 128
                    S.op("pe", lambda e, ps=ps, slot=slot, lo=lo, kc=kc: e.matmul(
                        ps[:, 0:T], slot[:, lo:lo + 128], self.u[kc][:], start=(kc == 0), stop=(kc == 7)),
                        reads=[slot, self.u[kc]], writes=[ps], inc=(kc == 7))
                idx = (b - 8) * 2 + j
                raw = araws.next()
                S.op("pool", lambda e, raw=raw, idx=idx: e.tensor_copy(raw[:, 0:3], self.a_tail[:, idx, :]),
                     reads=[self.a_tail], writes=[raw])
                S.op("act", lambda e, raw=raw, ps=ps: e.activation(out=raw[:, 3:3 + T], in_=ps[:, 0:T],
                                                                   func=AF.Identity),
                     reads=[ps, raw], writes=[raw])
                S.op("pool", lambda e, raw=raw, idx=idx: e.tensor_copy(self.a_tail[:, idx, :], raw[:, T:T + 3]),
                     reads=[raw], writes=[self.a_tail])
                items.append((idx, j, raw))
            if pend is not None:
                conv_stage(*pend)
            pend = (slot, items)
        conv_stage(*pend)
        slot = self.wload("a_in_dt", 0)
        dt = self.abufs("dt", NQ, 32, F32)
        dta = self.abufs("dta", NQ, 32, F32)
        dhi = self.abufs("dhi", NQ, 32, BF16)
        dlo = self.abufs("dlo", NQ, 32, BF16)
        for q in range(NQ):
            pd = self.pa.next()
            for kc in range(8):
                S.op("pe", lambda e, pd=pd, kc=kc, q=q: e.matmul(
                    pd[:, 0:32], self.u[kc][:, q * 128:(q + 1) * 128], slot[:, kc * 32:(kc + 1) * 32],
                    start=(kc == 0), stop=(kc == 7)), reads=[slot, self.u[kc]], writes=[pd], inc=(kc == 7))
            S.op("dve", lambda e, pd=pd, q=q: e.tensor_tensor(dt[q][:], pd[:, 0:32], self.prows[:, 0:32], ALU.add),
                 reads=[pd, self.prows], writes=[dt[q]])
            S.op("act", lambda e, q=q: e.activation(out=dt[q][:], in_=dt[q][:], func=AF.Exp),
                 reads=[dt[q]], writes=[dt[q]])
            S.op("act", lambda e, q=q: e.activation(out=dt[q][:], in_=dt[q][:], func=AF.Ln, bias=1.0),
                 reads=[dt[q]], writes=[dt[q]])
            S.op("dve", lambda e, q=q: e.tensor_tensor(dta[q][:], dt[q][:], self.a_neg[:], ALU.mult),
                 reads=[dt[q], self.a_neg], writes=[dta[q]])
            S.op("dve", lambda e, q=q: e.tensor_copy(dhi[q][:], dta[q][:]), reads=[dta[q]], writes=[dhi[q]])
            S.op("dve", lambda e, q=q: e.tensor_tensor(dlo[q][:], dta[q][:], dhi[q][:], ALU.subtract),
                 reads=[dta[q], dhi[q]], writes=[dlo[q]])
        STOP = int(os.environ.get("K_STOP", "99"))
        if STOP <= 2:
            return
        acs_b = self.abufs("acs", 2, 32, F32)
        dex_b = self.abufs("dex", 2, 64, F32)
        xdt = self.abuf("xdt", 2048, BF16)
        xde = self.abuf("xde", 2048, BF16)
        btok = self.abuf("btok", 1024, BF16)
        cbms = Ring(self.abufs("cbm", 3, 128, BF16))
        sgs = Ring(self.abufs("sg", 2, 512, F32))
        Es = Ring(self.abufs("Eh", 2, 512, BF16))
        eAs = Ring(self.abufs("eA", 2, 512, BF16))
        scs = Ring(self.abufs("sc", 2, 512, BF16))
        Css = Ring(self.abufs("Cs", 2, 512, BF16))
        for q in range(NQ):
            cs = slice(q * 128, (q + 1) * 128)
            acs, dex = acs_b[q % 2], dex_b[q % 2]
            p3 = self.pa.next()
            S.op("pe", lambda e, p3=p3, q=q: e.matmul(p3[:, 0:32], tri, dta[q][:], start=True, stop=True),
                 reads=[self.cm, dta[q]], writes=[p3], inc=False)
            S.op("pe", lambda e, p3=p3, q=q: e.matmul(p3[:, 32:64], Umat, dta[q][:], start=True, stop=True),
                 reads=[self.cm, dta[q]], writes=[p3], inc=False)
            S.op("pe", lambda e, p3=p3, q=q: e.matmul(p3[:, 64:96], onesf, dta[q][:], start=True, stop=True),
                 reads=[self.cm, dta[q]], writes=[p3])
            S.op("act", lambda e, p3=p3, acs=acs: e.activation(out=acs[:], in_=p3[:, 0:32], func=AF.Identity),
                 reads=[p3], writes=[acs])
            S.op("act", lambda e, p3=p3, dex=dex: e.activation(out=dex[:], in_=p3[:, 32:96], func=AF.Exp),
                 reads=[p3], writes=[dex])
            pt = self.pT
            if os.environ.get("K_SUB") == "A":
                continue
            for bt in range(2 if os.environ.get("K_SUB") != "B" else 0):
                for j in range(8):
                    xcn = bt * 8 + j
                    S.op("pe", lambda e, j=j, xcn=xcn, cs=cs: e.transpose(
                        pt[:, j * 128:(j + 1) * 128], xbc[xcn][:, cs], identb),
                        reads=[xbc[xcn], self.cmb], writes=[pt], inc=(j == 7))
                S.op("dve", lambda e, bt=bt, q=q: e.tensor_tensor(
                    xdt[:, bt * 1024:(bt + 1) * 1024].rearrange("p (h d) -> p h d", h=16),
                    pt[:].rearrange("p (h d) -> p h d", h=16),
                    dt[q][:, bt * 16:(bt + 1) * 16].unsqueeze(2).to_broadcast([128, 16, 64]), ALU.mult),
                    reads=[pt, dt[q]], writes=[xdt])
            for j in range(8):
                bcn = 16 + j
                S.op("pe", lambda e, j=j, bcn=bcn, cs=cs: e.transpose(
                    pt[:, j * 128:(j + 1) * 128], xbc[bcn][:, cs], identb),
                    reads=[xbc[bcn], self.cmb], writes=[pt], inc=(j == 7))
            S.op("dve", lambda e: e.tensor_copy(btok[:], pt[:]), reads=[pt], writes=[btok])
            if STOP <= 3:
                continue
            S.op("pool", lambda e, dex=dex: e.tensor_tensor(
                xde[:].rearrange("p (h d) -> p h d", h=32), xdt[:].rearrange("p (h d) -> p h d", h=32),
                dex[:, 0:32].unsqueeze(2).to_broadcast([128, 32, 64]), ALU.mult),
                reads=[xdt, dex], writes=[xde])
            if STOP <= 4:
                continue
            def stage_a(g, q=q, cs=cs):
                BT, CT = xbc[16 + g], xbc[24 + g]
                pyb = self.py.next()
                pab = self.pa.next()
                S.op("pe", lambda e, pyb=pyb, BT=BT, CT=CT, cs=cs: e.matmul(
                    pyb[:, 384:512], BT[:, cs], CT[:, cs], start=True, stop=True), reads=[BT, CT], writes=[pyb])
                cbm = cbms.next()
                S.op("dve", lambda e, pyb=pyb, cbm=cbm: e.tensor_tensor(cbm[:], pyb[:, 384:512], mask01, ALU.mult),
                     reads=[pyb, self.cmb], writes=[cbm])
                prb = self.pring2.next()
                for bank in (pab, prb):
                    for j in range(4):
                        hh = 4 * g + j
                        S.op("pe", lambda e, bank=bank, hh=hh, q=q, j=j: e.matmul(
                            bank[:, j * 128:(j + 1) * 128], dhi[q][:, hh:hh + 1].to_broadcast([128, 128]), mask01,
                            start=True, stop=False), reads=[dhi[q], self.cmb], writes=[bank], inc=False)
                        S.op("pe", lambda e, bank=bank, hh=hh, q=q, j=j: e.matmul(
                            bank[:, j * 128:(j + 1) * 128], dlo[q][:, hh:hh + 1].to_broadcast([128, 128]), mask01,
                            start=False, stop=True), reads=[dlo[q], self.cmb], writes=[bank], inc=(j == 3))
                return (g, BT, CT, pyb, pab, prb, cbm)

            def stage_b1(ctx, q=q, cs=cs, acs=acs, dex=dex):
                g, BT, CT, pyb, pab, prb, cbm = ctx
                eA4, r4, E4, sc4, Cs4 = eAs.next(), sgs.next(), Es.next(), scs.next(), Css.next()
                S.op("act", lambda e, pab=pab, eA4=eA4: e.activation(out=eA4[:], in_=pab[:], func=AF.Exp),
                     reads=[pab], writes=[eA4])
                for j in range(4):
                    hh = 4 * g + j
                    S.op("dve", lambda e, prb=prb, r4=r4, hh=hh, j=j, acs=acs: e.tensor_scalar(
                        r4[:, j * 128:(j + 1) * 128], prb[:, j * 128:(j + 1) * 128], acs[:, hh:hh + 1], 0.0,
                        ALU.subtract, ALU.min), reads=[prb, acs], writes=[r4])
                S.op("act", lambda e, r4=r4, E4=E4: e.activation(out=E4[:], in_=r4[:], func=AF.Exp),
                     reads=[r4], writes=[E4])
                S.op("dve", lambda e, sc4=sc4, E4=E4, cbm=cbm: e.tensor_tensor(
                    sc4[:].rearrange("p (h t) -> p h t", h=4), E4[:].rearrange("p (h t) -> p h t", h=4),
                    cbm[:].unsqueeze(1).to_broadcast([128, 4, 128]), ALU.mult), reads=[E4, cbm], writes=[sc4])
                S.op("pool", lambda e, Cs4=Cs4, eA4=eA4, CT=CT, cs=cs: e.tensor_tensor(
                    Cs4[:].rearrange("p (h t) -> p h t", h=4), eA4[:].rearrange("p (h t) -> p h t", h=4),
                    CT[:, cs].unsqueeze(1).to_broadcast([128, 4, 128]), ALU.mult), reads=[eA4, CT], writes=[Cs4])
                return (sc4, Cs4)

            def stage_b2(ctx, pre, q=q, cs=cs, acs=acs, dex=dex):
                g, BT, CT, pyb, pab, prb, cbm = ctx
                sc4, Cs4 = pre
                for j in range(4):
                    hh = 4 * g + j
                    xcn, half = hh // 2, hh % 2
                    lo = half * 64
                    pyr = (j // 2) * 128
                    S.op("pe", lambda e, pyb=pyb, pyr=pyr, lo=lo, hh=hh, sc4=sc4, j=j: e.matmul(
                        pyb[lo:lo + 64, pyr:pyr + 128], xdt[:, hh * 64:(hh + 1) * 64], sc4[:, j * 128:(j + 1) * 128],
                        start=True, stop=False, tile_position=(0, lo)), reads=[xdt, sc4], writes=[pyb], inc=False)
                    S.op("pe", lambda e, pyb=pyb, pyr=pyr, lo=lo, g=g, j=j, Cs4=Cs4: e.matmul(
                        pyb[lo:lo + 64, pyr:pyr + 128], self.stb[g][:, j * 64:(j + 1) * 64],
                        Cs4[:, j * 128:(j + 1) * 128], start=False, stop=True, tile_position=(0, lo)),
                        reads=[self.stb[g], Cs4], writes=[pyb])
                    if half == 1:
                        dsk = self.pc("a_dsk", xcn)
                        S.op("dve", lambda e, pyb=pyb, pyr=pyr, xcn=xcn, dsk=dsk, cs=cs: e.scalar_tensor_tensor(
                            y[xcn][:, cs], xbc[xcn][:, cs], dsk, pyb[:, pyr:pyr + 128], ALU.mult, ALU.add),
                            reads=[xbc[xcn], pyb, self.pcols], writes=[y[xcn]])
                pS = self.pstat.next()
                S.op("pe", lambda e, pS=pS, g=g: e.matmul(
                    pS[:, 0:256], btok[:, g * 128:(g + 1) * 128], xde[:, g * 256:(g + 1) * 256],
                    start=True, stop=True), reads=[btok, xde], writes=[pS])
                S.op("dve", lambda e, g=g, dex=dex: e.tensor_tensor(
                    self.st[g][:].rearrange("p (h d) -> p h d", h=4), self.st[g][:].rearrange("p (h d) -> p h d", h=4),
                    dex[:, 32 + 4 * g:36 + 4 * g].unsqueeze(2).to_broadcast([128, 4, 64]), ALU.mult),
                    reads=[self.st[g], dex], writes=[self.st[g]])
                S.op("dve", lambda e, g=g, pS=pS: e.tensor_tensor(self.st[g][:], self.st[g][:], pS[:, 0:256], ALU.add),
                     reads=[self.st[g], pS], writes=[self.st[g]])
                S.op("pool", lambda e, g=g: e.tensor_copy(self.stb[g][:], self.st[g][:]),
                     reads=[self.st[g]], writes=[self.stb[g]])

            ctxs = {0: stage_a(0), 1: stage_a(1)}
            pres = {0: stage_b1(ctxs[0])}
            for g in range(8):
                if g + 2 < 8:
                    ctxs[g + 2] = stage_a(g + 2)
                if g + 1 < 8:
                    pres[g + 1] = stage_b1(ctxs[g + 1])
                stage_b2(ctxs[g], pres[g])
        if STOP <= 5:
            return
        zss = accs

        def evac_z(oc, ps):
            zs = zss.next()
            S.op("act", lambda e: e.activation(out=zs[:], in_=ps[:], func=AF.Silu), reads=[ps], writes=[zs])
            S.op("pool", lambda e: e.tensor_tensor(y[oc][:], y[oc][:], zs[:], ALU.mult),
                 reads=[y[oc], zs], writes=[y[oc]])

        self.dense_fm("a_in_zx", lambda kc: (self.u[kc][:], self.u[kc]), evac_z, blocks=range(0, 8))
        yn = xbc[0:16]
        self.rmsnorm(y, "a_norm", 0, yn, 2048, tmp=(accs, raws))

        def evac_o(oc, ps):
            S.op("dve", lambda e: e.tensor_tensor(self.h[oc][:], self.h[oc][:], ps[:], ALU.add),
                 reads=[self.h[oc], ps], writes=[self.h[oc]])

        self.dense_fm("a_out", lambda kc: (yn[kc][:], yn[kc]), evac_o)


def blockify(W, ncb, perm=None):
    W = np.asarray(W, np.float32)
    if perm is not None:
        W = W[:, perm]
    K, N = W.shape
    KC, nblk = K // 128, N // ncb
    return np.ascontiguousarray(W.reshape(KC, 128, nblk, ncb).transpose(2, 1, 0, 3).reshape(nblk, 128, KC * ncb))


def cols(v):
    v = np.asarray(v, np.float32).reshape(-1)
    return v.reshape(-1, 128).T


def pair_perm(n_half, chunk=128):
    nch = n_half // chunk
    idx = []
    for c in range(nch):
        idx += list(range(c * chunk, (c + 1) * chunk))
        idx += list(range(n_half + c * chunk, n_half + (c + 1) * chunk))
    return np.array(idx)


def host_layout(inp, layers, do_ffn=True, ffn_layers=None):
    off, ncol = pcol_offsets()
    pcols = np.zeros((128, ncol), np.float32)

    def put(name, arr, o=0):
        a = np.asarray(arr, np.float32)
        pcols[:, off[name] + o: off[name] + o + a.shape[1]] = a

    for i in range(4):
        put("norm_mix", cols(inp["norm_mix"][i]), i * 8)
        put("norm_ffn", cols(inp["norm_ffn"][i]), i * 8)
    put("norm_final", cols(inp["norm_final"]))
    out = {}
    prows = np.zeros((128, 64), np.float32)
    cm = np.zeros((128, 512), np.float32)
    cm[:, 0:128] = np.eye(128)
    s = np.arange(128)
    cm[:, 128:256] = (s[:, None] <= s[None, :])
    cm[:, 256:384] = (s[:, None] > s[None, :])
    cm[:, 384:512] = 1.0
    out["cmats"] = cm
    if 0 in layers:
        cw = np.asarray(inp["a_conv_w"][0])
        for k in range(4):
            put("a_conv_w", cols(cw[k]), k * 32)
        put("a_conv_b", cols(inp["a_conv_b"][0]))
        put("a_dsk", cols(np.repeat(np.asarray(inp["a_d_skip"][0]), 64)))
        put("a_norm", cols(inp["a_norm"][0]))
        prows[:, 0:32] = np.asarray(inp["a_dt_bias"][0])[None, :]
        prows[:, 32:64] = np.asarray(inp["a_log"][0])[None, :]
        W = np.asarray(inp["a_in_proj"][0])
        zx = blockify(W[:, :6144], 256)
        dg = np.zeros((24, 128, 2, 4, 128), np.float32)
        ar = np.arange(128)
        for b in range(8, 24):
            for j in range(2):
                f0 = ((b - 8) * 2 + j) * 128
                for k in range(4):
                    dg[b, ar, j, k, ar] = cw[k, f0:f0 + 128]
        out["w_a_in_zx"] = np.concatenate([zx, dg.reshape(24, 128, 1024)], axis=2)
        out["w_a_in_dt"] = blockify(W[:, 6144:6176], 32)
        out["w_a_out"] = blockify(inp["a_out_proj"][0], 128)
    if 1 in layers:
        pp = pair_perm(1024)
        put("b_pw1_b", cols(np.asarray(inp["b_pw1_b"][0])[pp]))
        put("b_dw_b", cols(inp["b_dw_b"][0]))
        put("b_ln_g", cols(inp["b_ln_g"][0]))
        put("b_ln_b", cols(inp["b_ln_b"][0]))
        put("b_pw2_b", cols(inp["b_pw2_b"][0]))
        out["w_b_pw1"] = blockify(inp["b_pw1_w"][0], 256, pp)
        dw = np.asarray(inp["b_dw_w"][0], np.float32)
        cv = np.zeros((8, 128, 31, 128), np.float32)
        ar = np.arange(128)
        for c in range(8):
            for k in range(31):
                cv[c, ar, k, ar] = dw[k, c * 128:(c + 1) * 128]
        out["w_b_conv"] = cv.reshape(8, 128, 31 * 128)
        out["w_b_pw2"] = blockify(inp["b_pw2_w"][0], 256)
    if 2 in layers:
        put("c_in_b", cols(inp["c_in_b"][0]))
        cw = np.asarray(inp["c_conv_w"][0])
        for k in range(4):
            put("c_conv_w", cols(cw[k]), k * 10)
        put("c_conv_b", cols(inp["c_conv_b"][0]))
        put("c_ga_b", cols(np.asarray(inp["c_ga_b"][0]).reshape(-1)))
        put("c_gx_b", cols(np.asarray(inp["c_gx_b"][0]).reshape(-1)))
        put("c_lambda", cols(inp["c_lambda"][0]))
        put("c_out_b", cols(inp["c_out_b"][0]))
        out["w_c_in"] = blockify(inp["c_in_w"][0], 256)
        for nm, key in (("w_c_ga", "c_ga_w"), ("w_c_gx", "c_gx_w")):
            g = np.asarray(inp[key][0], np.float32)
            blk = np.zeros((10, 128, 2, 128), np.float32)
            for oc in range(10):
                bi, half = oc // 2, oc % 2
                blk[oc] = g[bi, :, half * 128:(half + 1) * 128].reshape(2, 128, 128).transpose(1, 0, 2)
            out[nm] = blk.reshape(10, 128, 256)
        out["w_c_out"] = blockify(inp["c_out_w"][0], 128)
    if 3 in layers:
        ib = np.asarray(inp["d_in_b"][0], np.float32)
        put("d_in_b_u", cols(ib[:2048]))
        put("d_out_b", cols(inp["d_out_b"][0]))
        W = np.asarray(inp["d_in_w"][0])
        out["w_d_in_u"] = blockify(W[:, :2048], 256)
        out["w_d_in_v"] = blockify(W[:, 2048:], 512)
        out["w_d_out"] = blockify(inp["d_out_w"][0], 128)
        spw = np.asarray(inp["d_sp_w"][0], np.float32)
        out["wt_sp"] = np.ascontiguousarray(spw.transpose(2, 0, 1).reshape(128, 1024))
        dbc = np.zeros((4, 128, 2048), np.float32)
        dbc[0] = ib[2048:][None, :]
        dbc[1] = np.asarray(inp["d_ln_g"][0])[None, :]
        dbc[2] = np.asarray(inp["d_ln_b"][0])[None, :]
        dbc[3, :, :1024] = np.asarray(inp["d_sp_b"][0]).reshape(-1)[None, :]
        out["d_bc"] = dbc
    if do_ffn:
        pp = pair_perm(2816)
        for l in (ffn_layers if ffn_layers is not None else layers):
            cw = np.asarray(inp["f_conv_w"][l])
            for k in range(3):
                put("f_conv_w", cols(cw[k]), l * 132 + k * 44)
            put("f_conv_b", cols(inp["f_conv_b"][l]), l * 44)
            up = blockify(inp["f_up_w"][l], 256, pp)
            dg = np.zeros((22, 128, 2, 3, 128), np.float32)
            ar = np.arange(128)
            for b in range(22):
                for half in range(2):
                    f0 = half * 2816 + b * 128
                    for k in range(3):
                        dg[b, ar, half, k, ar] = cw[k, f0:f0 + 128]
            out[f"w_f_up{l}"] = np.concatenate([up, dg.reshape(22, 128, 768)], axis=2)
            out[f"w_f_down{l}"] = blockify(inp["f_down_w"][l], 128)
    out["pcols"] = pcols
    out["prows"] = prows
    return out


_NC_CACHE = {}


def run(inp, Lseq, nbatch, layers=(0, 1, 2, 3), do_ffn=True, ncores=None, do_mix=True):
    key = (Lseq, tuple(layers), do_ffn, do_mix)
    if key not in _NC_CACHE:
        _NC_CACHE[key] = Builder(Lseq, layers, do_ffn, do_mix).build()
    nc = _NC_CACHE[key]
    shared = host_layout(inp, layers if do_mix else (), do_ffn, ffn_layers=layers)
    x = np.asarray(inp["x"], np.float32)
    ncores = ncores or nbatch
    in_maps = []
    for c in range(ncores):
        m = dict(shared)
        m["x"] = np.ascontiguousarray(x[c % nbatch, :Lseq])
        in_maps.append(m)
    res = run_bass_kernel_spmd(nc, in_maps, core_ids=list(range(ncores)))
    return np.stack([res.results[b]["y"] for b in range(nbatch)], 0)


def kernel(**inputs):
    return run(inputs, 8192, 4, ncores=8)
```

```python
import os
import numpy as np
from contextlib import ExitStack
import concourse.bass as bass
import concourse.mybir as mybir
from concourse.bass_utils import run_bass_kernel_spmd

F32 = mybir.dt.float32
BF16 = mybir.dt.bfloat16
AF = mybir.ActivationFunctionType
ALU = mybir.AluOpType

D = 1024
T = 512
NQ = T // 128
ENGS = ("pe", "act", "dve", "pool", "sp")
SLOT_EL = 4096
NSLOT = 4
ARENA_CAP = 108 * 1024


class Buf:
    __slots__ = ("name", "t", "lw", "rd", "excl")

    def __init__(self, name, t, excl=False):
        self.name = name
        self.t = t
        self.lw = None
        self.rd = {}
        self.excl = excl

    def __getitem__(self, idx):
        return self.t[idx]


class Sched:
    SEM_ROLL = 30000

    def __init__(self, nc, stack):
        self.nc = nc
        self.stack = stack
        self.ops = {e: [] for e in ENGS}
        self.sems = {}
        self.cur = {}
        self.gen = {e: 0 for e in ENGS}
        self.waited = {e: {} for e in ENGS}
        self.nins = 0
        for e in ENGS:
            self._new_sem(e)

    def _sem(self, key):
        if key not in self.sems:
            self.sems[key] = self.stack.enter_context(self.nc.semaphore("s_" + "_".join(str(k) for k in key)))
        return self.sems[key]

    def _new_sem(self, e):
        key = (e, self.gen[e])
        self.gen[e] += 1
        self._sem(key)
        self.cur[e] = [key, 0]

    def sbuf(self, name, shape, dtype):
        t = self.stack.enter_context(self.nc.sbuf_tensor(name, list(shape), dtype))
        return Buf(name, t)

    def psum(self, name, shape, dtype=F32):
        t = self.stack.enter_context(self.nc.psum_tensor(name, list(shape), dtype))
        return Buf(name, t, excl=True)

    def _wait(self, e, key, val):
        if self.waited[e].get(key, 0) >= val:
            return
        self.waited[e][key] = val
        sem = self._sem(key)
        self.ops[e].append(lambda eng, sem=sem, val=val: eng.wait_ge(sem, val))
        self.nins += 1

    def _deps(self, e, reads, writes):
        for b in reads:
            if b.lw is not None:
                k, v, we = b.lw
                if not (we == e and e == "pe"):
                    self._wait(e, k, v)
            if b.excl:
                for k, (v, re_) in b.rd.items():
                    if re_ != e:
                        self._wait(e, k, v)
        for b in writes:
            if b.lw is not None:
                k, v, we = b.lw
                if not (we == e and e == "pe"):
                    self._wait(e, k, v)
            for k, (v, re_) in b.rd.items():
                if re_ == e and k[0] == e:
                    continue
                self._wait(e, k, v)

    def op(self, e, fn, reads=(), writes=(), inc=True):
        self._deps(e, reads, writes)
        key, cnt = self.cur[e]
        nxt = cnt + 1
        for b in reads:
            b.rd[key] = (nxt, e)
        for b in writes:
            b.lw = (key, nxt, e)
            b.rd = {}
        self.nins += 1
        if inc:
            sem = self._sem(key)
            self.ops[e].append(lambda eng, fn=fn, sem=sem: fn(eng).then_inc(sem, 1))
            self.cur[e][1] = nxt
            if nxt >= self.SEM_ROLL:
                self._new_sem(e)
        else:
            self.ops[e].append(lambda eng, fn=fn: fn(eng))

    def dma(self, e, fn, reads=(), writes=(), sem_key=None):
        self._deps(e, reads, writes)
        if sem_key is None:
            b0 = (list(writes) + list(reads))[0]
            sem_key = ("dma", b0.name)
        sem = self._sem(sem_key)
        cnt = self.cur.setdefault(sem_key, [sem_key, 0])
        cnt[1] += 16
        val = cnt[1]
        for b in reads:
            b.rd[sem_key] = (val, "dma")
        for b in writes:
            b.lw = (sem_key, val, "dma")
            b.rd = {}
        self.ops[e].append(lambda eng, fn=fn, sem=sem: fn(eng).then_inc(sem, 16))
        self.nins += 1

    def wait_buf(self, e, b):
        if b.lw is not None:
            self._wait(e, b.lw[0], b.lw[1])
        for k, (v, _) in b.rd.items():
            self._wait(e, k, v)

    def barrier(self):
        engs = ("pe", "act", "dve", "pool")
        snap = {e: tuple(self.cur[e]) for e in engs}
        for e in engs:
            for e2 in engs:
                if e2 != e and snap[e2][1] > 0:
                    self._wait(e, snap[e2][0], snap[e2][1])

    def emit(self):
        ops = self.ops
        with self.nc.Block() as block:
            @block.tensor
            def _(eng):
                for f in ops["pe"]:
                    f(eng)

            @block.scalar
            def _(eng):
                for f in ops["act"]:
                    f(eng)

            @block.vector
            def _(eng):
                for f in ops["dve"]:
                    f(eng)

            @block.gpsimd
            def _(eng):
                for f in ops["pool"]:
                    f(eng)

            @block.sync
            def _(eng):
                for f in ops["sp"]:
                    f(eng)


class _alias(Buf):
    __slots__ = ("base",)

    def __init__(self, base, t):
        object.__setattr__(self, "base", base)
        self.name = base.name
        self.t = t
        self.excl = base.excl

    lw = property(lambda self: self.base.lw, lambda self, v: setattr(self.base, "lw", v))
    rd = property(lambda self: self.base.rd, lambda self, v: setattr(self.base, "rd", v))


class Ring:
    def __init__(self, bufs):
        self.bufs = bufs
        self.i = 0

    def next(self):
        b = self.bufs[self.i % len(self.bufs)]
        self.i += 1
        return b


def wset_table():
    t = {}
    for i in range(4):
        t[f"f_up{i}"] = (8, 256, 22, 768)
        t[f"f_down{i}"] = (22, 128, 8)
    t["a_in_zx"] = (8, 256, 24, 1024)
    t["a_in_dt"] = (8, 32, 1)
    t["a_out"] = (16, 128, 8)
    t["b_pw1"] = (8, 256, 8)
    t["b_conv"] = (31, 128, 8)
    t["b_pw2"] = (8, 256, 4)
    t["c_in"] = (8, 256, 10)
    t["c_ga"] = (2, 128, 10)
    t["c_gx"] = (2, 128, 10)
    t["c_out"] = (10, 128, 8)
    t["d_in_u"] = (8, 256, 8)
    t["d_in_v"] = (8, 512, 4)
    t["d_out"] = (16, 128, 8)
    return {k: (v if len(v) == 4 else v + (0,)) for k, v in t.items()}


LAYER_WSETS = {
    0: ["a_in_zx", "a_in_dt", "a_out"],
    1: ["b_pw1", "b_conv", "b_pw2"],
    2: ["c_in", "c_ga", "c_gx", "c_out"],
    3: ["d_in_u", "d_in_v", "d_out"],
}

PCOL_SPEC = [("norm_mix", 32), ("norm_ffn", 32), ("norm_final", 8),
             ("a_conv_w", 128), ("a_conv_b", 32), ("a_dsk", 16), ("a_norm", 16),
             ("b_pw1_b", 16), ("b_dw_b", 8), ("b_ln_g", 8), ("b_ln_b", 8), ("b_pw2_b", 8),
             ("c_in_b", 20), ("c_conv_w", 40), ("c_conv_b", 10), ("c_ga_b", 10), ("c_gx_b", 10),
             ("c_lambda", 10), ("c_out_b", 8),
             ("d_in_b_u", 16), ("d_out_b", 8),
             ("f_conv_w", 4 * 132), ("f_conv_b", 4 * 44)]


def pcol_offsets():
    off, o = {}, 0
    for n, c in PCOL_SPEC:
        off[n] = o
        o += c
    return off, o


class Builder:
    def __init__(self, Lseq, layers, do_ffn=True, do_mix=True):
        assert Lseq % T == 0
        self.do_mix = do_mix
        self.mixl = tuple(layers) if do_mix else ()
        self.Lseq = Lseq
        self.layers = tuple(layers)
        self.do_ffn = do_ffn
        self.ntile = Lseq // T
        self.wtab = wset_table()
        self.poff, self.ncol = pcol_offsets()

    def used_wsets(self):
        names = []
        for l in self.layers:
            if self.do_mix:
                names += LAYER_WSETS[l]
            if self.do_ffn:
                names += [f"f_up{l}", f"f_down{l}"]
        return names

    def build(self):
        nc = bass.Bass("TRN2", target_bir_lowering=False)
        self.nc = nc
        Lseq = self.Lseq
        self.x_d = nc.dram_tensor("x", [Lseq, D], F32, kind="ExternalInput").ap()
        self.y_d = nc.dram_tensor("y", [Lseq, D], F32, kind="ExternalOutput").ap()
        self.pcols_d = nc.dram_tensor("pcols", [128, self.ncol], F32, kind="ExternalInput").ap()
        self.prows_d = nc.dram_tensor("prows", [128, 64], F32, kind="ExternalInput").ap()
        self.cmats_d = nc.dram_tensor("cmats", [128, 512], F32, kind="ExternalInput").ap()
        self.w_d, self.s_d = {}, {}
        for n in self.used_wsets():
            KC, ncb, nblk, extra = self.wtab[n]
            self.w_d[n] = nc.dram_tensor("w_" + n, [nblk, 128, KC * ncb + extra], F32, kind="ExternalInput").ap()
            self.s_d[n] = nc.dram_tensor("s_" + n, [nblk, 128, KC * ncb + extra], BF16, kind="Internal").ap()
        if 3 in self.mixl:
            self.wtsp_d = nc.dram_tensor("wt_sp", [128, 1024], F32, kind="ExternalInput").ap()
            self.dbc_d = nc.dram_tensor("d_bc", [4, 128, 2048], F32, kind="ExternalInput").ap()
        with ExitStack() as st:
            self.S = S = Sched(nc, st)
            self.alloc()
            self.prologue()
            for it in range(self.ntile):
                self.tile(it)
            for b in self.io:
                S.wait_buf("act", b)
            for key, cnt in list(S.cur.items()):
                if key[0] == "dma":
                    S._wait("act", key, cnt[1])
            S.emit()
        return nc

    def alloc(self):
        S = self.S
        self.h_t = S.sbuf("h", [128, 8, T], F32)
        self.h = [Buf(f"h{c}", self.h_t[:, c, :]) for c in range(8)]
        self.u_t = S.sbuf("u", [128, 8, T], BF16)
        self.u = [Buf(f"u{c}", self.u_t[:, c, :]) for c in range(8)]
        self.io = [S.sbuf(f"io{i}", [128, D], F32) for i in range(2)]
        self.slots = Ring([S.sbuf(f"slot{i}", [128, SLOT_EL], BF16) for i in range(NSLOT)])
        self.pcols = S.sbuf("pcols_sb", [128, self.ncol], F32)
        self.prows = S.sbuf("prows_sb", [128, 64], F32)
        self.cm = S.sbuf("cmats_sb", [128, 512], F32)
        self.cmb = S.sbuf("cmats_bf", [128, 512], BF16)
        self.wsbuf = {n: Buf("scr_" + n, None) for n in self.used_wsets()}
        self.dbcbuf = Buf("dbc", None)
        if 0 in self.mixl:
            self.a_tail = S.sbuf("a_tail", [128, 32, 3], BF16)
            self.st_t = S.sbuf("ssd_st", [128, 8, 256], F32)
            self.st = [Buf(f"st{g}", self.st_t[:, g, :]) for g in range(8)]
            self.stb_t = S.sbuf("ssd_stb", [128, 8, 256], BF16)
            self.stb = [Buf(f"stb{g}", self.stb_t[:, g, :]) for g in range(8)]
            self.a_neg = S.sbuf("a_neg", [128, 32], F32)
        if 1 in self.mixl:
            self.hcx_t = S.sbuf("hcx", [128, 8, 32 + T], BF16)
            self.hcx = [Buf(f"hcx{c}", self.hcx_t[:, c, :]) for c in range(8)]
        if 2 in self.mixl:
            self.c_tail = S.sbuf("c_tail", [128, 10, 3], F32)
            self.c_hst = S.sbuf("c_hst", [128, 10], F32)
            self.c_nsp = S.sbuf("c_nsp", [128, 10], F32)
            self.c_nsp2 = S.sbuf("c_nsp2", [128, 10], F32)
        if 3 in self.mixl:
            self.wtsp = S.sbuf("wtsp", [128, 1024], BF16)
        if self.do_ffn:
            self.f_tail = {l: S.sbuf(f"f_tail{l}", [128, 44, 2], BF16) for l in self.layers}
        self.ARENA = min(ARENA_CAP, (self.nc.sbuf_bytes_remaining - 4096) // 64 * 64)
        print('ARENA bytes', self.ARENA)
        self.arena = S.sbuf("arena", [128, self.ARENA // 4], F32)
        self.aoff = 0
        self.live = []
        ps2 = [S.psum(f"ps{i}", [128, 512], F32) for i in range(2)]
        pa2 = [S.psum(f"pa{i}", [128, 512], F32) for i in range(2)]
        py2 = [S.psum(f"py{i}", [128, 512], F32) for i in range(2)]
        self.pring6 = Ring(ps2 + pa2 + py2)
        self.pring2 = Ring(ps2)
        self.pring = self.pring6
        self.pa = Ring(pa2)
        self.py = Ring(py2)
        self.pT = S.psum("pT", [128, 1024], BF16)
        self.pstat = Ring([S.psum("pstat", [128, 512], F32)])

    def phase(self):
        self.aoff = 0

    def av(self, name, nel, dtype):
        nbytes = nel * (4 if dtype == F32 else 2)
        nbytes = (nbytes + 63) // 64 * 64
        assert self.aoff + nbytes <= self.ARENA, (name, self.aoff, nbytes)
        a = self.arena[:, self.aoff // 4:(self.aoff + nbytes) // 4]
        self.last_range = (self.aoff, self.aoff + nbytes)
        self.aoff += nbytes
        if dtype != F32:
            a = a.bitcast(dtype)
        return a[:, 0:nel]

    def abuf(self, name, nel, dtype):
        b = Buf(name, self.av(name, nel, dtype))
        s0, e0 = self.last_range
        keep = []
        for (s1, e1, ob) in self.live:
            if s1 < e0 and s0 < e1:
                ents = list(ob.rd.items())
                if ob.lw is not None:
                    ents.append((ob.lw[0], (ob.lw[1], ob.lw[2])))
                for k, (v, eng) in ents:
                    if k not in b.rd or b.rd[k][0] < v:
                        b.rd[k] = (v, eng)
                if s0 <= s1 and e1 <= e0:
                    continue
            keep.append((s1, e1, ob))
        keep.append((s0, e0, b))
        self.live = keep
        return b

    def abufs(self, name, n, nel, dtype):
        return [self.abuf(f"{name}{i}", nel, dtype) for i in range(n)]

    def pc(self, name, c, n=1):
        o = self.poff[name] + c
        return self.pcols[:, o:o + n]

    def prologue(self):
        S = self.S
        S.dma("act", lambda e: e.dma_start(out=self.pcols[:], in_=self.pcols_d), writes=[self.pcols])
        S.dma("act", lambda e: e.dma_start(out=self.prows[:], in_=self.prows_d), writes=[self.prows])
        S.dma("act", lambda e: e.dma_start(out=self.cm[:], in_=self.cmats_d), writes=[self.cm])
        S.op("dve", lambda e: e.tensor_copy(self.cmb[:], self.cm[:]), reads=[self.cm], writes=[self.cmb])
        self.conv_done = set()
        self.phase_sets = []
        for l in self.layers:
            if self.do_mix:
                self.phase_sets.append(list(LAYER_WSETS[l]))
            if self.do_ffn:
                self.phase_sets.append([f"f_up{l}", f"f_down{l}"])
        self.phase_i = 0
        self.convert_ahead(0)
        self.convert_ahead(1)
        if 0 in self.mixl:
            S.op("pool", lambda e: e.memset(self.a_tail[:], 0.0), writes=[self.a_tail])
            S.op("pool", lambda e: e.memset(self.st_t[:], 0.0), writes=self.st)
            S.op("pool", lambda e: e.memset(self.stb_t[:], 0.0), writes=self.stb)
            S.op("act", lambda e: e.activation(out=self.a_neg[:], in_=self.prows[:, 32:64], func=AF.Exp),
                 reads=[self.prows], writes=[self.a_neg])
            S.op("dve", lambda e: e.tensor_scalar(self.a_neg[:], self.a_neg[:], -1.0, None, ALU.mult),
                 reads=[self.a_neg], writes=[self.a_neg])
        if 1 in self.mixl:
            S.op("pool", lambda e: e.memset(self.hcx_t[:], 0.0), writes=self.hcx)
        if 2 in self.mixl:
            S.op("pool", lambda e: e.memset(self.c_tail[:], 0.0), writes=[self.c_tail])
            S.op("pool", lambda e: e.memset(self.c_hst[:], 0.0), writes=[self.c_hst])
            lam = self.pc("c_lambda", 0, 10)
            S.op("act", lambda e: e.activation(out=self.c_nsp[:], in_=lam, func=AF.Exp, scale=-1.0),
                 reads=[self.pcols], writes=[self.c_nsp])
            S.op("act", lambda e: e.activation(out=self.c_nsp[:], in_=self.c_nsp[:], func=AF.Ln, bias=1.0),
                 reads=[self.c_nsp], writes=[self.c_nsp])
            S.op("dve", lambda e: e.tensor_scalar(self.c_nsp2[:], self.c_nsp[:], -16.0, None, ALU.mult),
                 reads=[self.c_nsp], writes=[self.c_nsp2])
            S.op("dve", lambda e: e.tensor_scalar(self.c_nsp[:], self.c_nsp[:], -8.0, None, ALU.mult),
                 reads=[self.c_nsp, self.c_nsp2], writes=[self.c_nsp])
        if 3 in self.mixl:
            tmp = self.abuf("wtsp_f", 1024, F32)
            S.dma("act", lambda e: e.dma_start(out=tmp[:], in_=self.wtsp_d), writes=[tmp])
            tri = self.cm[:, 128:256]
            S.op("dve", lambda e: e.tensor_tensor(
                self.wtsp[:].rearrange("p (g t) -> p g t", g=8), tmp[:].rearrange("p (g t) -> p g t", g=8),
                tri.unsqueeze(1).to_broadcast([128, 8, 128]), ALU.mult), reads=[tmp, self.cm], writes=[self.wtsp])
        if self.do_ffn:
            for l in self.layers:
                S.op("pool", lambda e, l=l: e.memset(self.f_tail[l][:], 0.0), writes=[self.f_tail[l]])

    def convert_ahead(self, i):
        if i >= len(self.phase_sets):
            return
        S = self.S
        for n in self.phase_sets[i]:
            if n in self.conv_done:
                continue
            self.conv_done.add(n)
            KC, ncb, nblk, extra = self.wtab[n]
            for b in range(nblk):
                S.dma("pool", lambda e, n=n, b=b: e.dma_start(out=self.s_d[n][b], in_=self.w_d[n][b]),
                      writes=[self.wsbuf[n]])

    def wload(self, name, b, f32src=None):
        S = self.S
        slot = self.slots.next()
        if f32src is not None:
            n = f32src.shape[-1]
            dst = slot[:, 0:2 * n].bitcast(F32)
            S.dma("sp", lambda e: e.dma_start(out=dst, in_=f32src), reads=[self.dbcbuf], writes=[slot])
            return slot
        KC, ncb, nblk, extra = self.wtab[name]
        n = KC * ncb + extra
        S.dma("sp", lambda e: e.dma_start(out=slot[:, 0:n], in_=self.s_d[name][b]),
              reads=[self.wsbuf[name]], writes=[slot])
        return slot

    def dense_fm(self, name, rhs_fn, evac, ncols=T, blocks=None):
        S = self.S
        KC, ncb, nblk, extra = self.wtab[name]
        for b in (blocks if blocks is not None else range(nblk)):
            slot = self.wload(name, b)
            for j in range(ncb // 128):
                ps = self.pring.next()
                for kc in range(KC):
                    rap, rbuf = rhs_fn(kc)
                    lo = kc * ncb + j * 128
                    S.op("pe", lambda e, ps=ps, slot=slot, lo=lo, rap=rap, kc=kc: e.matmul(
                        ps[:, 0:ncols], slot[:, lo:lo + 128], rap, start=(kc == 0), stop=(kc == KC - 1)),
                        reads=[slot, rbuf], writes=[ps], inc=(kc == KC - 1))
                evac(b * (ncb // 128) + j, ps)

    def rmsnorm(self, src, gname, goff, dst, dim, tmp=None):
        S = self.S
        n = len(src)
        ones = self.cmb[:, 384:512]
        if tmp is None:
            sq = Ring(self.abufs("nsq", 2, T, BF16))
            rstd = self.abuf("nrstd", T, F32)
        else:
            b0, b1 = tmp[1].bufs
            sq = Ring([Buf(b0.name, b0[:, 0:T // 2].bitcast(BF16)), Buf(b1.name, b1[:, 0:T // 2].bitcast(BF16))])
            sq = Ring([_alias(b0, b0[:, 0:T // 2].bitcast(BF16)), _alias(b1, b1[:, 0:T // 2].bitcast(BF16))])
            rstd = tmp[0].bufs[0]
        pst = self.pstat.next()
        for c in range(n):
            q = sq.next()
            S.op("act", lambda e, q=q, c=c: e.activation(out=q[:], in_=src[c][:], func=AF.Square),
                 reads=[src[c]], writes=[q])
            S.op("pe", lambda e, q=q, c=c: e.matmul(pst[:], ones, q[:], start=(c == 0), stop=(c == n - 1)),
                 reads=[q, self.cmb], writes=[pst])
        S.op("act", lambda e: e.activation(out=rstd[:], in_=pst[:], func=AF.Ln, scale=1.0 / dim, bias=1e-6),
             reads=[pst], writes=[rstd])
        S.op("act", lambda e: e.activation(out=rstd[:], in_=rstd[:], func=AF.Exp, scale=-0.5),
             reads=[rstd], writes=[rstd])
        for c in range(n):
            g = self.pc(gname, goff + c)
            S.op("dve", lambda e, c=c, g=g: e.scalar_tensor_tensor(
                dst[c][:], src[c][:], g, rstd[:], ALU.mult, ALU.mult),
                reads=[src[c], rstd, self.pcols], writes=[dst[c]])

    def conv_taps(self, ps, raw, acc, tail_buf, tail_ap, wcols, bcol, K, in_bias=None):
        S = self.S
        H = K - 1
        S.op("pool", lambda e: e.tensor_copy(raw[:, 0:H], tail_ap), reads=[tail_buf], writes=[raw])
        if in_bias is None:
            S.op("act", lambda e: e.activation(out=raw[:, H:H + T], in_=ps[:], func=AF.Identity),
                 reads=[ps, raw], writes=[raw])
            S.op("act", lambda e: e.activation(out=acc[:], in_=ps[:], func=AF.Identity, scale=wcols[K - 1],
                                               bias=bcol), reads=[ps, self.pcols], writes=[acc])
        else:
            S.op("act", lambda e: e.activation(out=raw[:, H:H + T], in_=ps[:], func=AF.Identity, bias=in_bias),
                 reads=[ps, raw, self.pcols], writes=[raw])
            S.op("act", lambda e: e.activation(out=acc[:], in_=raw[:, H:H + T], func=AF.Identity,
                                               scale=wcols[K - 1], bias=bcol), reads=[raw, self.pcols], writes=[acc])
        for k in range(K - 1):
            S.op("dve", lambda e, k=k: e.scalar_tensor_tensor(
                acc[:], raw[:, k:k + T], wcols[k], acc[:], ALU.mult, ALU.add),
                reads=[raw, acc, self.pcols], writes=[acc])
        S.op("pool", lambda e: e.tensor_copy(tail_ap, raw[:, T:T + H]), reads=[raw], writes=[tail_buf])

    def tile(self, it):
        S = self.S
        ident = self.cm[:, 0:128]
        self.phase()
        for q in range(NQ):
            r0 = it * T + q * 128
            io = self.io[q % 2]
            S.dma("act", lambda e, io=io, r0=r0: e.dma_start(out=io[:], in_=self.x_d[r0:r0 + 128, :]),
                  writes=[io])
            for half in range(2):
                ps = self.pring.next()
                for j in range(4):
                    c = half * 4 + j
                    S.op("pe", lambda e, ps=ps, j=j, c=c, io=io: e.transpose(
                        ps[:, j * 128:(j + 1) * 128], io[:, c * 128:(c + 1) * 128], ident),
                        reads=[io, self.cm], writes=[ps], inc=(j == 3))
                for j in range(4):
                    c = half * 4 + j
                    S.op("dve", lambda e, ps=ps, j=j, c=c, q=q:
                         e.tensor_copy(self.h[c][:, q * 128:(q + 1) * 128], ps[:, j * 128:(j + 1) * 128]),
                         reads=[ps], writes=[self.h[c]])
        for l in self.layers:
            if self.do_mix:
                if it == 0:
                    self.phase_i += 1
                    self.convert_ahead(self.phase_i + 1)
                getattr(self, f"mixer{l}")(it)
            if self.do_ffn:
                if it == 0:
                    self.phase_i += 1
                    self.convert_ahead(self.phase_i + 1)
                self.ffn(l)
        self.phase()
        o = self.abufs("fin", 8, T, F32)
        self.rmsnorm_f32(self.h, "norm_final", 0, o)
        for q in range(NQ):
            for half in range(2):
                ps = self.pring.next()
                for j in range(4):
                    c = half * 4 + j
                    S.op("pe", lambda e, ps=ps, j=j, c=c, q=q: e.transpose(
                        ps[:, j * 128:(j + 1) * 128], o[c][:, q * 128:(q + 1) * 128], ident),
                        reads=[o[c], self.cm], writes=[ps], inc=(j == 3))
                S.op("dve", lambda e, ps=ps, q=q, half=half: e.tensor_copy(
                    self.io[q % 2][:, half * 512:(half + 1) * 512], ps[:]), reads=[ps], writes=[self.io[q % 2]])
            r0 = it * T + q * 128
            S.dma("act", lambda e, q=q, r0=r0: e.dma_start(out=self.y_d[r0:r0 + 128, :], in_=self.io[q % 2][:]),
                  reads=[self.io[q % 2]])

    def rmsnorm_f32(self, src, gname, goff, dst):
        self.rmsnorm(src, gname, goff, dst, D)

    def ffn(self, l):
        S = self.S
        self.phase()
        self.rmsnorm(self.h, "norm_ffn", l * 8, self.u, D)
        hid = self.abufs("hid", 22, T, BF16)
        raws = Ring(self.abufs("fraw", 6, T + 2, BF16))
        sgs = Ring(self.abufs("fsg", 3, T, F32))
        tail = self.f_tail[l]
        bo = self.poff["f_conv_b"] + l * 44
        name = f"f_up{l}"
        pend = None

        def conv_stage(b, slot, rg, rv):
            psg, psv = self.pring.next(), self.pring.next()
            for half, raw, ps2 in ((0, rg, psg), (1, rv, psv)):
                for k in range(3):
                    lo = 2048 + half * 384 + k * 128
                    S.op("pe", lambda e, ps2=ps2, slot=slot, lo=lo, raw=raw, k=k: e.matmul(
                        ps2[:, 0:T], slot[:, lo:lo + 128], raw[:, k:k + T], start=(k == 0), stop=(k == 2)),
                        reads=[slot, raw], writes=[ps2], inc=(k == 2))
            sg = sgs.next()
            bg = self.pcols[:, bo + b:bo + b + 1]
            bv = self.pcols[:, bo + 22 + b:bo + 22 + b + 1]
            S.op("act", lambda e: e.activation(out=sg[:], in_=psg[:, 0:T], func=AF.Silu, bias=bg),
                 reads=[psg, self.pcols], writes=[sg])
            S.op("dve", lambda e: e.scalar_tensor_tensor(hid[b][:], psv[:, 0:T], bv, sg[:], ALU.add, ALU.mult),
                 reads=[psv, sg, self.pcols], writes=[hid[b]])

        for b in range(22):
            slot = self.wload(name, b)
            rr = []
            for half in range(2):
                ps = self.pring.next()
                for kc in range(8):
                    lo = kc * 256 + half * 128
                    S.op("pe", lambda e, ps=ps, slot=slot, lo=lo, kc=kc: e.matmul(
                        ps[:, 0:T], slot[:, lo:lo + 128], self.u[kc][:], start=(kc == 0), stop=(kc == 7)),
                        reads=[slot, self.u[kc]], writes=[ps], inc=(kc == 7))
                idx = half * 22 + b
                raw = raws.next()
                S.op("pool", lambda e, raw=raw, idx=idx: e.tensor_copy(raw[:, 0:2], tail[:, idx, :]),
                     reads=[tail], writes=[raw])
                S.op("act", lambda e, raw=raw, ps=ps: e.activation(out=raw[:, 2:2 + T], in_=ps[:, 0:T],
                                                                   func=AF.Identity),
                     reads=[ps, raw], writes=[raw])
                S.op("pool", lambda e, raw=raw, idx=idx: e.tensor_copy(tail[:, idx, :], raw[:, T:T + 2]),
                     reads=[raw], writes=[tail])
                rr.append(raw)
            if pend is not None:
                conv_stage(*pend)
            pend = (b, slot, rr[0], rr[1])
        conv_stage(*pend)

        def evac2(oc, ps):
            S.op("dve", lambda e: e.tensor_tensor(self.h[oc][:], self.h[oc][:], ps[:], ALU.add),
                 reads=[self.h[oc], ps], writes=[self.h[oc]])

        self.dense_fm(f"f_down{l}", lambda kc: (hid[kc][:], hid[kc]), evac2)

    def resid_evac(self, bname):
        S = self.S

        def evac(oc, ps):
            b = self.pc(bname, oc)
            S.op("dve", lambda e: e.scalar_tensor_tensor(self.h[oc][:], ps[:], b, self.h[oc][:], ALU.add, ALU.add),
                 reads=[ps, self.h[oc], self.pcols], writes=[self.h[oc]])
        return evac

    def rstd_from_var(self, var, eps):
        S = self.S
        S.op("act", lambda e: e.activation(out=var[:], in_=var[:], func=AF.Ln, bias=eps), reads=[var], writes=[var])
        S.op("act", lambda e: e.activation(out=var[:], in_=var[:], func=AF.Exp, scale=-0.5), reads=[var], writes=[var])

    def mixer1(self, it):
        S = self.S
        self.phase()
        self.rmsnorm(self.h, "norm_mix", 8, self.u, D)
        ta = Ring(self.abufs("m1a", 2, T, F32))
        tb = Ring(self.abufs("m1b", 2, T, F32))
        cv = self.abufs("cv", 8, T, F32)
        sl = self.abufs("sl", 8, T, BF16)
        hcx = self.hcx
        pend = {}

        def evac(oc, ps):
            c, half = oc // 2, oc % 2
            bcol = self.pc("b_pw1_b", oc)
            if half == 0:
                a = ta.next()
                S.op("act", lambda e: e.activation(out=a[:], in_=ps[:], func=AF.Identity, bias=bcol),
                     reads=[ps, self.pcols], writes=[a])
                pend[c] = a
            else:
                a = pend.pop(c)
                sb = tb.next()
                S.op("act", lambda e: e.activation(out=sb[:], in_=ps[:], func=AF.Sigmoid, bias=bcol),
                     reads=[ps, self.pcols], writes=[sb])
                S.op("dve", lambda e: e.tensor_tensor(hcx[c][:, 30:30 + T], a[:], sb[:], ALU.mult),
                     reads=[a, sb], writes=[hcx[c]])

        self.dense_fm("b_pw1", lambda kc: (self.u[kc][:], self.u[kc]), evac)
        for c in range(8):
            slot = self.wload("b_conv", c)
            ps = self.pring.next()
            for k in range(31):
                S.op("pe", lambda e, ps=ps, slot=slot, k=k, c=c: e.matmul(
                    ps[:, 0:T], slot[:, k * 128:(k + 1) * 128], hcx[c][:, k:k + T], start=(k == 0), stop=(k == 30)),
                    reads=[slot, hcx[c]], writes=[ps], inc=(k == 30))
            bcol = self.pc("b_dw_b", c)
            S.op("act", lambda e, ps=ps, c=c, bcol=bcol: e.activation(out=cv[c][:], in_=ps[:], func=AF.Identity,
                                                                     bias=bcol),
                 reads=[ps, self.pcols], writes=[cv[c]])
            S.op("pool", lambda e, c=c: e.tensor_copy(hcx[c][:, 0:30], hcx[c][:, T:T + 30]),
                 reads=[hcx[c]], writes=[hcx[c]])
        onesb = self.cmb[:, 384:512]
        cvb = Ring(self.abufs("cvb", 2, T, BF16))
        sq = Ring(self.abufs("csq", 2, T, BF16))
        pA = self.pstat.next()
        pB = self.pring.next()
        for c in range(8):
            b1 = cvb.next()
            S.op("pool", lambda e, b1=b1, c=c: e.tensor_copy(b1[:], cv[c][:]), reads=[cv[c]], writes=[b1])
            S.op("pe", lambda e, b1=b1, c=c: e.matmul(pA[:], onesb, b1[:], start=(c == 0), stop=(c == 7)),
                 reads=[b1, self.cmb], writes=[pA])
            b2 = sq.next()
            S.op("act", lambda e, b2=b2, c=c: e.activation(out=b2[:], in_=cv[c][:], func=AF.Square),
                 reads=[cv[c]], writes=[b2])
            S.op("pe", lambda e, b2=b2, c=c: e.matmul(pB[:], onesb, b2[:], start=(c == 0), stop=(c == 7)),
                 reads=[b2, self.cmb], writes=[pB])
        mean = self.abuf("lnmean", T, F32)
        var = self.abuf("lnvar", T, F32)
        S.op("act", lambda e: e.activation(out=mean[:], in_=pA[:], func=AF.Identity, scale=1.0 / D),
             reads=[pA], writes=[mean])
        S.op("dve", lambda e: e.tensor_tensor(var[:], mean[:], mean[:], ALU.mult), reads=[mean], writes=[var])
        S.op("dve", lambda e: e.scalar_tensor_tensor(var[:], pB[:], 1.0 / D, var[:], ALU.mult, ALU.subtract),
             reads=[pB, var], writes=[var])
        self.rstd_from_var(var, 1e-5)
        for c in range(8):
            S.op("dve", lambda e, c=c: e.tensor_tensor(cv[c][:], cv[c][:], mean[:], ALU.subtract),
                 reads=[cv[c], mean], writes=[cv[c]])
            S.op("pool", lambda e, c=c: e.tensor_tensor(cv[c][:], cv[c][:], var[:], ALU.mult),
                 reads=[cv[c], var], writes=[cv[c]])
            g, b = self.pc("b_ln_g", c), self.pc("b_ln_b", c)
            S.op("act", lambda e, c=c, g=g, b=b: e.activation(out=sl[c][:], in_=cv[c][:], func=AF.Silu,
                                                             scale=g, bias=b),
                 reads=[cv[c], self.pcols], writes=[sl[c]])
        self.dense_fm("b_pw2", lambda kc: (sl[kc][:], sl[kc]), self.resid_evac("b_pw2_b"))

    def mixer2(self, it):
        S = self.S
        self.phase()
        self.rmsnorm(self.h, "norm_mix", 16, self.u, D)
        gg = self.abufs("gg", 10, T, BF16)
        xc = self.abufs("xc", 10, T, F32)
        xcb = self.abufs("xcb", 10, T, BF16)
        rr = self.abufs("rr", 10, T, F32)
        ii = self.abufs("ii", 10, T, F32)
        tmp = Ring(self.abufs("ltmp", 5, T, F32))
        raws = Ring(self.abufs("craw", 2, T + 3, F32))
        wo = self.poff["c_conv_w"]

        def evac(oc, ps):
            bcol = self.pc("c_in_b", oc)
            if oc < 10:
                S.op("act", lambda e: e.activation(out=gg[oc][:], in_=ps[:], func=AF.Gelu_apprx_tanh, bias=bcol),
                     reads=[ps, self.pcols], writes=[gg[oc]])
            else:
                j = oc - 10
                wc = [self.pcols[:, wo + k * 10 + j: wo + k * 10 + j + 1] for k in range(4)]
                self.conv_taps(ps, raws.next(), xc[j], self.c_tail, self.c_tail[:, j, :], wc,
                               self.pc("c_conv_b", j), 4, in_bias=bcol)
                S.op("pool", lambda e: e.tensor_copy(xcb[j][:], xc[j][:]), reads=[xc[j]], writes=[xcb[j]])

        self.dense_fm("c_in", lambda kc: (self.u[kc][:], self.u[kc]), evac)
        for nm, bn, dst in (("c_ga", "c_ga_b", rr), ("c_gx", "c_gx_b", ii)):
            for oc in range(10):
                slot = self.wload(nm, oc)
                ps = self.pring.next()
                for kc in range(2):
                    src = xcb[(oc // 2) * 2 + kc]
                    S.op("pe", lambda e, ps=ps, slot=slot, kc=kc, src=src: e.matmul(
                        ps[:, 0:T], slot[:, kc * 128:(kc + 1) * 128], src[:], start=(kc == 0), stop=(kc == 1)),
                        reads=[slot, src], writes=[ps], inc=(kc == 1))
                bcol = self.pc(bn, oc)
                S.op("act", lambda e, ps=ps, oc=oc, bcol=bcol, dst=dst: e.activation(
                    out=dst[oc][:], in_=ps[:], func=AF.Sigmoid, bias=bcol),
                    reads=[ps, self.pcols], writes=[dst[oc]])
        for oc0 in (0, 5):
          t2s = {}
          for oc in range(oc0, oc0 + 5):
              t1 = tmp.next()
              S.op("act", lambda e, oc=oc, t1=t1: e.activation(out=t1[:], in_=rr[oc][:], func=AF.Exp,
                                                               scale=self.c_nsp2[:, oc:oc + 1]),
                   reads=[rr[oc], self.c_nsp2], writes=[t1])
              S.op("act", lambda e, oc=oc: e.activation(out=rr[oc][:], in_=rr[oc][:], func=AF.Exp,
                                                        scale=self.c_nsp[:, oc:oc + 1]),
                   reads=[rr[oc], self.c_nsp], writes=[rr[oc]])
              t2s[oc] = t1
          for oc in range(oc0, oc0 + 5):
              t1 = t2s[oc]
              S.op("act", lambda e, t1=t1: e.activation(out=t1[:], in_=t1[:], func=AF.Sqrt, scale=-1.0, bias=1.0),
                   reads=[t1], writes=[t1])
              S.op("dve", lambda e, oc=oc: e.tensor_tensor(ii[oc][:], ii[oc][:], xc[oc][:], ALU.mult),
                   reads=[ii[oc], xc[oc]], writes=[ii[oc]])
              S.op("dve", lambda e, oc=oc, t1=t1: e.tensor_tensor(ii[oc][:], ii[oc][:], t1[:], ALU.mult),
                   reads=[ii[oc], t1], writes=[ii[oc]])
              S.op("dve", lambda e, oc=oc: e.tensor_tensor_scan(
                  xc[oc][:], rr[oc][:], ii[oc][:], self.c_hst[:, oc:oc + 1], ALU.mult, ALU.add),
                  reads=[rr[oc], ii[oc], self.c_hst], writes=[xc[oc]])
              S.op("pool", lambda e, oc=oc: e.tensor_copy(self.c_hst[:, oc:oc + 1], xc[oc][:, T - 1:T]),
                   reads=[xc[oc]], writes=[self.c_hst])
              S.op("pool", lambda e, oc=oc: e.tensor_tensor(xcb[oc][:], gg[oc][:], xc[oc][:], ALU.mult),
                   reads=[gg[oc], xc[oc]], writes=[xcb[oc]])
        self.dense_fm("c_out", lambda kc: (xcb[kc][:], xcb[kc]), self.resid_evac("c_out_b"))

    def mixer3(self, it):
        S = self.S
        self.phase()
        self.rmsnorm(self.h, "norm_mix", 24, self.u, D)
        ug = self.abufs("ug", 16, T, BF16)
        vt = self.abufs("vt", NQ, 2048, F32)
        vtb = self.abufs("vtb", NQ, 2048, BF16)
        gated = self.abufs("gated", 16, T, BF16)

        def evac(oc, ps):
            bcol = self.pc("d_in_b_u", oc)
            S.op("act", lambda e: e.activation(out=ug[oc][:], in_=ps[:], func=AF.Gelu_apprx_tanh, bias=bcol),
                 reads=[ps, self.pcols], writes=[ug[oc]])

        self.dense_fm("d_in_u", lambda kc: (self.u[kc][:], self.u[kc]), evac)
        for blk in range(4):
            bsl = self.wload(None, 0, f32src=self.dbc_d[0][:, blk * 512:(blk + 1) * 512])
            bf = bsl[:, 0:1024].bitcast(F32)
            slot = self.wload("d_in_v", blk)
            for q in range(NQ):
                ps = self.pring.next()
                for kc in range(8):
                    S.op("pe", lambda e, ps=ps, slot=slot, kc=kc, q=q: e.matmul(
                        ps[:, 0:512], self.u[kc][:, q * 128:(q + 1) * 128], slot[:, kc * 512:(kc + 1) * 512],
                        start=(kc == 0), stop=(kc == 7)), reads=[slot, self.u[kc]], writes=[ps], inc=(kc == 7))
                S.op("dve", lambda e, ps=ps, q=q, blk=blk, bf=bf: e.tensor_tensor(
                    vt[q][:, blk * 512:(blk + 1) * 512], ps[:, 0:512], bf, ALU.add),
                    reads=[ps, bsl], writes=[vt[q]])
        gsl = self.wload(None, 0, f32src=self.dbc_d[1])
        gf = gsl[:, 0:4096].bitcast(F32)
        b2sl = self.wload(None, 0, f32src=self.dbc_d[2])
        b2f = b2sl[:, 0:4096].bitcast(F32)
        st = self.abuf("bnst", NQ * 24, F32)
        mv = self.abuf("bnmv", NQ * 2, F32)
        for q in range(NQ):
            S.op("act", lambda e, q=q: e.activation(out=vt[q][:], in_=vt[q][:], func=AF.Gelu_apprx_tanh),
                 reads=[vt[q]], writes=[vt[q]])
            for j in range(4):
                S.op("dve", lambda e, q=q, j=j: e.bn_stats(st[:, q * 24 + j * 6: q * 24 + (j + 1) * 6],
                                                           vt[q][:, j * 512:(j + 1) * 512]),
                     reads=[vt[q]], writes=[st])
            S.op("dve", lambda e, q=q: e.bn_aggr(mv[:, q * 2:q * 2 + 2], st[:, q * 24:(q + 1) * 24]),
                 reads=[st], writes=[mv])
        S.op("act", lambda e: e.activation(out=st[:, 0:NQ], in_=mv[:].rearrange("p (q two) -> p q two", two=2)[:, :, 1],
                                           func=AF.Ln, bias=1e-5), reads=[mv, st], writes=[st])
        S.op("act", lambda e: e.activation(out=st[:, 0:NQ], in_=st[:, 0:NQ], func=AF.Exp, scale=-0.5),
             reads=[st], writes=[st])
        for q in range(NQ):
            S.op("dve", lambda e, q=q: e.tensor_scalar(vt[q][:], vt[q][:], mv[:, 2 * q:2 * q + 1], st[:, q:q + 1],
                                                       ALU.subtract, ALU.mult),
                 reads=[vt[q], mv, st], writes=[vt[q]])
            S.op("pool", lambda e, q=q: e.tensor_tensor(vt[q][:], vt[q][:], gf, ALU.mult),
                 reads=[vt[q], gsl], writes=[vt[q]])
            S.op("dve", lambda e, q=q: e.tensor_tensor(vtb[q][:], vt[q][:], b2f, ALU.add),
                 reads=[vt[q], b2sl], writes=[vtb[q]])
        spsl = self.wload(None, 0, f32src=self.dbc_d[3][:, 0:1024])
        spf = spsl[:, 0:2048].bitcast(F32)
        ones1 = self.cm[0:1, 384:512]
        for j in range(16):
            g = j // 2
            ps = self.pring.next()
            for q in range(NQ):
                S.op("pe", lambda e, ps=ps, q=q, j=j, g=g: e.matmul(
                    ps[:, q * 128:(q + 1) * 128], vtb[q][:, j * 128:(j + 1) * 128],
                    self.wtsp[:, g * 128:(g + 1) * 128], start=True, stop=False),
                    reads=[vtb[q], self.wtsp], writes=[ps], inc=False)
                S.op("pe", lambda e, ps=ps, q=q, g=g: e.matmul(
                    ps[:, q * 128:(q + 1) * 128], ones1, spf[0:1, g * 128:(g + 1) * 128], start=False, stop=True),
                    reads=[spsl, self.cm], writes=[ps], inc=(q == NQ - 1))
            S.op("dve", lambda e, ps=ps, j=j: e.tensor_tensor(gated[j][:], ps[:], ug[j][:], ALU.mult),
                 reads=[ps, ug[j]], writes=[gated[j]])
        self.dense_fm("d_out", lambda kc: (gated[kc][:], gated[kc]), self.resid_evac("d_out_b"))

    def mixer0(self, it):
        S = self.S
        self.phase()
        y = self.abufs("ssd_y", 16, T, F32)
        self.rmsnorm(self.h, "norm_mix", 0, self.u, D)
        xbc = self.abufs("xbc", 32, T, BF16)
        raws = Ring(self.abufs("araw32", 2, T + 3, F32))
        accs = Ring(self.abufs("aacc", 2, T, F32))
        identb = self.cmb[:, 0:128]
        mask01 = self.cmb[:, 128:256]
        tri = self.cm[:, 128:256]
        Umat = self.cm[:, 256:384]
        onesf = self.cm[:, 384:512]

        araws = Ring(self.abufs("araw", 6, T + 4, BF16))
        pend = None

        def conv_stage(slot, items):
            for (idx, j, raw) in items:
                ps2 = self.pring.next()
                for k in range(4):
                    lo = 2048 + j * 512 + k * 128
                    S.op("pe", lambda e, ps2=ps2, slot=slot, lo=lo, raw=raw, k=k: e.matmul(
                        ps2[:, 0:T], slot[:, lo:lo + 128], raw[:, k:k + T], start=(k == 0), stop=(k == 3)),
                        reads=[slot, raw], writes=[ps2], inc=(k == 3))
                bcol = self.pc("a_conv_b", idx)
                S.op("act", lambda e, ps2=ps2, idx=idx, bcol=bcol: e.activation(
                    out=xbc[idx][:], in_=ps2[:, 0:T], func=AF.Silu, bias=bcol),
                    reads=[ps2, self.pcols], writes=[xbc[idx]])

        for b in range(8, 24):
            slot = self.wload("a_in_zx", b)
            items = []
            for j in range(2):
                ps = self.pring.next()
                for kc in range(8):
                    lo = kc * 256 + j * 128
                    S.op("pe", lambda e, ps=ps, slot=slot, lo=lo, kc=kc: e.matmul(
                        ps[:, 0:T], slot[:, lo:lo + 128], self.u[kc][:], start=(kc == 0), stop=(kc == 7)),
                        reads=[slot, self.u[kc]], writes=[ps], inc=(kc == 7))
                idx = (b - 8) * 2 + j
                raw = araws.next()
                S.op("pool", lambda e, raw=raw, idx=idx: e.tensor_copy(raw[:, 0:3], self.a_tail[:, idx, :]),
                     reads=[self.a_tail], writes=[raw])
                S.op("act", lambda e, raw=raw, ps=ps: e.activation(out=raw[:, 3:3 + T], in_=ps[:, 0:T],
                                                                   func=AF.Identity),
                     reads=[ps, raw], writes=[raw])
                S.op("pool", lambda e, raw=raw, idx=idx: e.tensor_copy(self.a_tail[:, idx, :], raw[:, T:T + 3]),
                     reads=[raw], writes=[self.a_tail])
                items.append((idx, j, raw))
            if pend is not None:
                conv_stage(*pend)
            pend = (slot, items)
        conv_stage(*pend)
        slot = self.wload("a_in_dt", 0)
        dt = self.abufs("dt", NQ, 32, F32)
        dta = self.abufs("dta", NQ, 32, F32)
        dhi = self.abufs("dhi", NQ, 32, BF16)
        dlo = self.abufs("dlo", NQ, 32, BF16)
        for q in range(NQ):
            pd = self.pa.next()
            for kc in range(8):
                S.op("pe", lambda e, pd=pd, kc=kc, q=q: e.matmul(
                    pd[:, 0:32], self.u[kc][:, q * 128:(q + 1) * 128], slot[:, kc * 32:(kc + 1) * 32],
                    start=(kc == 0), stop=(kc == 7)), reads=[slot, self.u[kc]], writes=[pd], inc=(kc == 7))
            S.op("dve", lambda e, pd=pd, q=q: e.tensor_tensor(dt[q][:], pd[:, 0:32], self.prows[:, 0:32], ALU.add),
                 reads=[pd, self.prows], writes=[dt[q]])
            S.op("act", lambda e, q=q: e.activation(out=dt[q][:], in_=dt[q][:], func=AF.Exp),
                 reads=[dt[q]], writes=[dt[q]])
            S.op("act", lambda e, q=q: e.activation(out=dt[q][:], in_=dt[q][:], func=AF.Ln, bias=1.0),
                 reads=[dt[q]], writes=[dt[q]])
            S.op("dve", lambda e, q=q: e.tensor_tensor(dta[q][:], dt[q][:], self.a_neg[:], ALU.mult),
                 reads=[dt[q], self.a_neg], writes=[dta[q]])
            S.op("dve", lambda e, q=q: e.tensor_copy(dhi[q][:], dta[q][:]), reads=[dta[q]], writes=[dhi[q]])
            S.op("dve", lambda e, q=q: e.tensor_tensor(dlo[q][:], dta[q][:], dhi[q][:], ALU.subtract),
                 reads=[dta[q], dhi[q]], writes=[dlo[q]])
        STOP = int(os.environ.get("K_STOP", "99"))
        if STOP <= 2:
            return
        acs_b = self.abufs("acs", 2, 32, F32)
        dex_b = self.abufs("dex", 2, 64, F32)
        xdt = self.abuf("xdt", 2048, BF16)
        xde = self.abuf("xde", 2048, BF16)
        btok = self.abuf("btok", 1024, BF16)
        cbms = Ring(self.abufs("cbm", 3, 128, BF16))
        sgs = Ring(self.abufs("sg", 2, 512, F32))
        Es = Ring(self.abufs("Eh", 2, 512, BF16))
        eAs = Ring(self.abufs("eA", 2, 512, BF16))
        scs = Ring(self.abufs("sc", 2, 512, BF16))
        Css = Ring(self.abufs("Cs", 2, 512, BF16))
        for q in range(NQ):
            cs = slice(q * 128, (q + 1) * 128)
            acs, dex = acs_b[q % 2], dex_b[q % 2]
            p3 = self.pa.next()
            S.op("pe", lambda e, p3=p3, q=q: e.matmul(p3[:, 0:32], tri, dta[q][:], start=True, stop=True),
                 reads=[self.cm, dta[q]], writes=[p3], inc=False)
            S.op("pe", lambda e, p3=p3, q=q: e.matmul(p3[:, 32:64], Umat, dta[q][:], start=True, stop=True),
                 reads=[self.cm, dta[q]], writes=[p3], inc=False)
            S.op("pe", lambda e, p3=p3, q=q: e.matmul(p3[:, 64:96], onesf, dta[q][:], start=True, stop=True),
                 reads=[self.cm, dta[q]], writes=[p3])
            S.op("act", lambda e, p3=p3, acs=acs: e.activation(out=acs[:], in_=p3[:, 0:32], func=AF.Identity),
                 reads=[p3], writes=[acs])
            S.op("act", lambda e, p3=p3, dex=dex: e.activation(out=dex[:], in_=p3[:, 32:96], func=AF.Exp),
                 reads=[p3], writes=[dex])
            pt = self.pT
            if os.environ.get("K_SUB") == "A":
                continue
            for bt in range(2 if os.environ.get("K_SUB") != "B" else 0):
                for j in range(8):
                    xcn = bt * 8 + j
                    S.op("pe", lambda e, j=j, xcn=xcn, cs=cs: e.transpose(
                        pt[:, j * 128:(j + 1) * 128], xbc[xcn][:, cs], identb),
                        reads=[xbc[xcn], self.cmb], writes=[pt], inc=(j == 7))
                S.op("dve", lambda e, bt=bt, q=q: e.tensor_tensor(
                    xdt[:, bt * 1024:(bt + 1) * 1024].rearrange("p (h d) -> p h d", h=16),
                    pt[:].rearrange("p (h d) -> p h d", h=16),
                    dt[q][:, bt * 16:(bt + 1) * 16].unsqueeze(2).to_broadcast([128, 16, 64]), ALU.mult),
                    reads=[pt, dt[q]], writes=[xdt])
            for j in range(8):
                bcn = 16 + j
                S.op("pe", lambda e, j=j, bcn=bcn, cs=cs: e.transpose(
                    pt[:, j * 128:(j + 1) * 128], xbc[bcn][:, cs], identb),
                    reads=[xbc[bcn], self.cmb], writes=[pt], inc=(j == 7))
            S.op("dve", lambda e: e.tensor_copy(btok[:], pt[:]), reads=[pt], writes=[btok])
            if STOP <= 3:
                continue
            S.op("pool", lambda e, dex=dex: e.tensor_tensor(
                xde[:].rearrange("p (h d) -> p h d", h=32), xdt[:].rearrange("p (h d) -> p h d", h=32),
                dex[:, 0:32].unsqueeze(2).to_broadcast([128, 32, 64]), ALU.mult),
                reads=[xdt, dex], writes=[xde])
            if STOP <= 4:
                continue
            def stage_a(g, q=q, cs=cs):
                BT, CT = xbc[16 + g], xbc[24 + g]
                pyb = self.py.next()
                pab = self.pa.next()
                S.op("pe", lambda e, pyb=pyb, BT=BT, CT=CT, cs=cs: e.matmul(
                    pyb[:, 384:512], BT[:, cs], CT[:, cs], start=True, stop=True), reads=[BT, CT], writes=[pyb])
                cbm = cbms.next()
                S.op("dve", lambda e, pyb=pyb, cbm=cbm: e.tensor_tensor(cbm[:], pyb[:, 384:512], mask01, ALU.mult),
                     reads=[pyb, self.cmb], writes=[cbm])
                prb = None
                for bank in (pab,):
                    for j in range(4):
                        hh = 4 * g + j
                        S.op("pe", lambda e, bank=bank, hh=hh, q=q, j=j: e.matmul(
                            bank[:, j * 128:(j + 1) * 128], dhi[q][:, hh:hh + 1].to_broadcast([128, 128]), mask01,
                            start=True, stop=False), reads=[dhi[q], self.cmb], writes=[bank], inc=False)
                        S.op("pe", lambda e, bank=bank, hh=hh, q=q, j=j: e.matmul(
                            bank[:, j * 128:(j + 1) * 128], dlo[q][:, hh:hh + 1].to_broadcast([128, 128]), mask01,
                            start=False, stop=True), reads=[dlo[q], self.cmb], writes=[bank], inc=(j == 3))
                return (g, BT, CT, pyb, pab, prb, cbm)

            def stage_b1(ctx, q=q, cs=cs, acs=acs, dex=dex):
                g, BT, CT, pyb, pab, prb, cbm = ctx
                eA4, r4, E4, sc4, Cs4 = eAs.next(), sgs.next(), Es.next(), scs.next(), Css.next()
                S.op("act", lambda e, pab=pab, eA4=eA4: e.activation(out=eA4[:], in_=pab[:], func=AF.Exp),
                     reads=[pab], writes=[eA4])
                for j in range(4):
                    hh = 4 * g + j
                    S.op("act", lambda e, pab=pab, r4=r4, hh=hh, j=j, acs=acs: e.activation(
                        out=r4[:, j * 128:(j + 1) * 128], in_=pab[:, j * 128:(j + 1) * 128], func=AF.Relu,
                        scale=-1.0, bias=acs[:, hh:hh + 1]), reads=[pab, acs], writes=[r4])
                S.op("act", lambda e, r4=r4, E4=E4: e.activation(out=E4[:], in_=r4[:], func=AF.Exp, scale=-1.0),
                     reads=[r4], writes=[E4])
                S.op("dve", lambda e, sc4=sc4, E4=E4, cbm=cbm: e.tensor_tensor(
                    sc4[:].rearrange("p (h t) -> p h t", h=4), E4[:].rearrange("p (h t) -> p h t", h=4),
                    cbm[:].unsqueeze(1).to_broadcast([128, 4, 128]), ALU.mult), reads=[E4, cbm], writes=[sc4])
                S.op("pool", lambda e, Cs4=Cs4, eA4=eA4, CT=CT, cs=cs: e.tensor_tensor(
                    Cs4[:].rearrange("p (h t) -> p h t", h=4), eA4[:].rearrange("p (h t) -> p h t", h=4),
                    CT[:, cs].unsqueeze(1).to_broadcast([128, 4, 128]), ALU.mult), reads=[eA4, CT], writes=[Cs4])
                return (sc4, Cs4)

            def stage_b2(ctx, pre, q=q, cs=cs, acs=acs, dex=dex):
                g, BT, CT, pyb, pab, prb, cbm = ctx
                sc4, Cs4 = pre
                for j in range(4):
                    hh = 4 * g + j
                    xcn, half = hh // 2, hh % 2
                    lo = half * 64
                    pyr = (j // 2) * 128
                    S.op("pe", lambda e, pyb=pyb, pyr=pyr, lo=lo, hh=hh, sc4=sc4, j=j: e.matmul(
                        pyb[lo:lo + 64, pyr:pyr + 128], xdt[:, hh * 64:(hh + 1) * 64], sc4[:, j * 128:(j + 1) * 128],
                        start=True, stop=False, tile_position=(0, lo)), reads=[xdt, sc4], writes=[pyb], inc=False)
                    S.op("pe", lambda e, pyb=pyb, pyr=pyr, lo=lo, g=g, j=j, Cs4=Cs4: e.matmul(
                        pyb[lo:lo + 64, pyr:pyr + 128], self.stb[g][:, j * 64:(j + 1) * 64],
                        Cs4[:, j * 128:(j + 1) * 128], start=False, stop=True, tile_position=(0, lo)),
                        reads=[self.stb[g], Cs4], writes=[pyb])
                    if half == 1:
                        dsk = self.pc("a_dsk", xcn)
                        S.op("dve", lambda e, pyb=pyb, pyr=pyr, xcn=xcn, dsk=dsk, cs=cs: e.scalar_tensor_tensor(
                            y[xcn][:, cs], xbc[xcn][:, cs], dsk, pyb[:, pyr:pyr + 128], ALU.mult, ALU.add),
                            reads=[xbc[xcn], pyb, self.pcols], writes=[y[xcn]])
                pS = self.pstat.next()
                S.op("pe", lambda e, pS=pS, g=g: e.matmul(
                    pS[:, 0:256], btok[:, g * 128:(g + 1) * 128], xde[:, g * 256:(g + 1) * 256],
                    start=True, stop=True), reads=[btok, xde], writes=[pS])
                S.op("dve", lambda e, g=g, dex=dex: e.tensor_tensor(
                    self.st[g][:].rearrange("p (h d) -> p h d", h=4), self.st[g][:].rearrange("p (h d) -> p h d", h=4),
                    dex[:, 32 + 4 * g:36 + 4 * g].unsqueeze(2).to_broadcast([128, 4, 64]), ALU.mult),
                    reads=[self.st[g], dex], writes=[self.st[g]])
                S.op("dve", lambda e, g=g, pS=pS: e.tensor_tensor(self.st[g][:], self.st[g][:], pS[:, 0:256], ALU.add),
                     reads=[self.st[g], pS], writes=[self.st[g]])
                S.op("pool", lambda e, g=g: e.tensor_copy(self.stb[g][:], self.st[g][:]),
                     reads=[self.st[g]], writes=[self.stb[g]])

            ctxs = {0: stage_a(0), 1: stage_a(1)}
            pres = {0: stage_b1(ctxs[0])}
            for g in range(8):
                if g + 2 < 8:
                    ctxs[g + 2] = stage_a(g + 2)
                if g + 1 < 8:
                    pres[g + 1] = stage_b1(ctxs[g + 1])
                stage_b2(ctxs[g], pres[g])
        if STOP <= 5:
            return
        zss = accs

        def evac_z(oc, ps):
            zs = zss.next()
            S.op("act", lambda e: e.activation(out=zs[:], in_=ps[:], func=AF.Silu), reads=[ps], writes=[zs])
            S.op("pool", lambda e: e.tensor_tensor(y[oc][:], y[oc][:], zs[:], ALU.mult),
                 reads=[y[oc], zs], writes=[y[oc]])

        self.dense_fm("a_in_zx", lambda kc: (self.u[kc][:], self.u[kc]), evac_z, blocks=range(0, 8))
        yn = xbc[0:16]
        self.rmsnorm(y, "a_norm", 0, yn, 2048, tmp=(accs, raws))

        def evac_o(oc, ps):
            S.op("dve", lambda e: e.tensor_tensor(self.h[oc][:], self.h[oc][:], ps[:], ALU.add),
                 reads=[self.h[oc], ps], writes=[self.h[oc]])

        self.dense_fm("a_out", lambda kc: (yn[kc][:], yn[kc]), evac_o)


def blockify(W, ncb, perm=None):
    W = np.asarray(W, np.float32)
    if perm is not None:
        W = W[:, perm]
    K, N = W.shape
    KC, nblk = K // 128, N // ncb
    return np.ascontiguousarray(W.reshape(KC, 128, nblk, ncb).transpose(2, 1, 0, 3).reshape(nblk, 128, KC * ncb))


def cols(v):
    v = np.asarray(v, np.float32).reshape(-1)
    return v.reshape(-1, 128).T


def pair_perm(n_half, chunk=128):
    nch = n_half // chunk
    idx = []
    for c in range(nch):
        idx += list(range(c * chunk, (c + 1) * chunk))
        idx += list(range(n_half + c * chunk, n_half + (c + 1) * chunk))
    return np.array(idx)


def host_layout(inp, layers, do_ffn=True, ffn_layers=None):
    off, ncol = pcol_offsets()
    pcols = np.zeros((128, ncol), np.float32)

    def put(name, arr, o=0):
        a = np.asarray(arr, np.float32)
        pcols[:, off[name] + o: off[name] + o + a.shape[1]] = a

    for i in range(4):
        put("norm_mix", cols(inp["norm_mix"][i]), i * 8)
        put("norm_ffn", cols(inp["norm_ffn"][i]), i * 8)
    put("norm_final", cols(inp["norm_final"]))
    out = {}
    prows = np.zeros((128, 64), np.float32)
    cm = np.zeros((128, 512), np.float32)
    cm[:, 0:128] = np.eye(128)
    s = np.arange(128)
    cm[:, 128:256] = (s[:, None] <= s[None, :])
    cm[:, 256:384] = (s[:, None] > s[None, :])
    cm[:, 384:512] = 1.0
    out["cmats"] = cm
    if 0 in layers:
        cw = np.asarray(inp["a_conv_w"][0])
        for k in range(4):
            put("a_conv_w", cols(cw[k]), k * 32)
        put("a_conv_b", cols(inp["a_conv_b"][0]))
        put("a_dsk", cols(np.repeat(np.asarray(inp["a_d_skip"][0]), 64)))
        put("a_norm", cols(inp["a_norm"][0]))
        prows[:, 0:32] = np.asarray(inp["a_dt_bias"][0])[None, :]
        prows[:, 32:64] = np.asarray(inp["a_log"][0])[None, :]
        W = np.asarray(inp["a_in_proj"][0])
        zx = blockify(W[:, :6144], 256)
        dg = np.zeros((24, 128, 2, 4, 128), np.float32)
        ar = np.arange(128)
        for b in range(8, 24):
            for j in range(2):
                f0 = ((b - 8) * 2 + j) * 128
                for k in range(4):
                    dg[b, ar, j, k, ar] = cw[k, f0:f0 + 128]
        out["w_a_in_zx"] = np.concatenate([zx, dg.reshape(24, 128, 1024)], axis=2)
        out["w_a_in_dt"] = blockify(W[:, 6144:6176], 32)
        out["w_a_out"] = blockify(inp["a_out_proj"][0], 128)
    if 1 in layers:
        pp = pair_perm(1024)
        put("b_pw1_b", cols(np.asarray(inp["b_pw1_b"][0])[pp]))
        put("b_dw_b", cols(inp["b_dw_b"][0]))
        put("b_ln_g", cols(inp["b_ln_g"][0]))
        put("b_ln_b", cols(inp["b_ln_b"][0]))
        put("b_pw2_b", cols(inp["b_pw2_b"][0]))
        out["w_b_pw1"] = blockify(inp["b_pw1_w"][0], 256, pp)
        dw = np.asarray(inp["b_dw_w"][0], np.float32)
        cv = np.zeros((8, 128, 31, 128), np.float32)
        ar = np.arange(128)
        for c in range(8):
            for k in range(31):
                cv[c, ar, k, ar] = dw[k, c * 128:(c + 1) * 128]
        out["w_b_conv"] = cv.reshape(8, 128, 31 * 128)
        out["w_b_pw2"] = blockify(inp["b_pw2_w"][0], 256)
    if 2 in layers:
        put("c_in_b", cols(inp["c_in_b"][0]))
        cw = np.asarray(inp["c_conv_w"][0])
        for k in range(4):
            put("c_conv_w", cols(cw[k]), k * 10)
        put("c_conv_b", cols(inp["c_conv_b"][0]))
        put("c_ga_b", cols(np.asarray(inp["c_ga_b"][0]).reshape(-1)))
        put("c_gx_b", cols(np.asarray(inp["c_gx_b"][0]).reshape(-1)))
        put("c_lambda", cols(inp["c_lambda"][0]))
        put("c_out_b", cols(inp["c_out_b"][0]))
        out["w_c_in"] = blockify(inp["c_in_w"][0], 256)
        for nm, key in (("w_c_ga", "c_ga_w"), ("w_c_gx", "c_gx_w")):
            g = np.asarray(inp[key][0], np.float32)
            blk = np.zeros((10, 128, 2, 128), np.float32)
            for oc in range(10):
                bi, half = oc // 2, oc % 2
                blk[oc] = g[bi, :, half * 128:(half + 1) * 128].reshape(2, 128, 128).transpose(1, 0, 2)
            out[nm] = blk.reshape(10, 128, 256)
        out["w_c_out"] = blockify(inp["c_out_w"][0], 128)
    if 3 in layers:
        ib = np.asarray(inp["d_in_b"][0], np.float32)
        put("d_in_b_u", cols(ib[:2048]))
        put("d_out_b", cols(inp["d_out_b"][0]))
        W = np.asarray(inp["d_in_w"][0])
        out["w_d_in_u"] = blockify(W[:, :2048], 256)
        out["w_d_in_v"] = blockify(W[:, 2048:], 512)
        out["w_d_out"] = blockify(inp["d_out_w"][0], 128)
        spw = np.asarray(inp["d_sp_w"][0], np.float32)
        out["wt_sp"] = np.ascontiguousarray(spw.transpose(2, 0, 1).reshape(128, 1024))
        dbc = np.zeros((4, 128, 2048), np.float32)
        dbc[0] = ib[2048:][None, :]
        dbc[1] = np.asarray(inp["d_ln_g"][0])[None, :]
        dbc[2] = np.asarray(inp["d_ln_b"][0])[None, :]
        dbc[3, :, :1024] = np.asarray(inp["d_sp_b"][0]).reshape(-1)[None, :]
        out["d_bc"] = dbc
    if do_ffn:
        pp = pair_perm(2816)
        for l in (ffn_layers if ffn_layers is not None else layers):
            cw = np.asarray(inp["f_conv_w"][l])
            for k in range(3):
                put("f_conv_w", cols(cw[k]), l * 132 + k * 44)
            put("f_conv_b", cols(inp["f_conv_b"][l]), l * 44)
            up = blockify(inp["f_up_w"][l], 256, pp)
            dg = np.zeros((22, 128, 2, 3, 128), np.float32)
            ar = np.arange(128)
            for b in range(22):
                for half in range(2):
                    f0 = half * 2816 + b * 128
                    for k in range(3):
                        dg[b, ar, half, k, ar] = cw[k, f0:f0 + 128]
            out[f"w_f_up{l}"] = np.concatenate([up, dg.reshape(22, 128, 768)], axis=2)
            out[f"w_f_down{l}"] = blockify(inp["f_down_w"][l], 128)
    out["pcols"] = pcols
    out["prows"] = prows
    return out


_NC_CACHE = {}


def run(inp, Lseq, nbatch, layers=(0, 1, 2, 3), do_ffn=True, ncores=None, do_mix=True):
    key = (Lseq, tuple(layers), do_ffn, do_mix)
    if key not in _NC_CACHE:
        _NC_CACHE[key] = Builder(Lseq, layers, do_ffn, do_mix).build()
    nc = _NC_CACHE[key]
    shared = host_layout(inp, layers if do_mix else (), do_ffn, ffn_layers=layers)
    x = np.asarray(inp["x"], np.float32)
    ncores = ncores or nbatch
    in_maps = []
    for c in range(ncores):
        m = dict(shared)
        m["x"] = np.ascontiguousarray(x[c % nbatch, :Lseq])
        in_maps.append(m)
    res = run_bass_kernel_spmd(nc, in_maps, core_ids=list(range(ncores)))
    return np.stack([res.results[b]["y"] for b in range(nbatch)], 0)


def kernel(**inputs):
    return run(inputs, 8192, 4, ncores=8)
```

```python
import os
import numpy as np
from contextlib import ExitStack
import concourse.bass as bass
import concourse.mybir as mybir
from concourse.bass_utils import run_bass_kernel_spmd

F32 = mybir.dt.float32
BF16 = mybir.dt.bfloat16
AF = mybir.ActivationFunctionType
ALU = mybir.AluOpType

D = 1024
T = 512
NQ = T // 128
ENGS = ("pe", "act", "dve", "pool", "sp")
SLOT_EL = 4096
NSLOT = 4
ARENA_CAP = 108 * 1024


class Buf:
    __slots__ = ("name", "t", "lw", "rd", "excl")

    def __init__(self, name, t, excl=False):
        self.name = name
        self.t = t
        self.lw = None
        self.rd = {}
        self.excl = excl

    def __getitem__(self, idx):
        return self.t[idx]


class Sched:
    SEM_ROLL = 30000

    def __init__(self, nc, stack):
        self.nc = nc
        self.stack = stack
        self.ops = {e: [] for e in ENGS}
        self.sems = {}
        self.cur = {}
        self.gen = {e: 0 for e in ENGS}
        self.waited = {e: {} for e in ENGS}
        self.nins = 0
        for e in ENGS:
            self._new_sem(e)

    def _sem(self, key):
        if key not in self.sems:
            self.sems[key] = self.stack.enter_context(self.nc.semaphore("s_" + "_".join(str(k) for k in key)))
        return self.sems[key]

    def _new_sem(self, e):
        key = (e, self.gen[e])
        self.gen[e] += 1
        self._sem(key)
        self.cur[e] = [key, 0]

    def sbuf(self, name, shape, dtype):
        t = self.stack.enter_context(self.nc.sbuf_tensor(name, list(shape), dtype))
        return Buf(name, t)

    def psum(self, name, shape, dtype=F32):
        t = self.stack.enter_context(self.nc.psum_tensor(name, list(shape), dtype))
        return Buf(name, t, excl=True)

    def _wait(self, e, key, val):
        if self.waited[e].get(key, 0) >= val:
            return
        self.waited[e][key] = val
        sem = self._sem(key)
        self.ops[e].append(lambda eng, sem=sem, val=val: eng.wait_ge(sem, val))
        self.nins += 1

    def _deps(self, e, reads, writes):
        for b in reads:
            if b.lw is not None:
                k, v, we = b.lw
                if not (we == e and e == "pe"):
                    self._wait(e, k, v)
            if b.excl:
                for k, (v, re_) in b.rd.items():
                    if re_ != e:
                        self._wait(e, k, v)
        for b in writes:
            if b.lw is not None:
                k, v, we = b.lw
                if not (we == e and e == "pe"):
                    self._wait(e, k, v)
            for k, (v, re_) in b.rd.items():
                if re_ == e and k[0] == e:
                    continue
                self._wait(e, k, v)

    def op(self, e, fn, reads=(), writes=(), inc=True):
        self._deps(e, reads, writes)
        key, cnt = self.cur[e]
        nxt = cnt + 1
        for b in reads:
            b.rd[key] = (nxt, e)
        for b in writes:
            b.lw = (key, nxt, e)
            b.rd = {}
        self.nins += 1
        if inc:
            sem = self._sem(key)
            self.ops[e].append(lambda eng, fn=fn, sem=sem: fn(eng).then_inc(sem, 1))
            self.cur[e][1] = nxt
            if nxt >= self.SEM_ROLL:
                self._new_sem(e)
        else:
            self.ops[e].append(lambda eng, fn=fn: fn(eng))

    def dma(self, e, fn, reads=(), writes=(), sem_key=None):
        self._deps(e, reads, writes)
        if sem_key is None:
            b0 = (list(writes) + list(reads))[0]
            sem_key = ("dma", b0.name)
        sem = self._sem(sem_key)
        cnt = self.cur.setdefault(sem_key, [sem_key, 0])
        cnt[1] += 16
        val = cnt[1]
        for b in reads:
            b.rd[sem_key] = (val, "dma")
        for b in writes:
            b.lw = (sem_key, val, "dma")
            b.rd = {}
        self.ops[e].append(lambda eng, fn=fn, sem=sem: fn(eng).then_inc(sem, 16))
        self.nins += 1

    def wait_buf(self, e, b):
        if b.lw is not None:
            self._wait(e, b.lw[0], b.lw[1])
        for k, (v, _) in b.rd.items():
            self._wait(e, k, v)

    def barrier(self):
        engs = ("pe", "act", "dve", "pool")
        snap = {e: tuple(self.cur[e]) for e in engs}
        for e in engs:
            for e2 in engs:
                if e2 != e and snap[e2][1] > 0:
                    self._wait(e, snap[e2][0], snap[e2][1])

    def emit(self):
        ops = self.ops
        with self.nc.Block() as block:
            @block.tensor
            def _(eng):
                for f in ops["pe"]:
                    f(eng)

            @block.scalar
            def _(eng):
                for f in ops["act"]:
                    f(eng)

            @block.vector
            def _(eng):
                for f in ops["dve"]:
                    f(eng)

            @block.gpsimd
            def _(eng):
                for f in ops["pool"]:
                    f(eng)

            @block.sync
            def _(eng):
                for f in ops["sp"]:
                    f(eng)


class _alias(Buf):
    __slots__ = ("base",)

    def __init__(self, base, t):
        object.__setattr__(self, "base", base)
        self.name = base.name
        self.t = t
        self.excl = base.excl

    lw = property(lambda self: self.base.lw, lambda self, v: setattr(self.base, "lw", v))
    rd = property(lambda self: self.base.rd, lambda self, v: setattr(self.base, "rd", v))


class Ring:
    def __init__(self, bufs):
        self.bufs = bufs
        self.i = 0

    def next(self):
        b = self.bufs[self.i % len(self.bufs)]
        self.i += 1
        return b


def wset_table():
    t = {}
    for i in range(4):
        t[f"f_up{i}"] = (8, 256, 22, 768)
        t[f"f_down{i}"] = (22, 128, 8)
    t["a_in_zx"] = (8, 256, 24, 1024)
    t["a_in_dt"] = (8, 32, 1)
    t["a_out"] = (16, 128, 8)
    t["b_pw1"] = (8, 256, 8)
    t["b_conv"] = (31, 128, 8)
    t["b_pw2"] = (8, 256, 4)
    t["c_in"] = (8, 256, 10)
    t["c_ga"] = (2, 128, 10)
    t["c_gx"] = (2, 128, 10)
    t["c_out"] = (10, 128, 8)
    t["d_in_u"] = (8, 256, 8)
    t["d_in_v"] = (8, 512, 4)
    t["d_out"] = (16, 128, 8)
    return {k: (v if len(v) == 4 else v + (0,)) for k, v in t.items()}


LAYER_WSETS = {
    0: ["a_in_zx", "a_in_dt", "a_out"],
    1: ["b_pw1", "b_conv", "b_pw2"],
    2: ["c_in", "c_ga", "c_gx", "c_out"],
    3: ["d_in_u", "d_in_v", "d_out"],
}

PCOL_SPEC = [("norm_mix", 32), ("norm_ffn", 32), ("norm_final", 8),
             ("a_conv_w", 128), ("a_conv_b", 32), ("a_dsk", 16), ("a_norm", 16),
             ("b_pw1_b", 16), ("b_dw_b", 8), ("b_ln_g", 8), ("b_ln_b", 8), ("b_pw2_b", 8),
             ("c_in_b", 20), ("c_conv_w", 40), ("c_conv_b", 10), ("c_ga_b", 10), ("c_gx_b", 10),
             ("c_lambda", 10), ("c_out_b", 8),
             ("d_in_b_u", 16), ("d_out_b", 8),
             ("f_conv_w", 4 * 132), ("f_conv_b", 4 * 44)]


def pcol_offsets():
    off, o = {}, 0
    for n, c in PCOL_SPEC:
        off[n] = o
        o += c
    return off, o


class Builder:
    def __init__(self, Lseq, layers, do_ffn=True, do_mix=True):
        assert Lseq % T == 0
        self.do_mix = do_mix
        self.mixl = tuple(layers) if do_mix else ()
        self.Lseq = Lseq
        self.layers = tuple(layers)
        self.do_ffn = do_ffn
        self.ntile = Lseq // T
        self.wtab = wset_table()
        self.poff, self.ncol = pcol_offsets()

    def used_wsets(self):
        names = []
        for l in self.layers:
            if self.do_mix:
                names += LAYER_WSETS[l]
            if self.do_ffn:
                names += [f"f_up{l}", f"f_down{l}"]
        return names

    def build(self):
        nc = bass.Bass("TRN2", target_bir_lowering=False)
        self.nc = nc
        Lseq = self.Lseq
        self.x_d = nc.dram_tensor("x", [Lseq, D], F32, kind="ExternalInput").ap()
        self.y_d = nc.dram_tensor("y", [Lseq, D], F32, kind="ExternalOutput").ap()
        self.pcols_d = nc.dram_tensor("pcols", [128, self.ncol], F32, kind="ExternalInput").ap()
        self.prows_d = nc.dram_tensor("prows", [128, 64], F32, kind="ExternalInput").ap()
        self.cmats_d = nc.dram_tensor("cmats", [128, 512], F32, kind="ExternalInput").ap()
        self.w_d, self.s_d = {}, {}
        for n in self.used_wsets():
            KC, ncb, nblk, extra = self.wtab[n]
            self.w_d[n] = nc.dram_tensor("w_" + n, [nblk, 128, KC * ncb + extra], F32, kind="ExternalInput").ap()
            self.s_d[n] = nc.dram_tensor("s_" + n, [nblk, 128, KC * ncb + extra], BF16, kind="Internal").ap()
        if 3 in self.mixl:
            self.wtsp_d = nc.dram_tensor("wt_sp", [128, 1024], F32, kind="ExternalInput").ap()
            self.dbc_d = nc.dram_tensor("d_bc", [4, 128, 2048], F32, kind="ExternalInput").ap()
        with ExitStack() as st:
            self.S = S = Sched(nc, st)
            self.alloc()
            self.prologue()
            for it in range(self.ntile):
                self.tile(it)
            for b in self.io:
                S.wait_buf("act", b)
            for key, cnt in list(S.cur.items()):
                if key[0] == "dma":
                    S._wait("act", key, cnt[1])
            S.emit()
        return nc

    def alloc(self):
        S = self.S
        self.h_t = S.sbuf("h", [128, 8, T], F32)
        self.h = [Buf(f"h{c}", self.h_t[:, c, :]) for c in range(8)]
        self.u_t = S.sbuf("u", [128, 8, T], BF16)
        self.u = [Buf(f"u{c}", self.u_t[:, c, :]) for c in range(8)]
        self.io = [S.sbuf(f"io{i}", [128, D], F32) for i in range(2)]
        self.slots = Ring([S.sbuf(f"slot{i}", [128, SLOT_EL], BF16) for i in range(NSLOT)])
        self.pcols = S.sbuf("pcols_sb", [128, self.ncol], F32)
        self.prows = S.sbuf("prows_sb", [128, 64], F32)
        self.cm = S.sbuf("cmats_sb", [128, 512], F32)
        self.cmb = S.sbuf("cmats_bf", [128, 512], BF16)
        self.wsbuf = {n: Buf("scr_" + n, None) for n in self.used_wsets()}
        self.dbcbuf = Buf("dbc", None)
        if 0 in self.mixl:
            self.a_tail = S.sbuf("a_tail", [128, 32, 3], BF16)
            self.st_t = S.sbuf("ssd_st", [128, 8, 256], F32)
            self.st = [Buf(f"st{g}", self.st_t[:, g, :]) for g in range(8)]
            self.stb_t = S.sbuf("ssd_stb", [128, 8, 256], BF16)
            self.stb = [Buf(f"stb{g}", self.stb_t[:, g, :]) for g in range(8)]
            self.a_neg = S.sbuf("a_neg", [128, 32], F32)
        if 1 in self.mixl:
            self.hcx_t = S.sbuf("hcx", [128, 8, 32 + T], BF16)
            self.hcx = [Buf(f"hcx{c}", self.hcx_t[:, c, :]) for c in range(8)]
        if 2 in self.mixl:
            self.c_tail = S.sbuf("c_tail", [128, 10, 3], F32)
            self.c_hst = S.sbuf("c_hst", [128, 10], F32)
            self.c_nsp = S.sbuf("c_nsp", [128, 10], F32)
            self.c_nsp2 = S.sbuf("c_nsp2", [128, 10], F32)
        if 3 in self.mixl:
            self.wtsp = S.sbuf("wtsp", [128, 1024], BF16)
        if self.do_ffn:
            self.f_tail = {l: S.sbuf(f"f_tail{l}", [128, 44, 2], BF16) for l in self.layers}
        self.ARENA = min(ARENA_CAP, (self.nc.sbuf_bytes_remaining - 4096) // 64 * 64)
        print('ARENA bytes', self.ARENA)
        self.arena = S.sbuf("arena", [128, self.ARENA // 4], F32)
        self.aoff = 0
        self.live = []
        ps2 = [S.psum(f"ps{i}", [128, 512], F32) for i in range(2)]
        pa2 = [S.psum(f"pa{i}", [128, 512], F32) for i in range(2)]
        py2 = [S.psum(f"py{i}", [128, 512], F32) for i in range(2)]
        self.pring6 = Ring(ps2 + pa2 + py2)
        self.pring2 = Ring(ps2)
        self.pring = self.pring6
        self.pa = Ring(pa2)
        self.py = Ring(py2)
        self.pT = S.psum("pT", [128, 1024], BF16)
        self.pstat = Ring([S.psum("pstat", [128, 512], F32)])

    def phase(self):
        self.aoff = 0

    def av(self, name, nel, dtype):
        nbytes = nel * (4 if dtype == F32 else 2)
        nbytes = (nbytes + 63) // 64 * 64
        assert self.aoff + nbytes <= self.ARENA, (name, self.aoff, nbytes)
        a = self.arena[:, self.aoff // 4:(self.aoff + nbytes) // 4]
        self.last_range = (self.aoff, self.aoff + nbytes)
        self.aoff += nbytes
        if dtype != F32:
            a = a.bitcast(dtype)
        return a[:, 0:nel]

    def abuf(self, name, nel, dtype):
        b = Buf(name, self.av(name, nel, dtype))
        s0, e0 = self.last_range
        keep = []
        for (s1, e1, ob) in self.live:
            if s1 < e0 and s0 < e1:
                ents = list(ob.rd.items())
                if ob.lw is not None:
                    ents.append((ob.lw[0], (ob.lw[1], ob.lw[2])))
                for k, (v, eng) in ents:
                    if k not in b.rd or b.rd[k][0] < v:
                        b.rd[k] = (v, eng)
                if s0 <= s1 and e1 <= e0:
                    continue
            keep.append((s1, e1, ob))
        keep.append((s0, e0, b))
        self.live = keep
        return b

    def abufs(self, name, n, nel, dtype):
        return [self.abuf(f"{name}{i}", nel, dtype) for i in range(n)]

    def pc(self, name, c, n=1):
        o = self.poff[name] + c
        return self.pcols[:, o:o + n]

    def prologue(self):
        S = self.S
        S.dma("act", lambda e: e.dma_start(out=self.pcols[:], in_=self.pcols_d), writes=[self.pcols])
        S.dma("act", lambda e: e.dma_start(out=self.prows[:], in_=self.prows_d), writes=[self.prows])
        S.dma("act", lambda e: e.dma_start(out=self.cm[:], in_=self.cmats_d), writes=[self.cm])
        S.op("dve", lambda e: e.tensor_copy(self.cmb[:], self.cm[:]), reads=[self.cm], writes=[self.cmb])
        self.conv_done = set()
        self.phase_sets = []
        for l in self.layers:
            if self.do_mix:
                self.phase_sets.append(list(LAYER_WSETS[l]))
            if self.do_ffn:
                self.phase_sets.append([f"f_up{l}", f"f_down{l}"])
        self.phase_i = 0
        self.convert_ahead(0)
        self.convert_ahead(1)
        if 0 in self.mixl:
            S.op("pool", lambda e: e.memset(self.a_tail[:], 0.0), writes=[self.a_tail])
            S.op("pool", lambda e: e.memset(self.st_t[:], 0.0), writes=self.st)
            S.op("pool", lambda e: e.memset(self.stb_t[:], 0.0), writes=self.stb)
            S.op("act", lambda e: e.activation(out=self.a_neg[:], in_=self.prows[:, 32:64], func=AF.Exp),
                 reads=[self.prows], writes=[self.a_neg])
            S.op("dve", lambda e: e.tensor_scalar(self.a_neg[:], self.a_neg[:], -1.0, None, ALU.mult),
                 reads=[self.a_neg], writes=[self.a_neg])
        if 1 in self.mixl:
            S.op("pool", lambda e: e.memset(self.hcx_t[:], 0.0), writes=self.hcx)
        if 2 in self.mixl:
            S.op("pool", lambda e: e.memset(self.c_tail[:], 0.0), writes=[self.c_tail])
            S.op("pool", lambda e: e.memset(self.c_hst[:], 0.0), writes=[self.c_hst])
            lam = self.pc("c_lambda", 0, 10)
            S.op("act", lambda e: e.activation(out=self.c_nsp[:], in_=lam, func=AF.Exp, scale=-1.0),
                 reads=[self.pcols], writes=[self.c_nsp])
            S.op("act", lambda e: e.activation(out=self.c_nsp[:], in_=self.c_nsp[:], func=AF.Ln, bias=1.0),
                 reads=[self.c_nsp], writes=[self.c_nsp])
            S.op("dve", lambda e: e.tensor_scalar(self.c_nsp2[:], self.c_nsp[:], -16.0, None, ALU.mult),
                 reads=[self.c_nsp], writes=[self.c_nsp2])
            S.op("dve", lambda e: e.tensor_scalar(self.c_nsp[:], self.c_nsp[:], -8.0, None, ALU.mult),
                 reads=[self.c_nsp, self.c_nsp2], writes=[self.c_nsp])
        if 3 in self.mixl:
            tmp = self.abuf("wtsp_f", 1024, F32)
            S.dma("act", lambda e: e.dma_start(out=tmp[:], in_=self.wtsp_d), writes=[tmp])
            tri = self.cm[:, 128:256]
            S.op("dve", lambda e: e.tensor_tensor(
                self.wtsp[:].rearrange("p (g t) -> p g t", g=8), tmp[:].rearrange("p (g t) -> p g t", g=8),
                tri.unsqueeze(1).to_broadcast([128, 8, 128]), ALU.mult), reads=[tmp, self.cm], writes=[self.wtsp])
        if self.do_ffn:
            for l in self.layers:
                S.op("pool", lambda e, l=l: e.memset(self.f_tail[l][:], 0.0), writes=[self.f_tail[l]])

    def convert_ahead(self, i):
        if i >= len(self.phase_sets):
            return
        S = self.S
        for n in self.phase_sets[i]:
            if n in self.conv_done:
                continue
            self.conv_done.add(n)
            KC, ncb, nblk, extra = self.wtab[n]
            for b in range(nblk):
                S.dma("pool", lambda e, n=n, b=b: e.dma_start(out=self.s_d[n][b], in_=self.w_d[n][b]),
                      writes=[self.wsbuf[n]])

    def wload(self, name, b, f32src=None):
        S = self.S
        slot = self.slots.next()
        if f32src is not None:
            n = f32src.shape[-1]
            dst = slot[:, 0:2 * n].bitcast(F32)
            S.dma("sp", lambda e: e.dma_start(out=dst, in_=f32src), reads=[self.dbcbuf], writes=[slot])
            return slot
        KC, ncb, nblk, extra = self.wtab[name]
        n = KC * ncb + extra
        S.dma("sp", lambda e: e.dma_start(out=slot[:, 0:n], in_=self.s_d[name][b]),
              reads=[self.wsbuf[name]], writes=[slot])
        return slot

    def dense_fm(self, name, rhs_fn, evac, ncols=T, blocks=None):
        S = self.S
        KC, ncb, nblk, extra = self.wtab[name]
        for b in (blocks if blocks is not None else range(nblk)):
            slot = self.wload(name, b)
            for j in range(ncb // 128):
                ps = self.pring.next()
                for kc in range(KC):
                    rap, rbuf = rhs_fn(kc)
                    lo = kc * ncb + j * 128
                    S.op("pe", lambda e, ps=ps, slot=slot, lo=lo, rap=rap, kc=kc: e.matmul(
                        ps[:, 0:ncols], slot[:, lo:lo + 128], rap, start=(kc == 0), stop=(kc == KC - 1)),
                        reads=[slot, rbuf], writes=[ps], inc=(kc == KC - 1))
                evac(b * (ncb // 128) + j, ps)

    def rmsnorm(self, src, gname, goff, dst, dim, tmp=None):
        S = self.S
        n = len(src)
        ones = self.cmb[:, 384:512]
        if tmp is None:
            sq = Ring(self.abufs("nsq", 2, T, BF16))
            rstd = self.abuf("nrstd", T, F32)
        else:
            b0, b1 = tmp[1].bufs
            sq = Ring([Buf(b0.name, b0[:, 0:T // 2].bitcast(BF16)), Buf(b1.name, b1[:, 0:T // 2].bitcast(BF16))])
            sq = Ring([_alias(b0, b0[:, 0:T // 2].bitcast(BF16)), _alias(b1, b1[:, 0:T // 2].bitcast(BF16))])
            rstd = tmp[0].bufs[0]
        pst = self.pstat.next()
        for c in range(n):
            q = sq.next()
            S.op("act", lambda e, q=q, c=c: e.activation(out=q[:], in_=src[c][:], func=AF.Square),
                 reads=[src[c]], writes=[q])
            S.op("pe", lambda e, q=q, c=c: e.matmul(pst[:], ones, q[:], start=(c == 0), stop=(c == n - 1)),
                 reads=[q, self.cmb], writes=[pst])
        S.op("act", lambda e: e.activation(out=rstd[:], in_=pst[:], func=AF.Ln, scale=1.0 / dim, bias=1e-6),
             reads=[pst], writes=[rstd])
        S.op("act", lambda e: e.activation(out=rstd[:], in_=rstd[:], func=AF.Exp, scale=-0.5),
             reads=[rstd], writes=[rstd])
        for c in range(n):
            g = self.pc(gname, goff + c)
            S.op("dve", lambda e, c=c, g=g: e.scalar_tensor_tensor(
                dst[c][:], src[c][:], g, rstd[:], ALU.mult, ALU.mult),
                reads=[src[c], rstd, self.pcols], writes=[dst[c]])

    def conv_taps(self, ps, raw, acc, tail_buf, tail_ap, wcols, bcol, K, in_bias=None):
        S = self.S
        H = K - 1
        S.op("pool", lambda e: e.tensor_copy(raw[:, 0:H], tail_ap), reads=[tail_buf], writes=[raw])
        if in_bias is None:
            S.op("act", lambda e: e.activation(out=raw[:, H:H + T], in_=ps[:], func=AF.Identity),
                 reads=[ps, raw], writes=[raw])
            S.op("act", lambda e: e.activation(out=acc[:], in_=ps[:], func=AF.Identity, scale=wcols[K - 1],
                                               bias=bcol), reads=[ps, self.pcols], writes=[acc])
        else:
            S.op("act", lambda e: e.activation(out=raw[:, H:H + T], in_=ps[:], func=AF.Identity, bias=in_bias),
                 reads=[ps, raw, self.pcols], writes=[raw])
            S.op("act", lambda e: e.activation(out=acc[:], in_=raw[:, H:H + T], func=AF.Identity,
                                               scale=wcols[K - 1], bias=bcol), reads=[raw, self.pcols], writes=[acc])
        for k in range(K - 1):
            S.op("dve", lambda e, k=k: e.scalar_tensor_tensor(
                acc[:], raw[:, k:k + T], wcols[k], acc[:], ALU.mult, ALU.add),
                reads=[raw, acc, self.pcols], writes=[acc])
        S.op("pool", lambda e: e.tensor_copy(tail_ap, raw[:, T:T + H]), reads=[raw], writes=[tail_buf])

    def tile(self, it):
        S = self.S
        ident = self.cm[:, 0:128]
        self.phase()
        for q in range(NQ):
            r0 = it * T + q * 128
            io = self.io[q % 2]
            S.dma("act", lambda e, io=io, r0=r0: e.dma_start(out=io[:], in_=self.x_d[r0:r0 + 128, :]),
                  writes=[io])
            for half in range(2):
                ps = self.pring.next()
                for j in range(4):
                    c = half * 4 + j
                    S.op("pe", lambda e, ps=ps, j=j, c=c, io=io: e.transpose(
                        ps[:, j * 128:(j + 1) * 128], io[:, c * 128:(c + 1) * 128], ident),
                        reads=[io, self.cm], writes=[ps], inc=(j == 3))
                for j in range(4):
                    c = half * 4 + j
                    S.op("dve", lambda e, ps=ps, j=j, c=c, q=q:
                         e.tensor_copy(self.h[c][:, q * 128:(q + 1) * 128], ps[:, j * 128:(j + 1) * 128]),
                         reads=[ps], writes=[self.h[c]])
        for l in self.layers:
            if self.do_mix:
                if it == 0:
                    self.phase_i += 1
                    self.convert_ahead(self.phase_i + 1)
                getattr(self, f"mixer{l}")(it)
            if self.do_ffn:
                if it == 0:
                    self.phase_i += 1
                    self.convert_ahead(self.phase_i + 1)
                self.ffn(l)
        self.phase()
        o = self.abufs("fin", 8, T, F32)
        self.rmsnorm_f32(self.h, "norm_final", 0, o)
        for q in range(NQ):
            for half in range(2):
                ps = self.pring.next()
                for j in range(4):
                    c = half * 4 + j
                    S.op("pe", lambda e, ps=ps, j=j, c=c, q=q: e.transpose(
                        ps[:, j * 128:(j + 1) * 128], o[c][:, q * 128:(q + 1) * 128], ident),
                        reads=[o[c], self.cm], writes=[ps], inc=(j == 3))
                S.op("dve", lambda e, ps=ps, q=q, half=half: e.tensor_copy(
                    self.io[q % 2][:, half * 512:(half + 1) * 512], ps[:]), reads=[ps], writes=[self.io[q % 2]])
            r0 = it * T + q * 128
            S.dma("act", lambda e, q=q, r0=r0: e.dma_start(out=self.y_d[r0:r0 + 128, :], in_=self.io[q % 2][:]),
                  reads=[self.io[q % 2]])

    def rmsnorm_f32(self, src, gname, goff, dst):
        self.rmsnorm(src, gname, goff, dst, D)

    def ffn(self, l):
        S = self.S
        self.phase()
        self.rmsnorm(self.h, "norm_ffn", l * 8, self.u, D)
        hid = self.abufs("hid", 22, T, BF16)
        raws = Ring(self.abufs("fraw", 6, T + 2, BF16))
        sgs = Ring(self.abufs("fsg", 3, T, F32))
        tail = self.f_tail[l]
        bo = self.poff["f_conv_b"] + l * 44
        name = f"f_up{l}"
        pend = None

        def conv_stage(b, slot, rg, rv):
            psg, psv = self.pring.next(), self.pring.next()
            for half, raw, ps2 in ((0, rg, psg), (1, rv, psv)):
                for k in range(3):
                    lo = 2048 + half * 384 + k * 128
                    S.op("pe", lambda e, ps2=ps2, slot=slot, lo=lo, raw=raw, k=k: e.matmul(
                        ps2[:, 0:T], slot[:, lo:lo + 128], raw[:, k:k + T], start=(k == 0), stop=(k == 2)),
                        reads=[slot, raw], writes=[ps2], inc=(k == 2))
            sg = sgs.next()
            bg = self.pcols[:, bo + b:bo + b + 1]
            bv = self.pcols[:, bo + 22 + b:bo + 22 + b + 1]
            S.op("act", lambda e: e.activation(out=sg[:], in_=psg[:, 0:T], func=AF.Silu, bias=bg),
                 reads=[psg, self.pcols], writes=[sg])
            S.op("dve", lambda e: e.scalar_tensor_tensor(hid[b][:], psv[:, 0:T], bv, sg[:], ALU.add, ALU.mult),
                 reads=[psv, sg, self.pcols], writes=[hid[b]])

        for b in range(22):
            slot = self.wload(name, b)
            rr = []
            for half in range(2):
                ps = self.pring.next()
                for kc in range(8):
                    lo = kc * 256 + half * 128
                    S.op("pe", lambda e, ps=ps, slot=slot, lo=lo, kc=kc: e.matmul(
                        ps[:, 0:T], slot[:, lo:lo + 128], self.u[kc][:], start=(kc == 0), stop=(kc == 7)),
                        reads=[slot, self.u[kc]], writes=[ps], inc=(kc == 7))
                idx = half * 22 + b
                raw = raws.next()
                S.op("pool", lambda e, raw=raw, idx=idx: e.tensor_copy(raw[:, 0:2], tail[:, idx, :]),
                     reads=[tail], writes=[raw])
                S.op("act", lambda e, raw=raw, ps=ps: e.activation(out=raw[:, 2:2 + T], in_=ps[:, 0:T],
                                                                   func=AF.Identity),
                     reads=[ps, raw], writes=[raw])
                S.op("pool", lambda e, raw=raw, idx=idx: e.tensor_copy(tail[:, idx, :], raw[:, T:T + 2]),
                     reads=[raw], writes=[tail])
                rr.append(raw)
            if pend is not None:
                conv_stage(*pend)
            pend = (b, slot, rr[0], rr[1])
        conv_stage(*pend)

        def evac2(oc, ps):
            S.op("dve", lambda e: e.tensor_tensor(self.h[oc][:], self.h[oc][:], ps[:], ALU.add),
                 reads=[self.h[oc], ps], writes=[self.h[oc]])

        self.dense_fm(f"f_down{l}", lambda kc: (hid[kc][:], hid[kc]), evac2)

    def resid_evac(self, bname):
        S = self.S

        def evac(oc, ps):
            b = self.pc(bname, oc)
            S.op("dve", lambda e: e.scalar_tensor_tensor(self.h[oc][:], ps[:], b, self.h[oc][:], ALU.add, ALU.add),
                 reads=[ps, self.h[oc], self.pcols], writes=[self.h[oc]])
        return evac

    def rstd_from_var(self, var, eps):
        S = self.S
        S.op("act", lambda e: e.activation(out=var[:], in_=var[:], func=AF.Ln, bias=eps), reads=[var], writes=[var])
        S.op("act", lambda e: e.activation(out=var[:], in_=var[:], func=AF.Exp, scale=-0.5), reads=[var], writes=[var])

    def mixer1(self, it):
        S = self.S
        self.phase()
        self.rmsnorm(self.h, "norm_mix", 8, self.u, D)
        ta = Ring(self.abufs("m1a", 2, T, F32))
        tb = Ring(self.abufs("m1b", 2, T, F32))
        cv = self.abufs("cv", 8, T, F32)
        sl = self.abufs("sl", 8, T, BF16)
        hcx = self.hcx
        pend = {}

        def evac(oc, ps):
            c, half = oc // 2, oc % 2
            bcol = self.pc("b_pw1_b", oc)
            if half == 0:
                a = ta.next()
                S.op("act", lambda e: e.activation(out=a[:], in_=ps[:], func=AF.Identity, bias=bcol),
                     reads=[ps, self.pcols], writes=[a])
                pend[c] = a
            else:
                a = pend.pop(c)
                sb = tb.next()
                S.op("act", lambda e: e.activation(out=sb[:], in_=ps[:], func=AF.Sigmoid, bias=bcol),
                     reads=[ps, self.pcols], writes=[sb])
                S.op("dve", lambda e: e.tensor_tensor(hcx[c][:, 30:30 + T], a[:], sb[:], ALU.mult),
                     reads=[a, sb], writes=[hcx[c]])

        self.dense_fm("b_pw1", lambda kc: (self.u[kc][:], self.u[kc]), evac)
        onesb = self.cmb[:, 384:512]
        cvb = Ring(self.abufs("cvb", 2, T, BF16))
        sq = Ring(self.abufs("csq", 2, T, BF16))
        pA = self.pstat.next()
        pB = _alias(self.pT, self.pT[:].bitcast(F32))

        def stats(c):
            b1, b2 = cvb.next(), sq.next()
            S.op("pool", lambda e: e.tensor_copy(b1[:], cv[c][:]), reads=[cv[c]], writes=[b1])
            S.op("act", lambda e: e.activation(out=b2[:], in_=cv[c][:], func=AF.Square), reads=[cv[c]], writes=[b2])
            S.op("pe", lambda e: e.matmul(pA[:], onesb, b1[:], start=(c == 0), stop=(c == 7)),
                 reads=[b1, self.cmb], writes=[pA])
            S.op("pe", lambda e: e.matmul(pB[:], onesb, b2[:], start=(c == 0), stop=(c == 7)),
                 reads=[b2, self.cmb], writes=[pB])

        for c in range(8):
            slot = self.wload("b_conv", c)
            ps = self.pring.next()
            for k in range(31):
                S.op("pe", lambda e, ps=ps, slot=slot, k=k, c=c: e.matmul(
                    ps[:, 0:T], slot[:, k * 128:(k + 1) * 128], hcx[c][:, k:k + T], start=(k == 0), stop=(k == 30)),
                    reads=[slot, hcx[c]], writes=[ps], inc=(k == 30))
            if c >= 1:
                stats(c - 1)
            bcol = self.pc("b_dw_b", c)
            S.op("act", lambda e, ps=ps, c=c, bcol=bcol: e.activation(out=cv[c][:], in_=ps[:], func=AF.Identity,
                                                                     bias=bcol),
                 reads=[ps, self.pcols], writes=[cv[c]])
            S.op("pool", lambda e, c=c: e.tensor_copy(hcx[c][:, 0:30], hcx[c][:, T:T + 30]),
                 reads=[hcx[c]], writes=[hcx[c]])
        stats(7)
        mean = self.abuf("lnmean", T, F32)
        var = self.abuf("lnvar", T, F32)
        S.op("act", lambda e: e.activation(out=mean[:], in_=pA[:], func=AF.Identity, scale=1.0 / D),
             reads=[pA], writes=[mean])
        S.op("dve", lambda e: e.tensor_tensor(var[:], mean[:], mean[:], ALU.mult), reads=[mean], writes=[var])
        S.op("dve", lambda e: e.scalar_tensor_tensor(var[:], pB[:], 1.0 / D, var[:], ALU.mult, ALU.subtract),
             reads=[pB, var], writes=[var])
        self.rstd_from_var(var, 1e-5)
        for c in range(8):
            S.op("dve", lambda e, c=c: e.tensor_tensor(cv[c][:], cv[c][:], mean[:], ALU.subtract),
                 reads=[cv[c], mean], writes=[cv[c]])
            S.op("pool", lambda e, c=c: e.tensor_tensor(cv[c][:], cv[c][:], var[:], ALU.mult),
                 reads=[cv[c], var], writes=[cv[c]])
            g, b = self.pc("b_ln_g", c), self.pc("b_ln_b", c)
            S.op("act", lambda e, c=c, g=g, b=b: e.activation(out=sl[c][:], in_=cv[c][:], func=AF.Silu,
                                                             scale=g, bias=b),
                 reads=[cv[c], self.pcols], writes=[sl[c]])
        self.dense_fm("b_pw2", lambda kc: (sl[kc][:], sl[kc]), self.resid_evac("b_pw2_b"))

    def mixer2(self, it):
        S = self.S
        self.phase()
        self.rmsnorm(self.h, "norm_mix", 16, self.u, D)
        gg = self.abufs("gg", 10, T, BF16)
        xc = self.abufs("xc", 10, T, F32)
        xcb = self.abufs("xcb", 10, T, BF16)
        rr = self.abufs("rr", 10, T, F32)
        ii = self.abufs("ii", 10, T, F32)
        tmp = Ring(self.abufs("ltmp", 5, T, F32))
        raws = Ring(self.abufs("craw", 2, T + 3, F32))
        wo = self.poff["c_conv_w"]

        def evac(oc, ps):
            bcol = self.pc("c_in_b", oc)
            if oc < 10:
                S.op("act", lambda e: e.activation(out=gg[oc][:], in_=ps[:], func=AF.Gelu_apprx_tanh, bias=bcol),
                     reads=[ps, self.pcols], writes=[gg[oc]])
            else:
                j = oc - 10
                wc = [self.pcols[:, wo + k * 10 + j: wo + k * 10 + j + 1] for k in range(4)]
                self.conv_taps(ps, raws.next(), xc[j], self.c_tail, self.c_tail[:, j, :], wc,
                               self.pc("c_conv_b", j), 4, in_bias=bcol)
                S.op("pool", lambda e: e.tensor_copy(xcb[j][:], xc[j][:]), reads=[xc[j]], writes=[xcb[j]])

        self.dense_fm("c_in", lambda kc: (self.u[kc][:], self.u[kc]), evac)
        for nm, bn, dst in (("c_ga", "c_ga_b", rr), ("c_gx", "c_gx_b", ii)):
            for oc in range(10):
                slot = self.wload(nm, oc)
                ps = self.pring.next()
                for kc in range(2):
                    src = xcb[(oc // 2) * 2 + kc]
                    S.op("pe", lambda e, ps=ps, slot=slot, kc=kc, src=src: e.matmul(
                        ps[:, 0:T], slot[:, kc * 128:(kc + 1) * 128], src[:], start=(kc == 0), stop=(kc == 1)),
                        reads=[slot, src], writes=[ps], inc=(kc == 1))
                bcol = self.pc(bn, oc)
                S.op("act", lambda e, ps=ps, oc=oc, bcol=bcol, dst=dst: e.activation(
                    out=dst[oc][:], in_=ps[:], func=AF.Sigmoid, bias=bcol),
                    reads=[ps, self.pcols], writes=[dst[oc]])
        for oc0 in (0, 5):
          t2s = {}
          for oc in range(oc0, oc0 + 5):
              t1 = tmp.next()
              S.op("act", lambda e, oc=oc, t1=t1: e.activation(out=t1[:], in_=rr[oc][:], func=AF.Exp,
                                                               scale=self.c_nsp2[:, oc:oc + 1]),
                   reads=[rr[oc], self.c_nsp2], writes=[t1])
              S.op("act", lambda e, oc=oc: e.activation(out=rr[oc][:], in_=rr[oc][:], func=AF.Exp,
                                                        scale=self.c_nsp[:, oc:oc + 1]),
                   reads=[rr[oc], self.c_nsp], writes=[rr[oc]])
              t2s[oc] = t1
          for oc in range(oc0, oc0 + 5):
              t1 = t2s[oc]
              S.op("act", lambda e, t1=t1: e.activation(out=t1[:], in_=t1[:], func=AF.Sqrt, scale=-1.0, bias=1.0),
                   reads=[t1], writes=[t1])
              S.op("dve", lambda e, oc=oc: e.tensor_tensor(ii[oc][:], ii[oc][:], xc[oc][:], ALU.mult),
                   reads=[ii[oc], xc[oc]], writes=[ii[oc]])
              S.op("dve", lambda e, oc=oc, t1=t1: e.tensor_tensor(ii[oc][:], ii[oc][:], t1[:], ALU.mult),
                   reads=[ii[oc], t1], writes=[ii[oc]])
              S.op("dve", lambda e, oc=oc: e.tensor_tensor_scan(
                  xc[oc][:], rr[oc][:], ii[oc][:], self.c_hst[:, oc:oc + 1], ALU.mult, ALU.add),
                  reads=[rr[oc], ii[oc], self.c_hst], writes=[xc[oc]])
              S.op("pool", lambda e, oc=oc: e.tensor_copy(self.c_hst[:, oc:oc + 1], xc[oc][:, T - 1:T]),
                   reads=[xc[oc]], writes=[self.c_hst])
              S.op("pool", lambda e, oc=oc: e.tensor_tensor(xcb[oc][:], gg[oc][:], xc[oc][:], ALU.mult),
                   reads=[gg[oc], xc[oc]], writes=[xcb[oc]])
        self.dense_fm("c_out", lambda kc: (xcb[kc][:], xcb[kc]), self.resid_evac("c_out_b"))

    def mixer3(self, it):
        S = self.S
        self.phase()
        self.rmsnorm(self.h, "norm_mix", 24, self.u, D)
        ug = self.abufs("ug", 16, T, BF16)
        vt = self.abufs("vt", NQ, 2048, F32)
        vtb = self.abufs("vtb", NQ, 2048, BF16)
        gated = self.abufs("gated", 16, T, BF16)

        def evac(oc, ps):
            bcol = self.pc("d_in_b_u", oc)
            S.op("act", lambda e: e.activation(out=ug[oc][:], in_=ps[:], func=AF.Gelu_apprx_tanh, bias=bcol),
                 reads=[ps, self.pcols], writes=[ug[oc]])

        self.dense_fm("d_in_u", lambda kc: (self.u[kc][:], self.u[kc]), evac)
        for blk in range(4):
            bsl = self.wload(None, 0, f32src=self.dbc_d[0][:, blk * 512:(blk + 1) * 512])
            bf = bsl[:, 0:1024].bitcast(F32)
            slot = self.wload("d_in_v", blk)
            for q in range(NQ):
                ps = self.pring.next()
                for kc in range(8):
                    S.op("pe", lambda e, ps=ps, slot=slot, kc=kc, q=q: e.matmul(
                        ps[:, 0:512], self.u[kc][:, q * 128:(q + 1) * 128], slot[:, kc * 512:(kc + 1) * 512],
                        start=(kc == 0), stop=(kc == 7)), reads=[slot, self.u[kc]], writes=[ps], inc=(kc == 7))
                S.op("dve", lambda e, ps=ps, q=q, blk=blk, bf=bf: e.tensor_tensor(
                    vt[q][:, blk * 512:(blk + 1) * 512], ps[:, 0:512], bf, ALU.add),
                    reads=[ps, bsl], writes=[vt[q]])
        gsl = self.wload(None, 0, f32src=self.dbc_d[1])
        gf = gsl[:, 0:4096].bitcast(F32)
        b2sl = self.wload(None, 0, f32src=self.dbc_d[2])
        b2f = b2sl[:, 0:4096].bitcast(F32)
        st = self.abuf("bnst", NQ * 24, F32)
        mv = self.abuf("bnmv", NQ * 2, F32)
        for q in range(NQ):
            S.op("act", lambda e, q=q: e.activation(out=vt[q][:], in_=vt[q][:], func=AF.Gelu_apprx_tanh),
                 reads=[vt[q]], writes=[vt[q]])
            for j in range(4):
                S.op("dve", lambda e, q=q, j=j: e.bn_stats(st[:, q * 24 + j * 6: q * 24 + (j + 1) * 6],
                                                           vt[q][:, j * 512:(j + 1) * 512]),
                     reads=[vt[q]], writes=[st])
            S.op("dve", lambda e, q=q: e.bn_aggr(mv[:, q * 2:q * 2 + 2], st[:, q * 24:(q + 1) * 24]),
                 reads=[st], writes=[mv])
        S.op("act", lambda e: e.activation(out=st[:, 0:NQ], in_=mv[:].rearrange("p (q two) -> p q two", two=2)[:, :, 1],
                                           func=AF.Ln, bias=1e-5), reads=[mv, st], writes=[st])
        S.op("act", lambda e: e.activation(out=st[:, 0:NQ], in_=st[:, 0:NQ], func=AF.Exp, scale=-0.5),
             reads=[st], writes=[st])
        for q in range(NQ):
            S.op("dve", lambda e, q=q: e.tensor_scalar(vt[q][:], vt[q][:], mv[:, 2 * q:2 * q + 1], st[:, q:q + 1],
                                                       ALU.subtract, ALU.mult),
                 reads=[vt[q], mv, st], writes=[vt[q]])
            S.op("pool", lambda e, q=q: e.tensor_tensor(vt[q][:], vt[q][:], gf, ALU.mult),
                 reads=[vt[q], gsl], writes=[vt[q]])
            S.op("dve", lambda e, q=q: e.tensor_tensor(vtb[q][:], vt[q][:], b2f, ALU.add),
                 reads=[vt[q], b2sl], writes=[vtb[q]])
        spsl = self.wload(None, 0, f32src=self.dbc_d[3][:, 0:1024])
        spf = spsl[:, 0:2048].bitcast(F32)
        ones1 = self.cm[0:1, 384:512]
        for j in range(16):
            g = j // 2
            ps = self.pring.next()
            for q in range(NQ):
                S.op("pe", lambda e, ps=ps, q=q, j=j, g=g: e.matmul(
                    ps[:, q * 128:(q + 1) * 128], vtb[q][:, j * 128:(j + 1) * 128],
                    self.wtsp[:, g * 128:(g + 1) * 128], start=True, stop=False),
                    reads=[vtb[q], self.wtsp], writes=[ps], inc=False)
                S.op("pe", lambda e, ps=ps, q=q, g=g: e.matmul(
                    ps[:, q * 128:(q + 1) * 128], ones1, spf[0:1, g * 128:(g + 1) * 128], start=False, stop=True),
                    reads=[spsl, self.cm], writes=[ps], inc=(q == NQ - 1))
            S.op("dve", lambda e, ps=ps, j=j: e.tensor_tensor(gated[j][:], ps[:], ug[j][:], ALU.mult),
                 reads=[ps, ug[j]], writes=[gated[j]])
        self.dense_fm("d_out", lambda kc: (gated[kc][:], gated[kc]), self.resid_evac("d_out_b"))

    def mixer0(self, it):
        S = self.S
        self.phase()
        y = self.abufs("ssd_y", 16, T, F32)
        self.rmsnorm(self.h, "norm_mix", 0, self.u, D)
        xbc = self.abufs("xbc", 32, T, BF16)
        raws = Ring(self.abufs("araw32", 2, T + 3, F32))
        accs = Ring(self.abufs("aacc", 2, T, F32))
        identb = self.cmb[:, 0:128]
        mask01 = self.cmb[:, 128:256]
        tri = self.cm[:, 128:256]
        Umat = self.cm[:, 256:384]
        onesf = self.cm[:, 384:512]

        araws = Ring(self.abufs("araw", 6, T + 4, BF16))
        pend = None

        def conv_stage(slot, items):
            for (idx, j, raw) in items:
                ps2 = self.pring.next()
                for k in range(4):
                    lo = 2048 + j * 512 + k * 128
                    S.op("pe", lambda e, ps2=ps2, slot=slot, lo=lo, raw=raw, k=k: e.matmul(
                        ps2[:, 0:T], slot[:, lo:lo + 128], raw[:, k:k + T], start=(k == 0), stop=(k == 3)),
                        reads=[slot, raw], writes=[ps2], inc=(k == 3))
                bcol = self.pc("a_conv_b", idx)
                S.op("act", lambda e, ps2=ps2, idx=idx, bcol=bcol: e.activation(
                    out=xbc[idx][:], in_=ps2[:, 0:T], func=AF.Silu, bias=bcol),
                    reads=[ps2, self.pcols], writes=[xbc[idx]])

        for b in range(8, 24):
            slot = self.wload("a_in_zx", b)
            items = []
            for j in range(2):
                ps = self.pring.next()
                for kc in range(8):
                    lo = kc * 256 + j * 128
                    S.op("pe", lambda e, ps=ps, slot=slot, lo=lo, kc=kc: e.matmul(
                        ps[:, 0:T], slot[:, lo:lo + 128], self.u[kc][:], start=(kc == 0), stop=(kc == 7)),
                        reads=[slot, self.u[kc]], writes=[ps], inc=(kc == 7))
                idx = (b - 8) * 2 + j
                raw = araws.next()
                S.op("pool", lambda e, raw=raw, idx=idx: e.tensor_copy(raw[:, 0:3], self.a_tail[:, idx, :]),
                     reads=[self.a_tail], writes=[raw])
                S.op("act", lambda e, raw=raw, ps=ps: e.activation(out=raw[:, 3:3 + T], in_=ps[:, 0:T],
                                                                   func=AF.Identity),
                     reads=[ps, raw], writes=[raw])
                S.op("pool", lambda e, raw=raw, idx=idx: e.tensor_copy(self.a_tail[:, idx, :], raw[:, T:T + 3]),
                     reads=[raw], writes=[self.a_tail])
                items.append((idx, j, raw))
            if pend is not None:
                conv_stage(*pend)
            pend = (slot, items)
        conv_stage(*pend)
        slot = self.wload("a_in_dt", 0)
        dt = self.abufs("dt", NQ, 32, F32)
        dta = self.abufs("dta", NQ, 32, F32)
        dhi = self.abufs("dhi", NQ, 32, BF16)
        dlo = self.abufs("dlo", NQ, 32, BF16)
        for q in range(NQ):
            pd = self.pa.next()
            for kc in range(8):
                S.op("pe", lambda e, pd=pd, kc=kc, q=q: e.matmul(
                    pd[:, 0:32], self.u[kc][:, q * 128:(q + 1) * 128], slot[:, kc * 32:(kc + 1) * 32],
                    start=(kc == 0), stop=(kc == 7)), reads=[slot, self.u[kc]], writes=[pd], inc=(kc == 7))
            S.op("dve", lambda e, pd=pd, q=q: e.tensor_tensor(dt[q][:], pd[:, 0:32], self.prows[:, 0:32], ALU.add),
                 reads=[pd, self.prows], writes=[dt[q]])
            S.op("act", lambda e, q=q: e.activation(out=dt[q][:], in_=dt[q][:], func=AF.Exp),
                 reads=[dt[q]], writes=[dt[q]])
            S.op("act", lambda e, q=q: e.activation(out=dt[q][:], in_=dt[q][:], func=AF.Ln, bias=1.0),
                 reads=[dt[q]], writes=[dt[q]])
            S.op("dve", lambda e, q=q: e.tensor_tensor(dta[q][:], dt[q][:], self.a_neg[:], ALU.mult),
                 reads=[dt[q], self.a_neg], writes=[dta[q]])
            S.op("dve", lambda e, q=q: e.tensor_copy(dhi[q][:], dta[q][:]), reads=[dta[q]], writes=[dhi[q]])
            S.op("dve", lambda e, q=q: e.tensor_tensor(dlo[q][:], dta[q][:], dhi[q][:], ALU.subtract),
                 reads=[dta[q], dhi[q]], writes=[dlo[q]])
        STOP = int(os.environ.get("K_STOP", "99"))
        if STOP <= 2:
            return
        acs_b = self.abufs("acs", 2, 32, F32)
        dex_b = self.abufs("dex", 2, 64, F32)
        xdt = self.abuf("xdt", 2048, BF16)
        xde = self.abuf("xde", 2048, BF16)
        btok = self.abuf("btok", 1024, BF16)
        cbms = Ring(self.abufs("cbm", 3, 128, BF16))
        sgs = Ring(self.abufs("sg", 2, 512, F32))
        Es = Ring(self.abufs("Eh", 2, 512, BF16))
        eAs = Ring(self.abufs("eA", 2, 512, BF16))
        scs = Ring(self.abufs("sc", 2, 512, BF16))
        Css = Ring(self.abufs("Cs", 2, 512, BF16))
        for q in range(NQ):
            cs = slice(q * 128, (q + 1) * 128)
            acs, dex = acs_b[q % 2], dex_b[q % 2]
            p3 = self.pa.next()
            S.op("pe", lambda e, p3=p3, q=q: e.matmul(p3[:, 0:32], tri, dta[q][:], start=True, stop=True),
                 reads=[self.cm, dta[q]], writes=[p3], inc=False)
            S.op("pe", lambda e, p3=p3, q=q: e.matmul(p3[:, 32:64], Umat, dta[q][:], start=True, stop=True),
                 reads=[self.cm, dta[q]], writes=[p3], inc=False)
            S.op("pe", lambda e, p3=p3, q=q: e.matmul(p3[:, 64:96], onesf, dta[q][:], start=True, stop=True),
                 reads=[self.cm, dta[q]], writes=[p3])
            S.op("act", lambda e, p3=p3, acs=acs: e.activation(out=acs[:], in_=p3[:, 0:32], func=AF.Identity),
                 reads=[p3], writes=[acs])
            S.op("act", lambda e, p3=p3, dex=dex: e.activation(out=dex[:], in_=p3[:, 32:96], func=AF.Exp),
                 reads=[p3], writes=[dex])
            ps0, ps1 = self.pring2.bufs
            ptx = [self.pT, _alias(ps0, ps0[:].bitcast(BF16))]
            ptb = _alias(ps1, ps1[:].bitcast(BF16))
            for bt in range(2):
                pt = ptx[bt]
                for j in range(8):
                    xcn = bt * 8 + j
                    S.op("pe", lambda e, pt=pt, j=j, xcn=xcn, cs=cs: e.transpose(
                        pt[:, j * 128:(j + 1) * 128], xbc[xcn][:, cs], identb),
                        reads=[xbc[xcn], self.cmb], writes=[pt], inc=(j == 7))
            for j in range(8):
                bcn = 16 + j
                S.op("pe", lambda e, j=j, bcn=bcn, cs=cs: e.transpose(
                    ptb[:, j * 128:(j + 1) * 128], xbc[bcn][:, cs], identb),
                    reads=[xbc[bcn], self.cmb], writes=[ptb], inc=(j == 7))
            for bt in range(2):
                pt = ptx[bt]
                S.op("dve", lambda e, pt=pt, bt=bt, q=q: e.tensor_tensor(
                    xdt[:, bt * 1024:(bt + 1) * 1024].rearrange("p (h d) -> p h d", h=16),
                    pt[:].rearrange("p (h d) -> p h d", h=16),
                    dt[q][:, bt * 16:(bt + 1) * 16].unsqueeze(2).to_broadcast([128, 16, 64]), ALU.mult),
                    reads=[pt, dt[q]], writes=[xdt])
            S.op("act", lambda e: e.activation(out=btok[:], in_=ptb[:], func=AF.Identity),
                 reads=[ptb], writes=[btok])
            if STOP <= 3:
                continue
            S.op("pool", lambda e, dex=dex: e.tensor_tensor(
                xde[:].rearrange("p (h d) -> p h d", h=32), xdt[:].rearrange("p (h d) -> p h d", h=32),
                dex[:, 0:32].unsqueeze(2).to_broadcast([128, 32, 64]), ALU.mult),
                reads=[xdt, dex], writes=[xde])
            if STOP <= 4:
                continue
            def stage_a(g, q=q, cs=cs):
                BT, CT = xbc[16 + g], xbc[24 + g]
                pyb = self.py.next()
                pab = self.pa.next()
                S.op("pe", lambda e, pyb=pyb, BT=BT, CT=CT, cs=cs: e.matmul(
                    pyb[:, 384:512], BT[:, cs], CT[:, cs], start=True, stop=True), reads=[BT, CT], writes=[pyb])
                cbm = cbms.next()
                S.op("dve", lambda e, pyb=pyb, cbm=cbm: e.tensor_tensor(cbm[:], pyb[:, 384:512], mask01, ALU.mult),
                     reads=[pyb, self.cmb], writes=[cbm])
                prb = None
                for bank in (pab,):
                    for j in range(4):
                        hh = 4 * g + j
                        S.op("pe", lambda e, bank=bank, hh=hh, q=q, j=j: e.matmul(
                            bank[:, j * 128:(j + 1) * 128], dhi[q][:, hh:hh + 1].to_broadcast([128, 128]), mask01,
                            start=True, stop=False), reads=[dhi[q], self.cmb], writes=[bank], inc=False)
                        S.op("pe", lambda e, bank=bank, hh=hh, q=q, j=j: e.matmul(
                            bank[:, j * 128:(j + 1) * 128], dlo[q][:, hh:hh + 1].to_broadcast([128, 128]), mask01,
                            start=False, stop=True), reads=[dlo[q], self.cmb], writes=[bank], inc=(j == 3))
                return (g, BT, CT, pyb, pab, prb, cbm)

            def stage_b1(ctx, q=q, cs=cs, acs=acs, dex=dex):
                g, BT, CT, pyb, pab, prb, cbm = ctx
                eA4, r4, E4, sc4, Cs4 = eAs.next(), sgs.next(), Es.next(), scs.next(), Css.next()
                S.op("act", lambda e, pab=pab, eA4=eA4: e.activation(out=eA4[:], in_=pab[:], func=AF.Exp),
                     reads=[pab], writes=[eA4])
                for j in range(4):
                    hh = 4 * g + j
                    S.op("act", lambda e, pab=pab, r4=r4, hh=hh, j=j, acs=acs: e.activation(
                        out=r4[:, j * 128:(j + 1) * 128], in_=pab[:, j * 128:(j + 1) * 128], func=AF.Relu,
                        scale=-1.0, bias=acs[:, hh:hh + 1]), reads=[pab, acs], writes=[r4])
                S.op("act", lambda e, r4=r4, E4=E4: e.activation(out=E4[:], in_=r4[:], func=AF.Exp, scale=-1.0),
                     reads=[r4], writes=[E4])
                S.op("dve", lambda e, sc4=sc4, E4=E4, cbm=cbm: e.tensor_tensor(
                    sc4[:].rearrange("p (h t) -> p h t", h=4), E4[:].rearrange("p (h t) -> p h t", h=4),
                    cbm[:].unsqueeze(1).to_broadcast([128, 4, 128]), ALU.mult), reads=[E4, cbm], writes=[sc4])
                S.op("pool", lambda e, Cs4=Cs4, eA4=eA4, CT=CT, cs=cs: e.tensor_tensor(
                    Cs4[:].rearrange("p (h t) -> p h t", h=4), eA4[:].rearrange("p (h t) -> p h t", h=4),
                    CT[:, cs].unsqueeze(1).to_broadcast([128, 4, 128]), ALU.mult), reads=[eA4, CT], writes=[Cs4])
                return (sc4, Cs4)

            def stage_b2(ctx, pre, q=q, cs=cs, acs=acs, dex=dex):
                g, BT, CT, pyb, pab, prb, cbm = ctx
                sc4, Cs4 = pre
                for j in range(4):
                    hh = 4 * g + j
                    xcn, half = hh // 2, hh % 2
                    lo = half * 64
                    pyr = (j // 2) * 128
                    S.op("pe", lambda e, pyb=pyb, pyr=pyr, lo=lo, hh=hh, sc4=sc4, j=j: e.matmul(
                        pyb[lo:lo + 64, pyr:pyr + 128], xdt[:, hh * 64:(hh + 1) * 64], sc4[:, j * 128:(j + 1) * 128],
                        start=True, stop=False, tile_position=(0, lo)), reads=[xdt, sc4], writes=[pyb], inc=False)
                    S.op("pe", lambda e, pyb=pyb, pyr=pyr, lo=lo, g=g, j=j, Cs4=Cs4: e.matmul(
                        pyb[lo:lo + 64, pyr:pyr + 128], self.stb[g][:, j * 64:(j + 1) * 64],
                        Cs4[:, j * 128:(j + 1) * 128], start=False, stop=True, tile_position=(0, lo)),
                        reads=[self.stb[g], Cs4], writes=[pyb])
                    if half == 1:
                        dsk = self.pc("a_dsk", xcn)
                        S.op("dve", lambda e, pyb=pyb, pyr=pyr, xcn=xcn, dsk=dsk, cs=cs: e.scalar_tensor_tensor(
                            y[xcn][:, cs], xbc[xcn][:, cs], dsk, pyb[:, pyr:pyr + 128], ALU.mult, ALU.add),
                            reads=[xbc[xcn], pyb, self.pcols], writes=[y[xcn]])
                pS = self.pstat.next()
                S.op("pe", lambda e, pS=pS, g=g: e.matmul(
                    pS[:, 0:256], btok[:, g * 128:(g + 1) * 128], xde[:, g * 256:(g + 1) * 256],
                    start=True, stop=True), reads=[btok, xde], writes=[pS])
                S.op("dve", lambda e, g=g, dex=dex: e.tensor_tensor(
                    self.st[g][:].rearrange("p (h d) -> p h d", h=4), self.st[g][:].rearrange("p (h d) -> p h d", h=4),
                    dex[:, 32 + 4 * g:36 + 4 * g].unsqueeze(2).to_broadcast([128, 4, 64]), ALU.mult),
                    reads=[self.st[g], dex], writes=[self.st[g]])
                S.op("dve", lambda e, g=g, pS=pS: e.tensor_tensor(self.st[g][:], self.st[g][:], pS[:, 0:256], ALU.add),
                     reads=[self.st[g], pS], writes=[self.st[g]])
                S.op("pool", lambda e, g=g: e.tensor_copy(self.stb[g][:], self.st[g][:]),
                     reads=[self.st[g]], writes=[self.stb[g]])

            ctxs = {0: stage_a(0), 1: stage_a(1)}
            pres = {0: stage_b1(ctxs[0])}
            for g in range(8):
                if g + 2 < 8:
                    ctxs[g + 2] = stage_a(g + 2)
                if g + 1 < 8:
                    pres[g + 1] = stage_b1(ctxs[g + 1])
                stage_b2(ctxs[g], pres[g])
        if STOP <= 5:
            return
        zss = accs

        def evac_z(oc, ps):
            zs = zss.next()
            S.op("act", lambda e: e.activation(out=zs[:], in_=ps[:], func=AF.Silu), reads=[ps], writes=[zs])
            S.op("pool", lambda e: e.tensor_tensor(y[oc][:], y[oc][:], zs[:], ALU.mult),
                 reads=[y[oc], zs], writes=[y[oc]])

        self.dense_fm("a_in_zx", lambda kc: (self.u[kc][:], self.u[kc]), evac_z, blocks=range(0, 8))
        yn = xbc[0:16]
        self.rmsnorm(y, "a_norm", 0, yn, 2048, tmp=(accs, raws))

        def evac_o(oc, ps):
            S.op("dve", lambda e: e.tensor_tensor(self.h[oc][:], self.h[oc][:], ps[:], ALU.add),
                 reads=[self.h[oc], ps], writes=[self.h[oc]])

        self.dense_fm("a_out", lambda kc: (yn[kc][:], yn[kc]), evac_o)


def blockify(W, ncb, perm=None):
    W = np.asarray(W, np.float32)
    if perm is not None:
        W = W[:, perm]
    K, N = W.shape
    KC, nblk = K // 128, N // ncb
    return np.ascontiguousarray(W.reshape(KC, 128, nblk, ncb).transpose(2, 1, 0, 3).reshape(nblk, 128, KC * ncb))


def cols(v):
    v = np.asarray(v, np.float32).reshape(-1)
    return v.reshape(-1, 128).T


def pair_perm(n_half, chunk=128):
    nch = n_half // chunk
    idx = []
    for c in range(nch):
        idx += list(range(c * chunk, (c + 1) * chunk))
        idx += list(range(n_half + c * chunk, n_half + (c + 1) * chunk))
    return np.array(idx)


def host_layout(inp, layers, do_ffn=True, ffn_layers=None):
    off, ncol = pcol_offsets()
    pcols = np.zeros((128, ncol), np.float32)

    def put(name, arr, o=0):
        a = np.asarray(arr, np.float32)
        pcols[:, off[name] + o: off[name] + o + a.shape[1]] = a

    for i in range(4):
        put("norm_mix", cols(inp["norm_mix"][i]), i * 8)
        put("norm_ffn", cols(inp["norm_ffn"][i]), i * 8)
    put("norm_final", cols(inp["norm_final"]))
    out = {}
    prows = np.zeros((128, 64), np.float32)
    cm = np.zeros((128, 512), np.float32)
    cm[:, 0:128] = np.eye(128)
    s = np.arange(128)
    cm[:, 128:256] = (s[:, None] <= s[None, :])
    cm[:, 256:384] = (s[:, None] > s[None, :])
    cm[:, 384:512] = 1.0
    out["cmats"] = cm
    if 0 in layers:
        cw = np.asarray(inp["a_conv_w"][0])
        for k in range(4):
            put("a_conv_w", cols(cw[k]), k * 32)
        put("a_conv_b", cols(inp["a_conv_b"][0]))
        put("a_dsk", cols(np.repeat(np.asarray(inp["a_d_skip"][0]), 64)))
        put("a_norm", cols(inp["a_norm"][0]))
        prows[:, 0:32] = np.asarray(inp["a_dt_bias"][0])[None, :]
        prows[:, 32:64] = np.asarray(inp["a_log"][0])[None, :]
        W = np.asarray(inp["a_in_proj"][0])
        zx = blockify(W[:, :6144], 256)
        dg = np.zeros((24, 128, 2, 4, 128), np.float32)
        ar = np.arange(128)
        for b in range(8, 24):
            for j in range(2):
                f0 = ((b - 8) * 2 + j) * 128
                for k in range(4):
                    dg[b, ar, j, k, ar] = cw[k, f0:f0 + 128]
        out["w_a_in_zx"] = np.concatenate([zx, dg.reshape(24, 128, 1024)], axis=2)
        out["w_a_in_dt"] = blockify(W[:, 6144:6176], 32)
        out["w_a_out"] = blockify(inp["a_out_proj"][0], 128)
    if 1 in layers:
        pp = pair_perm(1024)
        put("b_pw1_b", cols(np.asarray(inp["b_pw1_b"][0])[pp]))
        put("b_dw_b", cols(inp["b_dw_b"][0]))
        put("b_ln_g", cols(inp["b_ln_g"][0]))
        put("b_ln_b", cols(inp["b_ln_b"][0]))
        put("b_pw2_b", cols(inp["b_pw2_b"][0]))
        out["w_b_pw1"] = blockify(inp["b_pw1_w"][0], 256, pp)
        dw = np.asarray(inp["b_dw_w"][0], np.float32)
        cv = np.zeros((8, 128, 31, 128), np.float32)
        ar = np.arange(128)
        for c in range(8):
            for k in range(31):
                cv[c, ar, k, ar] = dw[k, c * 128:(c + 1) * 128]
        out["w_b_conv"] = cv.reshape(8, 128, 31 * 128)
        out["w_b_pw2"] = blockify(inp["b_pw2_w"][0], 256)
    if 2 in layers:
        put("c_in_b", cols(inp["c_in_b"][0]))
        cw = np.asarray(inp["c_conv_w"][0])
        for k in range(4):
            put("c_conv_w", cols(cw[k]), k * 10)
        put("c_conv_b", cols(inp["c_conv_b"][0]))
        put("c_ga_b", cols(np.asarray(inp["c_ga_b"][0]).reshape(-1)))
        put("c_gx_b", cols(np.asarray(inp["c_gx_b"][0]).reshape(-1)))
        put("c_lambda", cols(inp["c_lambda"][0]))
        put("c_out_b", cols(inp["c_out_b"][0]))
        out["w_c_in"] = blockify(inp["c_in_w"][0], 256)
        for nm, key in (("w_c_ga", "c_ga_w"), ("w_c_gx", "c_gx_w")):
            g = np.asarray(inp[key][0], np.float32)
            blk = np.zeros((10, 128, 2, 128), np.float32)
            for oc in range(10):
                bi, half = oc // 2, oc % 2
                blk[oc] = g[bi, :, half * 128:(half + 1) * 128].reshape(2, 128, 128).transpose(1, 0, 2)
            out[nm] = blk.reshape(10, 128, 256)
        out["w_c_out"] = blockify(inp["c_out_w"][0], 128)
    if 3 in layers:
        ib = np.asarray(inp["d_in_b"][0], np.float32)
        put("d_in_b_u", cols(ib[:2048]))
        put("d_out_b", cols(inp["d_out_b"][0]))
        W = np.asarray(inp["d_in_w"][0])
        out["w_d_in_u"] = blockify(W[:, :2048], 256)
        out["w_d_in_v"] = blockify(W[:, 2048:], 512)
        out["w_d_out"] = blockify(inp["d_out_w"][0], 128)
        spw = np.asarray(inp["d_sp_w"][0], np.float32)
        out["wt_sp"] = np.ascontiguousarray(spw.transpose(2, 0, 1).reshape(128, 1024))
        dbc = np.zeros((4, 128, 2048), np.float32)
        dbc[0] = ib[2048:][None, :]
        dbc[1] = np.asarray(inp["d_ln_g"][0])[None, :]
        dbc[2] = np.asarray(inp["d_ln_b"][0])[None, :]
        dbc[3, :, :1024] = np.asarray(inp["d_sp_b"][0]).reshape(-1)[None, :]
        out["d_bc"] = dbc
    if do_ffn:
        pp = pair_perm(2816)
        for l in (ffn_layers if ffn_layers is not None else layers):
            cw = np.asarray(inp["f_conv_w"][l])
            for k in range(3):
                put("f_conv_w", cols(cw[k]), l * 132 + k * 44)
            put("f_conv_b", cols(inp["f_conv_b"][l]), l * 44)
            up = blockify(inp["f_up_w"][l], 256, pp)
            dg = np.zeros((22, 128, 2, 3, 128), np.float32)
            ar = np.arange(128)
            for b in range(22):
                for half in range(2):
                    f0 = half * 2816 + b * 128
                    for k in range(3):
                        dg[b, ar, half, k, ar] = cw[k, f0:f0 + 128]
            out[f"w_f_up{l}"] = np.concatenate([up, dg.reshape(22, 128, 768)], axis=2)
            out[f"w_f_down{l}"] = blockify(inp["f_down_w"][l], 128)
    out["pcols"] = pcols
    out["prows"] = prows
    return out


_NC_CACHE = {}


def run(inp, Lseq, nbatch, layers=(0, 1, 2, 3), do_ffn=True, ncores=None, do_mix=True):
    key = (Lseq, tuple(layers), do_ffn, do_mix)
    if key not in _NC_CACHE:
        _NC_CACHE[key] = Builder(Lseq, layers, do_ffn, do_mix).build()
    nc = _NC_CACHE[key]
    shared = host_layout(inp, layers if do_mix else (), do_ffn, ffn_layers=layers)
    x = np.asarray(inp["x"], np.float32)
    ncores = ncores or nbatch
    in_maps = []
    for c in range(ncores):
        m = dict(shared)
        m["x"] = np.ascontiguousarray(x[c % nbatch, :Lseq])
        in_maps.append(m)
    res = run_bass_kernel_spmd(nc, in_maps, core_ids=list(range(ncores)))
    return np.stack([res.results[b]["y"] for b in range(nbatch)], 0)


def kernel(**inputs):
    return run(inputs, 8192, 4, ncores=8)
```
